# Optimizing a Trainium2 kernel written in Bass

```python
import math
import jax, jax.numpy as jnp
from jax import lax
import numpy as np

D_MODEL = 2048
BATCH = 4
SEQ = 2048
DEPTH = 1
DEC_BATCH = 32
DEC_SEQ = 1
PAST_LEN = 16384
PAGE_SIZE = 128

HEAD_DIM = 64
ATT_WIDTH = D_MODEL // 2
N_Q_HEADS = ATT_WIDTH // HEAD_DIM
N_KV_HEADS = 4
GQA_GROUP = N_Q_HEADS // N_KV_HEADS
KV_WIDTH = N_KV_HEADS * HEAD_DIM
RWKV_WIDTH = D_MODEL - ATT_WIDTH
RWKV_HEADS = RWKV_WIDTH // HEAD_DIM
WINDOW = 128
BLOCK = 128
N_BUCKETS = 32
MAX_DISTANCE = 128
DECAY_LORA = 64
AAA_LORA = 64
GATE_LORA = 128
D_FF = 5504
ATT_COLS = ATT_WIDTH + 2 * KV_WIDTH
SHIFT_COLS = 3 * RWKV_WIDTH + DECAY_LORA + AAA_LORA + GATE_LORA
IN_COLS = ATT_COLS + SHIFT_COLS
RMS_EPS = 1e-5
GN_EPS = 64e-5
FFN_RES = 0.5

kernel_name = "hymba_swa_sink_rwkv7_macaron_step"


def rmsnorm(x, g):
    xf = x.astype(jnp.float32)
    y = xf * lax.rsqrt(jnp.mean(xf * xf, axis=-1, keepdims=True) + RMS_EPS)
    return (y * g.astype(jnp.float32)).astype(x.dtype)


def swiglu(x, wg, wu, wd):
    return (jax.nn.silu(x @ wg) * (x @ wu)) @ wd


def t5_bucket(dist):
    n = jnp.maximum(dist, 0)
    max_exact = N_BUCKETS // 2
    nf = jnp.maximum(n, 1).astype(jnp.float32)
    large = max_exact + (jnp.log(nf / max_exact) / math.log(MAX_DISTANCE / max_exact)
                         * (N_BUCKETS - max_exact)).astype(jnp.int32)
    large = jnp.minimum(large, N_BUCKETS - 1)
    return jnp.where(n < max_exact, n, large)


def sink_attention(q, k, v, dist, valid, rel_bias, sinks):
    bias = rel_bias.astype(jnp.float32)[t5_bucket(dist)]
    bias = jnp.transpose(bias, (2, 0, 1)).reshape(N_KV_HEADS, GQA_GROUP, *dist.shape)
    s = jnp.einsum('...qhgd,...khd->...hgqk', q, k).astype(jnp.float32) * (HEAD_DIM ** -0.5) + bias
    s = jnp.where(valid[..., None, None, :, :], s, -jnp.inf)
    sink = sinks.astype(jnp.float32).reshape(N_KV_HEADS, GQA_GROUP)[:, :, None, None]
    m = jnp.maximum(jnp.max(s, axis=-1, keepdims=True), sink)
    p = jnp.exp(s - m)
    denom = jnp.sum(p, axis=-1, keepdims=True) + jnp.exp(sink - m)
    return jnp.einsum('...hgqk,...khd->...qhgd', (p / denom).astype(v.dtype), v)


def swa_prompt(q, k, v, rel_bias, sinks):
    B, S = q.shape[:2]
    nb = S // BLOCK
    qb = q.reshape(B, nb, BLOCK, N_KV_HEADS, GQA_GROUP, HEAD_DIM)
    pad = ((0, 0), (BLOCK, 0), (0, 0), (0, 0))
    kp = jnp.pad(k, pad).reshape(B, nb + 1, BLOCK, N_KV_HEADS, HEAD_DIM)
    vp = jnp.pad(v, pad).reshape(B, nb + 1, BLOCK, N_KV_HEADS, HEAD_DIM)
    kc = jnp.concatenate([kp[:, :-1], kp[:, 1:]], axis=2)
    vc = jnp.concatenate([vp[:, :-1], vp[:, 1:]], axis=2)
    qi = jnp.arange(BLOCK)[:, None]
    kj = jnp.arange(2 * BLOCK)[None, :]
    dist = BLOCK + qi - kj
    blk = jnp.arange(nb)[:, None, None]
    valid = (dist >= 0) & (dist <= WINDOW) & ((blk > 0) | (kj >= BLOCK))
    o = sink_attention(qb, kc, vc, dist, valid, rel_bias, sinks)
    L = min(WINDOW, S)
    return o.reshape(B, S, ATT_WIDTH), k[:, S - L:], v[:, S - L:]


def swa_decode(q, k, v, k_buf, v_buf, rel_bias, sinks):
    B, T = q.shape[:2]
    L = k_buf.shape[1]
    kc = jnp.concatenate([k_buf.astype(k.dtype), k], axis=1)
    vc = jnp.concatenate([v_buf.astype(v.dtype), v], axis=1)
    qi = jnp.arange(T)[:, None]
    kj = jnp.arange(L + T)[None, :]
    dist = L + qi - kj
    valid = (dist >= 0) & (dist <= WINDOW)
    o = sink_attention(q.reshape(B, T, N_KV_HEADS, GQA_GROUP, HEAD_DIM), kc, vc, dist, valid, rel_bias, sinks)
    return o.reshape(B, T, ATT_WIDTH), kc[:, -L:], vc[:, -L:]


def wkv_scan(S0, r, logw, k, v, kk, a):
    xs = tuple(jnp.swapaxes(t, 0, 1) for t in (r, logw, k, v, kk, a))

    def step(S, inp):
        r_t, lw_t, k_t, v_t, kk_t, a_t = inp
        sa = jnp.einsum('bhij,bhj->bhi', S, -kk_t)
        S = (S * jnp.exp(lw_t)[:, :, None, :] + sa[..., None] * (kk_t * a_t)[:, :, None, :]
             + v_t[..., None] * k_t[:, :, None, :])
        return S, jnp.einsum('bhij,bhj->bhi', S, r_t)

    S, ys = lax.scan(step, S0.astype(jnp.float32), xs)
    return S, jnp.swapaxes(ys, 0, 1)


def rwkv_time_mix(proj, shift0, wkv0, mu, w0, w2, a0, a2, g2, k_k, k_a, r_k, ln_w, ln_b):
    f32 = jnp.float32
    B, T = proj.shape[:2]
    prev = jnp.concatenate([shift0[:, None].astype(proj.dtype), proj[:, :-1]], axis=1)
    xm = (proj + mu * (prev - proj)).astype(f32)
    o1, o2, o3 = RWKV_WIDTH, 2 * RWKV_WIDTH, 3 * RWKV_WIDTH
    o4, o5 = o3 + DECAY_LORA, o3 + DECAY_LORA + AAA_LORA
    r, k, v = xm[..., :o1], xm[..., o1:o2], xm[..., o2:o3]
    wl, al, gl = xm[..., o3:o4], xm[..., o4:o5], xm[..., o5:]
    w = -jax.nn.softplus(-(w0.astype(f32) + jnp.tanh(wl) @ w2.astype(f32))) - 0.5
    logw = -jnp.exp(w)
    a = jax.nn.sigmoid(a0.astype(f32) + al @ a2.astype(f32))
    g = jax.nn.sigmoid(gl) @ g2.astype(f32)
    kk = k * k_k.astype(f32)
    k = k * (1.0 + (a - 1.0) * k_a.astype(f32))
    hs = lambda t: t.reshape(B, T, RWKV_HEADS, HEAD_DIM)
    r, logw, k, v, kk, a = hs(r), hs(logw), hs(k), hs(v), hs(kk), hs(a)
    kk = kk / jnp.maximum(jnp.sqrt(jnp.sum(kk * kk, axis=-1, keepdims=True)), 1e-12)
    S, y = wkv_scan(wkv0, r, logw, k, v, kk, a)
    mean = jnp.mean(y, axis=-1, keepdims=True)
    var = jnp.mean((y - mean) ** 2, axis=-1, keepdims=True)
    yn = ((y - mean) * lax.rsqrt(var + GN_EPS)).reshape(B, T, RWKV_WIDTH) * ln_w.astype(f32) + ln_b.astype(f32)
    bonus = jnp.sum(r * k * r_k.astype(f32), axis=-1, keepdims=True) * v
    out = (yn + bonus.reshape(B, T, RWKV_WIDTH)) * g
    return out, proj[:, -1], S


def trunk_layer(x, attend, shift0, wkv0, lp):
    (n1, f1g, f1u, f1d, nm, w_in, sinks, mu, w0, w2, a0, a2, g2,
     k_k, k_a, r_k, ln_w, ln_b, w_out, n2, f2g, f2u, f2d) = lp
    B, T = x.shape[:2]
    x = x + FFN_RES * swiglu(rmsnorm(x, n1), f1g, f1u, f1d)
    h = rmsnorm(x, nm)
    proj = h @ w_in
    q = proj[..., :ATT_WIDTH].reshape(B, T, N_Q_HEADS, HEAD_DIM)
    k = proj[..., ATT_WIDTH:ATT_WIDTH + KV_WIDTH].reshape(B, T, N_KV_HEADS, HEAD_DIM)
    v = proj[..., ATT_WIDTH + KV_WIDTH:ATT_COLS].reshape(B, T, N_KV_HEADS, HEAD_DIM)
    o_att, k_buf, v_buf = attend(q, k, v, sinks)
    o_rwkv, shift, S = rwkv_time_mix(proj[..., ATT_COLS:], shift0, wkv0, mu, w0, w2, a0, a2, g2,
                                     k_k, k_a, r_k, ln_w, ln_b)
    mixed = jnp.concatenate([o_att, o_rwkv.astype(o_att.dtype)], axis=-1)
    x = x + mixed @ w_out
    x = x + FFN_RES * swiglu(rmsnorm(x, n2), f2g, f2u, f2d)
    return x, k_buf, v_buf, S, shift


def setup_inputs(seed: int = 0) -> dict:
    key = jax.random.key(seed)
    ks = jax.random.split(key, 32)
    f32 = jnp.float32
    nrm = lambda k, shape, s: s * jax.random.normal(k, shape, f32)
    L_win = min(WINDOW, PAST_LEN)
    ramp = (jnp.arange(RWKV_WIDTH, dtype=f32) / (RWKV_WIDTH - 1)) ** 0.85
    return {
        "x_prompt": nrm(ks[0], (BATCH, SEQ, D_MODEL), 1.0),
        "x_sample": nrm(ks[1], (DEC_BATCH, DEC_SEQ, D_MODEL), 1.0),
        "cache_k": nrm(ks[2], (DEPTH, DEC_BATCH, L_win, N_KV_HEADS, HEAD_DIM), 1.0),
        "cache_v": nrm(ks[3], (DEPTH, DEC_BATCH, L_win, N_KV_HEADS, HEAD_DIM), 1.0),
        "state_wkv": nrm(ks[4], (DEPTH, DEC_BATCH, RWKV_HEADS, HEAD_DIM, HEAD_DIM), 0.5),
        "state_shift": nrm(ks[5], (DEPTH, DEC_BATCH, SHIFT_COLS), 1.0),
        "rel_bias": nrm(ks[6], (N_BUCKETS, N_Q_HEADS), 0.5),
        "ffn1_norm": 1.0 + nrm(ks[7], (DEPTH, D_MODEL), 0.02),
        "ffn1_w_gate": nrm(ks[8], (DEPTH, D_MODEL, D_FF), D_MODEL ** -0.5),
        "ffn1_w_up": nrm(ks[9], (DEPTH, D_MODEL, D_FF), D_MODEL ** -0.5),
        "ffn1_w_down": nrm(ks[10], (DEPTH, D_FF, D_MODEL), D_FF ** -0.5),
        "mix_norm": 1.0 + nrm(ks[11], (DEPTH, D_MODEL), 0.02),
        "w_in": nrm(ks[12], (DEPTH, D_MODEL, IN_COLS), D_MODEL ** -0.5),
        "attn_sinks": nrm(ks[13], (DEPTH, N_Q_HEADS), 0.5),
        "shift_mu": jax.random.uniform(ks[14], (DEPTH, SHIFT_COLS), f32),
        "decay_w0": -6.0 + 5.0 * ramp[None, :] + nrm(ks[15], (DEPTH, RWKV_WIDTH), 0.05),
        "decay_w2": nrm(ks[16], (DEPTH, DECAY_LORA, RWKV_WIDTH), 0.1 * DECAY_LORA ** -0.5),
        "aaa_a0": nrm(ks[17], (DEPTH, RWKV_WIDTH), 0.1),
        "aaa_a2": nrm(ks[18], (DEPTH, AAA_LORA, RWKV_WIDTH), AAA_LORA ** -0.5),
        "gate_g2": nrm(ks[19], (DEPTH, GATE_LORA, RWKV_WIDTH), GATE_LORA ** -0.5),
        "key_k": 0.85 + nrm(ks[20], (DEPTH, RWKV_WIDTH), 0.02),
        "key_a": 1.0 + nrm(ks[21], (DEPTH, RWKV_WIDTH), 0.02),
        "bonus_r_k": nrm(ks[22], (DEPTH, RWKV_HEADS, HEAD_DIM), 0.1),
        "ln_x_w": 1.0 + nrm(ks[23], (DEPTH, RWKV_WIDTH), 0.02),
        "ln_x_b": nrm(ks[24], (DEPTH, RWKV_WIDTH), 0.02),
        "w_out": nrm(ks[25], (DEPTH, D_MODEL, D_MODEL), D_MODEL ** -0.5),
        "ffn2_norm": 1.0 + nrm(ks[26], (DEPTH, D_MODEL), 0.02),
        "ffn2_w_gate": nrm(ks[27], (DEPTH, D_MODEL, D_FF), D_MODEL ** -0.5),
        "ffn2_w_up": nrm(ks[28], (DEPTH, D_MODEL, D_FF), D_MODEL ** -0.5),
        "ffn2_w_down": nrm(ks[29], (DEPTH, D_FF, D_MODEL), D_FF ** -0.5),
        "final_norm": 1.0 + nrm(ks[30], (D_MODEL,), 0.02),
    }


def reference(x_prompt, x_sample, cache_k, cache_v, state_wkv, state_shift, rel_bias,
              ffn1_norm, ffn1_w_gate, ffn1_w_up, ffn1_w_down, mix_norm, w_in, attn_sinks,
              shift_mu, decay_w0, decay_w2, aaa_a0, aaa_a2, gate_g2, key_k, key_a, bonus_r_k,
              ln_x_w, ln_x_b, w_out, ffn2_norm, ffn2_w_gate, ffn2_w_up, ffn2_w_down, final_norm):
    xp, xs = x_prompt, x_sample
    B = x_prompt.shape[0]
    kp_l, vp_l, sp_l, shp_l = [], [], [], []
    ks_l, vs_l, ss_l, shs_l = [], [], [], []
    for l in range(DEPTH):
        lp = (ffn1_norm[l], ffn1_w_gate[l], ffn1_w_up[l], ffn1_w_down[l], mix_norm[l], w_in[l],
              attn_sinks[l], shift_mu[l], decay_w0[l], decay_w2[l], aaa_a0[l], aaa_a2[l], gate_g2[l],
              key_k[l], key_a[l], bonus_r_k[l], ln_x_w[l], ln_x_b[l], w_out[l],
              ffn2_norm[l], ffn2_w_gate[l], ffn2_w_up[l], ffn2_w_down[l])
        shift0 = jnp.zeros((B, SHIFT_COLS), x_prompt.dtype)
        wkv0 = jnp.zeros((B, RWKV_HEADS, HEAD_DIM, HEAD_DIM), jnp.float32)
        xp, kp, vp, sp, shp = trunk_layer(
            xp, lambda q, k, v, s: swa_prompt(q, k, v, rel_bias, s), shift0, wkv0, lp)
        xs, kd, vd, sd, shd = trunk_layer(
            xs, lambda q, k, v, s, l=l: swa_decode(q, k, v, cache_k[l], cache_v[l], rel_bias, s),
            state_shift[l], state_wkv[l], lp)
        kp_l.append(kp); vp_l.append(vp); sp_l.append(sp); shp_l.append(shp)
        ks_l.append(kd); vs_l.append(vd); ss_l.append(sd); shs_l.append(shd)
    y_prompt = rmsnorm(xp, final_norm)
    y_sample = rmsnorm(xs, final_norm)
    new_k_prompt = jnp.stack(kp_l, 0)
    new_v_prompt = jnp.stack(vp_l, 0)
    new_wkv_prompt = jnp.stack(sp_l, 0)
    new_shift_prompt = jnp.stack(shp_l, 0)
    new_k_sample = jnp.stack(ks_l, 0)
    new_v_sample = jnp.stack(vs_l, 0)
    new_wkv_sample = jnp.stack(ss_l, 0)
    new_shift_sample = jnp.stack(shs_l, 0)
    return (y_prompt, y_sample, new_k_prompt, new_v_prompt, new_wkv_prompt, new_shift_prompt,
            new_k_sample, new_v_sample, new_wkv_sample, new_shift_sample)
```

```python
import numpy as np
import concourse.bass as bass
import concourse.mybir as mybir
from concourse.bass_utils import run_bass_kernel_spmd

F32 = mybir.dt.float32
BF16 = mybir.dt.bfloat16
I32 = mybir.dt.int32
AF = mybir.ActivationFunctionType
ALU = mybir.AluOpType
AX = mybir.AxisListType

ENGS = ("pe", "act", "dve", "pool", "sp")
DMAQ = ("sp", "act", "pool")
NSEM_DMA = 12
SEG = 6000


class Prog:
    def __init__(self):
        self.ops = {e: [] for e in ENGS}
        self.lastw = {}
        self.lastr = {}
        self.ndma = {q: 0 for q in DMAQ}
        self.pending = {e: [] for e in ENGS}

    def _deps(self, eng, r, w):
        deps = list(self.pending[eng])
        self.pending[eng] = []
        for reg in r:
            lw = self.lastw.get(reg)
            if lw is not None:
                deps.append(lw)
        for reg in w:
            lw = self.lastw.get(reg)
            if lw is not None and not (lw[0] == "eng" and lw[1] == eng and eng == "pe"):
                deps.append(lw)
            for k, ref in self.lastr.get(reg, {}).items():
                if not (ref[0] == "eng" and ref[1] == eng and eng == "pe"):
                    deps.append(ref)
        return deps

    def _mark(self, ref, r, w):
        for reg in r:
            key = (ref[0], ref[1]) if ref[0] == "eng" else ("dma", ref[1], ref[2] % NSEM_DMA)
            if ref[0] == "cc":
                key = ("cc", ref[2])
            self.lastr.setdefault(reg, {})[key] = ref
        for reg in w:
            self.lastw[reg] = ref
            self.lastr[reg] = {}

    @staticmethod
    def _excl(r, w):
        extra = [x for x in r if isinstance(x, tuple) and x[0] == "PS" and x not in w]
        return list(r), list(w) + extra

    def op(self, eng, fn, r=(), w=()):
        r, w = self._excl(r, w)
        deps = self._deps(eng, r, w)
        idx = len(self.ops[eng])
        self.ops[eng].append({"fn": fn, "deps": deps, "sig": False, "dma": None})
        self._mark(("eng", eng, idx), r, w)
        return ("eng", eng, idx)

    def dma(self, q, fn, r=(), w=()):
        deps = self._deps(q, r, w)
        k = self.ndma[q]
        self.ndma[q] += 1
        if k >= NSEM_DMA:
            deps.append(("dma", q, k - NSEM_DMA))
        idx = len(self.ops[q])
        self.ops[q].append({"fn": fn, "deps": deps, "sig": False, "dma": k})
        self._mark(("dma", q, k), r, w)
        return ("dma", q, k)

    def cc(self, fn, r=(), w=()):
        deps = self._deps("pool", r, w)
        k = getattr(self, "ncc", 0)
        self.ncc = k + 1
        self.ops["pool"].append({"fn": fn, "deps": deps, "sig": False, "dma": None, "cc": k})
        self._mark(("cc", "pool", k), r, w)
        return ("cc", "pool", k)

    def barrier(self):
        refs = []
        for e in ENGS:
            if self.ops[e]:
                last = len(self.ops[e]) - 1
                j = last
                while j >= 0 and (self.ops[e][j]["dma"] is not None or self.ops[e][j].get("cc") is not None):
                    j -= 1
                if j >= 0:
                    refs.append(("eng", e, j))
        for q in DMAQ:
            n = self.ndma[q]
            for k in range(max(0, n - NSEM_DMA), n):
                refs.append(("dma", q, k))
        for k in range(getattr(self, "ncc", 0)):
            refs.append(("cc", "pool", k))
        for e in ENGS:
            self.pending[e] = list(refs)

    def emit(self, nc, stack):
        for e in ENGS:
            for o in self.ops[e]:
                for d in o["deps"]:
                    if d[0] == "eng":
                        self.ops[d[1]][d[2]]["sig"] = True
        sigval = {}
        nsig = {}
        for e in ENGS:
            c = 0
            for i, o in enumerate(self.ops[e]):
                if o["sig"]:
                    c += 1
                    sigval[(e, i)] = c
            nsig[e] = c
        esem = {}
        for e in ENGS:
            nseg = (nsig[e] + SEG - 1) // SEG
            esem[e] = [stack.enter_context(nc.semaphore(f"s_{e}_{j}")) for j in range(max(1, nseg))]
        dsem = {q: [stack.enter_context(nc.semaphore(f"d_{q}_{j}")) for j in range(NSEM_DMA)]
                for q in DMAQ if self.ndma[q] > 0}
        csem = [stack.enter_context(nc.semaphore(f"cc_{j}")) for j in range(getattr(self, "ncc", 0))]
        block = stack.enter_context(nc.Block())
        prog = self

        def emit_engine(e, eng):
            waited = {}
            for i, o in enumerate(prog.ops[e]):
                for d in o["deps"]:
                    if d[0] == "eng":
                        if d[1] == e and d[2] >= i:
                            continue
                        v = sigval[(d[1], d[2])]
                        key = ("eng", d[1])
                        if waited.get(key, 0) >= v:
                            continue
                        waited[key] = v
                        seg, val = (v - 1) // SEG, (v - 1) % SEG + 1
                        eng.wait_ge(esem[d[1]][seg], val)
                    elif d[0] == "cc":
                        key = ("cc", d[2])
                        if waited.get(key, 0) >= 1:
                            continue
                        waited[key] = 1
                        eng.wait_ge(csem[d[2]], 1)
                    else:
                        _, q, k = d
                        key = ("dma", q, k % NSEM_DMA)
                        v = 16 * (k // NSEM_DMA + 1)
                        if waited.get(key, 0) >= v:
                            continue
                        waited[key] = v
                        eng.wait_ge(dsem[q][k % NSEM_DMA], v)
                ins = o["fn"](eng)
                if o.get("cc") is not None:
                    ins.then_inc(csem[o["cc"]])
                elif o["dma"] is not None:
                    ins.then_inc(dsem[e][o["dma"] % NSEM_DMA], 16)
                elif o["sig"]:
                    v = sigval[(e, i)]
                    ins.then_inc(esem[e][(v - 1) // SEG], 1)
            if e in DMAQ and prog.ndma[e] > 0:
                n = prog.ndma[e]
                for j in range(NSEM_DMA):
                    cnt = (n - j + NSEM_DMA - 1) // NSEM_DMA if n > j else 0
                    if cnt > 0 and waited.get(("dma", e, j), 0) < 16 * cnt:
                        eng.wait_ge(dsem[e][j], 16 * cnt)

        @block.tensor
        def _(t):
            emit_engine("pe", t)

        @block.scalar
        def _(s):
            emit_engine("act", s)

        @block.vector
        def _(v):
            emit_engine("dve", v)

        @block.gpsimd
        def _(g):
            emit_engine("pool", g)

        @block.sync
        def _(s):
            emit_engine("sp", s)


class Cfg:
    def __init__(self, D=2048, DFF=5504, NP=1024, NS=4, n_cores=8, halves=2, GS=8):
        self.D, self.DFF, self.NP, self.NS = D, DFF, NP, NS
        self.n_cores, self.halves, self.GS = n_cores, halves, GS
        self.KD = D // 128
        self.ATT = D // 2
        self.QC = self.ATT // 128
        self.NQH = self.ATT // 64
        self.NKV = self.NQH // 4
        self.KVW = self.NKV * 64
        self.KC = self.KVW // 128
        self.RW = D - self.ATT
        self.RH = self.RW // 64
        self.RP = self.RW // 128
        self.NF = (DFF + 127) // 128
        assert DFF % 128 == 0
        self.NT = NP + NS + 1
        self.NTP = (self.NT + 1) // 2 * 2
        self.ATT_COLS = self.ATT + 2 * self.KVW
        self.SHIFT_COLS = 3 * self.RW + 256
        self.IN_COLS = self.ATT_COLS + self.SHIFT_COLS
        self.SC = self.SHIFT_COLS // 128
        self.NB = NP // 128
        self.NCH = NP // 64
        n = (self.NT + 511) // 512
        base, rem = divmod(self.NT, n)
        self.CH = []
        s = 0
        for i in range(n):
            w = base + (1 if i < rem else 0)
            self.CH.append((s, s + w))
            s += w


def _groups(n, gs):
    out, s = [], 0
    while s < n:
        out.append((s, min(n, s + gs)))
        s += gs
    return out


class Builder:
    def __init__(self, cfg, stages=("ffn1", "mixer", "ffn2")):
        self.cfg = cfg
        self.stages = stages
        self.p = Prog()
        self.nc = bass.Bass("TRN2", target_bir_lowering=False)
        self.din = {}
        self.dout = {}
        self.psum_rr = 0

    def inp(self, name, shape, dt=F32):
        self.din[name] = self.nc.dram_tensor(name, list(shape), dt, kind="ExternalInput").ap()
        return self.din[name]

    def outp(self, name, shape, dt=F32):
        self.dout[name] = self.nc.dram_tensor(name, list(shape), dt, kind="ExternalOutput").ap()
        return self.dout[name]

    def pvec_layout(self):
        c = self.cfg
        off = {}
        o = 0
        for nm, n in (("n1", c.KD), ("nm", c.KD), ("n2", c.KD), ("nf", c.KD), ("mu", c.SC),
                      ("w0", c.RP), ("a0", c.RP), ("k_k", c.RP), ("k_a", c.RP), ("r_k", c.RP),
                      ("ln_w", c.RP), ("ln_b", c.RP)):
            off[nm] = o
            o += n
        off["_n"] = o
        return off

    def build(self):
        import contextlib
        c, nc, p = self.cfg, self.nc, self.p
        KD, NT, NF = c.KD, c.NT, c.NF
        with contextlib.ExitStack() as st:
            self.st = st
            sb = lambda name, shape, dt: st.enter_context(nc.sbuf_tensor(name, list(shape), dt))
            xT = self.inp("xT", [128, KD, NT])
            pv = self.pvec_layout()
            self.pv = pv
            pvec_d = self.inp("pvec", [128, pv["_n"]])
            cst_d = self.inp("cst", [128, 7, 128])
            wd = {}
            for nm, shp in (("f1g", [c.D, c.DFF]), ("f1u", [c.D, c.DFF]), ("f1d", [c.DFF, c.D]),
                            ("w_in", [c.D, c.IN_COLS]), ("w_out", [c.D, c.D]),
                            ("f2g", [c.D, c.DFF]), ("f2u", [c.D, c.DFF]), ("f2d", [c.DFF, c.D])):
                wd[nm] = self.inp(nm, shp)
            self.wd = wd
            yT = self.outp("yT", [128, KD, c.NP + c.NS])
            self.declare_io()

            XN = sb("XN", [128, KD, NT], BF16)
            PVEC = sb("PVEC", [128, pv["_n"]], F32)
            CSTF = sb("CSTF", [128, 7, 128], F32)
            CSTB = sb("CSTB", [128, 7, 128], BF16)
            ONESB = sb("ONESB", [128, 128], BF16)
            RSTD = sb("RSTD", [128, 512], F32)
            SQ = sb("SQ", [128, 2, 512], BF16)
            self.XN, self.PVEC, self.CSTF, self.CSTB, self.ONESB = XN, PVEC, CSTF, CSTB, ONESB
            self.RSTD, self.SQ = RSTD, SQ
            self.SG = sb("SG", [128, 2, 512], F32)
            self.alloc_extra(sb)
            SCR_BYTES = self.scratch_bytes()
            SCR = sb("SCR", [128, SCR_BYTES // 4], F32)
            self.SCR = SCR
            self.XB = (KD * NT * 4 + 63) // 64 * 64
            X = self.carve(0, [KD, NT], F32)
            self.X = X
            self.xpark = nc.dram_tensor("xpark", [128, KD, NT], F32, kind="Internal").ap()
            self.PSALL = st.enter_context(nc.psum_tensor("psall", [128, 4096], F32))
            self.PS = [self.PSALL[:, 512 * i:512 * (i + 1)] for i in range(8)]

            self.dma("sp", X[:], xT[:], w=[("X", k) for k in range(KD)])
            self.dma("sp", PVEC[:], pvec_d[:], w=["PVEC"])
            self.dma("sp", CSTF[:], cst_d[:], w=["CSTF"])
            self.cp(CSTB[:], CSTF[:], r=["CSTF"], w=["CSTB"])
            self.mset(ONESB[:], 1.0, w=["ONESB"])

            if "ffn1" in self.stages:
                self.rmsnorm("n1")
                self.ffn("f1g", "f1u", "f1d")
            if "mixer" in self.stages:
                self.rmsnorm("nm")
                self.dma("sp", self.xpark[:], X[:], r=[("X", k) for k in range(KD)], w=["xpark"])
                p.barrier()
                self.mixer()
                p.barrier()
            if "ffn2" in self.stages:
                self.rmsnorm("n2")
                self.ffn("f2g", "f2u", "f2d")
            self.rmsnorm("nf", final=True)
            self.dma("sp", yT[:], X[:, :, 1:NT], r=[("X", k) for k in range(KD)])
            self.extra_outputs()
            p.emit(nc, st)
        return nc

    def extra_outputs(self):
        pass

    def declare_io(self):
        pass

    def alloc_extra(self, sb):
        pass

    def mixer(self):
        pass

    def carve(self, off_bytes, shape, dt):
        n = int(np.prod(shape))
        esz = 4 if dt == F32 else 2
        assert off_bytes % 4 == 0
        nb = n * esz
        assert nb % 4 == 0
        assert off_bytes + nb <= self.SCR.shape[1] * 4, (off_bytes, nb, self.SCR.shape)
        v = self.SCR[:, off_bytes // 4:(off_bytes + nb) // 4]
        if dt != F32:
            v = v.bitcast(dt)
        if len(shape) == 1:
            return v
        names = "abcdefg"[:len(shape)]
        pat = "p (" + " ".join(names) + ") -> p " + " ".join(names)
        return v.rearrange(pat, **{n: int(sz) for n, sz in zip(names[:-1], shape[:-1])})

    def scratch_bytes(self):
        c = self.cfg
        ffn = (c.GS * c.NTP * 2 + 63) // 64 * 64 + c.GS * c.D * 2 + 4 * 2 * c.KD * 128 * 2
        ffn = (ffn + 63) // 64 * 64
        xb = (c.KD * c.NT * 4 + 63) // 64 * 64
        return max(xb + ffn, self.mixer_bytes())

    def mixer_bytes(self):
        return 0

    def ps(self):
        b = self.PS[self.psum_rr % 8]
        self.psum_rr += 1
        return b

    def mm(self, out, lhsT, rhs, start=True, stop=True, r=(), w=()):
        return self.p.op("pe", lambda e: e.matmul(out, lhsT=lhsT, rhs=rhs, start=start, stop=stop), r, w)

    def tr(self, out, in_, ident, r=(), w=()):
        return self.p.op("pe", lambda e: e.transpose(out, in_, ident), r, w)

    def act(self, out, in_, func, bias=None, scale=None, r=(), w=(), accum_out=None):
        kw = {}
        if bias is not None:
            kw["bias"] = bias
        if scale is not None:
            kw["scale"] = scale
        if accum_out is not None:
            kw["accum_out"] = accum_out
        return self.p.op("act", lambda e: e.activation(out=out, in_=in_, func=func, **kw), r, w)

    def ts(self, out, in0, s1, s2, op0, op1=None, r=(), w=(), eng="dve"):
        if op1 is None:
            return self.p.op(eng, lambda e: e.tensor_scalar(out=out, in0=in0, scalar1=s1, scalar2=None, op0=op0), r, w)
        return self.p.op(eng, lambda e: e.tensor_scalar(out=out, in0=in0, scalar1=s1, scalar2=s2, op0=op0, op1=op1), r, w)

    def stt(self, out, in0, scalar, in1, op0, op1, r=(), w=()):
        return self.p.op("dve", lambda e: e.scalar_tensor_tensor(out=out, in0=in0, scalar=scalar, in1=in1, op0=op0, op1=op1), r, w)

    def tt(self, out, in0, in1, op, r=(), w=(), eng="dve"):
        return self.p.op(eng, lambda e: e.tensor_tensor(out=out, in0=in0, in1=in1, op=op), r, w)

    def cp(self, out, in_, r=(), w=(), eng="dve"):
        if eng == "act":
            return self.p.op("act", lambda e: e.copy(out=out, in_=in_), r, w)
        return self.p.op(eng, lambda e: e.tensor_copy(out=out, in_=in_), r, w)

    def red(self, out, in_, op, r=(), w=()):
        return self.p.op("dve", lambda e: e.tensor_reduce(out=out, in_=in_, axis=AX.X, op=op), r, w)

    def recip(self, out, in_, r=(), w=()):
        return self.p.op("dve", lambda e: e.reciprocal(out=out, in_=in_), r, w)

    def mset(self, ap, val, r=(), w=(), eng="dve"):
        return self.p.op(eng, lambda e: e.memset(ap, val), r, w)

    def dma(self, q, out, in_, r=(), w=()):
        return self.p.dma(q, lambda e: e.dma_start(out=out, in_=in_), r, w)

    def rmsnorm(self, gname, final=False):
        c = self.cfg
        X, XN, PVEC, ONESB, RSTD, SQ = self.X, self.XN, self.PVEC, self.ONESB, self.RSTD, self.SQ
        KD = c.KD
        goff = self.pv[gname]
        for ci, (a, b) in enumerate(c.CH):
            w = b - a
            bi = 6 + (ci % 2)
            bank = self.PS[bi]
            breg = ("PS", bi)
            for k in range(KD):
                sl = k % 2
                self.act(SQ[:, sl, 0:w], X[:, k, a:b], AF.Square, r=[("X", k)], w=[("SQ", sl)])
                self.mm(bank[:, 0:w], ONESB[:], SQ[:, sl, 0:w], start=(k == 0), stop=(k == KD - 1),
                        r=[("SQ", sl), "ONESB"], w=[breg])
            self.ts(RSTD[:, 0:w], bank[:, 0:w], 1.0 / c.D, 1e-5, ALU.mult, ALU.add, r=[breg], w=["RSTD"])
            self.act(RSTD[:, 0:w], RSTD[:, 0:w], AF.Sqrt, r=["RSTD"], w=["RSTD"])
            self.recip(RSTD[:, 0:w], RSTD[:, 0:w], r=["RSTD"], w=["RSTD"])
            for k in range(KD):
                dst = X if final else XN
                dreg = ("X", k) if final else ("XN", k)
                self.stt(dst[:, k, a:b], X[:, k, a:b], PVEC[:, goff + k:goff + k + 1], RSTD[:, 0:w],
                         ALU.mult, ALU.mult, r=[("X", k), "RSTD", "PVEC"], w=[dreg])

    def ffn(self, gname, uname, dname):
        c = self.cfg
        X, XN, SG = self.X, self.XN, self.SG
        KD, NT, GS = c.KD, c.NT, c.GS
        HB = self.carve(self.XB, [GS, c.NTP], BF16)
        o1 = self.XB + (GS * c.NTP * 2 + 63) // 64 * 64
        WD = self.carve(o1, [GS, c.D], BF16)
        o2 = o1 + GS * c.D * 2
        NSLOT = 4
        WGU = self.carve(o2, [NSLOT * 2, KD, 128], BF16)
        wg_d = self.wd[gname].rearrange("(k q) n -> q k n", q=128)
        wu_d = self.wd[uname].rearrange("(k q) n -> q k n", q=128)
        wdn_d = self.wd[dname].rearrange("(j q) n -> q j n", q=128)
        step = 0
        slot_i = 0
        for (g0, g1) in _groups(c.NF, GS):
            ng = g1 - g0
            self.dma("pool", WD[:, 0:ng, :], wdn_d[:, g0:g1, :], w=[("WD", j) for j in range(ng)])
            for j in range(g0, g1):
                sl = slot_i % NSLOT
                slot_i += 1
                self.dma("pool", WGU[:, 2 * sl, :, :], wg_d[:, :, j * 128:(j + 1) * 128], w=[("WGU", 2 * sl)])
                self.dma("pool", WGU[:, 2 * sl + 1, :, :], wu_d[:, :, j * 128:(j + 1) * 128], w=[("WGU", 2 * sl + 1)])
                for ci, (a, b) in enumerate(c.CH):
                    w = b - a
                    bi = (step % 2) * 2
                    sgs = step % 2
                    step += 1
                    bg, bu = self.PS[bi], self.PS[bi + 1]
                    for k in range(KD):
                        self.mm(bg[:, 0:w], WGU[:, 2 * sl, k, :], XN[:, k, a:b], start=(k == 0), stop=(k == KD - 1),
                                r=[("WGU", 2 * sl), ("XN", k)], w=[("PS", bi)])
                    for k in range(KD):
                        self.mm(bu[:, 0:w], WGU[:, 2 * sl + 1, k, :], XN[:, k, a:b], start=(k == 0), stop=(k == KD - 1),
                                r=[("WGU", 2 * sl + 1), ("XN", k)], w=[("PS", bi + 1)])
                    self.act(SG[:, sgs, 0:w], bg[:, 0:w], AF.Silu, r=[("PS", bi)], w=[("SG", sgs)])
                    self.tt(HB[:, j - g0, a:b], bu[:, 0:w], SG[:, sgs, 0:w], ALU.mult,
                            r=[("PS", bi + 1), ("SG", sgs)], w=[("HB", j - g0, ci)])
            dstep = 0
            for o in range(KD):
                for ci, (a, b) in enumerate(c.CH):
                    w = b - a
                    bi = 4 + (dstep % 2)
                    dstep += 1
                    bd = self.PS[bi]
                    for jj in range(ng):
                        self.mm(bd[:, 0:w], WD[:, jj, o * 128:(o + 1) * 128], HB[:, jj, a:b], start=(jj == 0), stop=(jj == ng - 1),
                                r=[("WD", jj), ("HB", jj, ci)], w=[("PS", bi)])
                    self.stt(X[:, o, a:b], bd[:, 0:w], 0.5, X[:, o, a:b], ALU.mult, ALU.add,
                             r=[("PS", bi), ("X", o)], w=[("X", o)])


class MixerBuilder(Builder):
    TB = 256
    NW = 4

    def layout(self):
        c = self.cfg
        L = {}
        xb = (c.KD * c.NT * 4 + 63) // 64 * 64
        o = 0

        def add(name, shape, dt):
            nonlocal o
            esz = 4 if dt == F32 else 2
            L[name] = (o, shape, dt)
            o += (int(np.prod(shape)) * esz + 63) // 64 * 64

        add("KD2", [c.NKV, 128 + c.NTP], BF16)
        add("VT", [c.NB + 1, c.KVW], BF16)
        add("BIAS", [c.NQH, 256], BF16)
        add("BREV", [c.NQH, 256], BF16)
        add("MASK0", [128], BF16)
        add("SINKB", [c.NQH], F32)
        add("S", [4, 260], F32)
        add("E", [4, 260], F32)
        add("PB", [4, 256], BF16)
        add("PT", [8, 128], BF16)
        add("MX", [16], F32)
        add("CKD", [c.NKV, 128], BF16)
        add("CVD", [c.NKV, 128], BF16)
        add("KTD", [c.NKV, 130], BF16)
        add("QP", [c.NS, c.NKV, c.NQH], BF16)
        add("SD", [132], F32)
        add("ED", [132], F32)
        add("PD", [132], BF16)
        add("PTD", [c.NQH + 16], BF16)
        add("PT1", [c.NQH + 16], BF16)
        add("VNS", [c.KVW], F32)
        add("VN0", [c.NS, c.KVW], F32)
        add("VND", [c.NS, c.NKV, 128], BF16)
        add("MXD", [8], F32)
        add("SINKP", [2], F32)
        oa1 = o
        o = 0
        TB = self.TB
        for nm in ("PR",):
            add(nm, [3, TB + 1], F32)
        add("D", [3, TB], F32)
        add("XM", [3, TB], F32)
        for nm in ("Aa", "KK", "T1", "KP", "NBb", "DD", "RS"):
            add(nm, [TB], F32)
        for nm in ("YP", "BVP", "GGP"):
            add(nm, [c.NT], F32)
        for nm in ("PHITP", "GP", "QTP"):
            add(nm, [c.NCH, 64], F32)
        add("HIN", [64], F32)
        for nm in ("RKb", "SQb"):
            add(nm, [TB], BF16)
        add("W", [4, 4, 64], F32)
        add("WCC", [4], F32)
        add("KR", [4, 2, 64], BF16)
        for nm in ("KH", "NBH", "KDE", "NBDE", "VB"):
            add(nm, [4, 64], BF16)
        add("RTF", [4, 64], F32)
        add("SIGT", [4, 128], F32)
        add("TM", [4, 4, 128], BF16)
        add("AM", [2, 4, 256], BF16)
        add("NM", [2, 2, 8, 64], BF16)
        add("TT", [2, 8, 64], BF16)
        add("MV", [8, 64], BF16)
        add("PZ", [8, 128], BF16)
        add("H", [64], F32)
        add("DIAGW", [4, 64], F32)
        add("MASK4", [256], F32)
        add("TRIL", [64], F32)
        add("IDH", [64], F32)
        add("IDHB", [64], BF16)
        add("ST", [c.NS, 2, 64], F32)
        add("T2", [c.NS, 2, 64], F32)
        add("T3", [c.NS, 2, 64], F32)
        add("DIAG", [c.NS, 128], F32)
        add("VI", [2, c.NS], F32)
        add("SA", [c.NS, 2], F32)
        add("YI", [c.NS, 2], F32)
        add("SIGF", [c.NS], F32)
        add("WDEC", [c.NS], F32)
        add("SEL1", [128], F32)
        o = max(xb, oa1, o)
        add("WR", [self.NW, c.KD, 128], BF16)
        add("Q", [c.QC, c.NTP], BF16)
        add("MIXR", [c.RP, c.NTP], BF16)
        add("LORA", [3, c.NTP], BF16)
        add("LW", [2, c.RW], BF16)
        add("W0R", [c.RW], F32)
        add("ONEF", [128], F32)
        self.L = L
        return o

    def mixer_bytes(self):
        return self.layout()

    def declare_io(self):
        c = self.cfg
        self.inp("oh_rev", [33, 384])
        self.inp("rb_ext", [33, c.NQH])
        self.inp("mask0", [128, 128])
        self.inp("sinks", [c.NQH])
        self.outp("new_k", [64, c.NKV, 128])
        self.outp("new_v", [128, c.KVW])
        self.gscr = self.nc.dram_tensor("gscr", [c.NQH, 384], F32, kind="Internal").ap()
        if self.debug:
            self.outp("dbg_q", [128, c.QC, c.NTP], BF16)

    def alloc_extra(self, sb):
        c = self.cfg
        self.KST = sb("KST", [64, c.NKV, 128], F32)
        self.VST = sb("VST", [128, c.KVW], F32)

    debug = False
    no_cc = False

    def dbg(self, name, ap, regs, dt=F32):
        if not self.debug:
            return
        d = self.nc.dram_tensor("dbg_" + name, list(ap.shape), dt, kind="ExternalOutput").ap()
        self.dma("sp", d, ap, r=regs)

    def V(self, name):
        off, shape, dt = self.L[name]
        return self.carve(off, shape, dt)

    def colregs(self, name, c_, a, b):
        cfg = self.cfg
        regs = set()
        for col in (a, b - 1):
            pass
        if a == 0:
            regs.add((name, c_, "h"))
        lo, hi = max(a, 1), min(b, cfg.NP + 1)
        if lo < hi:
            for blk in range((lo - 1) // 128, (hi - 2) // 128 + 1):
                regs.add((name, c_, blk))
        if b > cfg.NP + 1:
            regs.add((name, c_, "s"))
        return list(regs)

    def allcols(self, name, c_):
        return [(name, c_, "h")] + [(name, c_, b) for b in range(self.cfg.NB)] + [(name, c_, "s")]

    def slab(self, src_ap, dup64=False):
        WR = self.V("WR")
        sl = self.slab_i % self.NW
        self.slab_i += 1
        reg = ("WR", sl)
        if dup64:
            self.dma("pool", WR[:, sl, :, 0:64], src_ap, w=[reg])
            self.dma("pool", WR[:, sl, :, 64:128], src_ap, w=[])
            self.p.lastw[reg] = ("dma", "pool", self.p.ndma["pool"] - 1)
            self.p.lastw[("WRa", sl)] = ("dma", "pool", self.p.ndma["pool"] - 2)
            return WR[:, sl], [reg, ("WRa", sl)]
        self.dma("pool", WR[:, sl], src_ap, w=[reg])
        return WR[:, sl], [reg]

    def mixer(self):
        c = self.cfg
        self.slab_i = 0
        self.attention_inputs()
        self.proj_qkv()
        self.attention_prompt()
        if self.debug:
            self.dma("sp", self.dout["dbg_q"][:], self.V("Q")[:], r=[r_ for ch in range(c.QC) for r_ in self.allcols("Q", ch)])
            self.dbg("S", self.V("S"), [("S", 0), ("S", 1)])
            self.dbg("E", self.V("E"), [("E", s_) for s_ in range(4)])
            self.dbg("MX", self.V("MX"), ["MX", "NMX", "RDEN"])
            self.dbg("BIAS", self.V("BIAS"), ["BIAS"], BF16)
            self.dbg("KD2", self.V("KD2"), [], BF16)

    def attention_inputs(self):
        c = self.cfg
        BIAS, MASK0, SINKB = self.V("BIAS"), self.V("MASK0"), self.V("SINKB")
        oh = self.din["oh_rev"]
        rb = self.din["rb_ext"]
        OH = self.st.enter_context(self.nc.sbuf_tensor("OH", [33, 384], F32))
        RB = self.st.enter_context(self.nc.sbuf_tensor("RB", [33, c.NQH], F32))
        GREV = self.st.enter_context(self.nc.sbuf_tensor("GREV", [c.NQH, 384], F32))
        self.GREV = GREV
        self.dma("sp", OH[:], oh[:], w=["OH"])
        self.dma("sp", RB[:], rb[:], w=["RB"])
        bank = self.PS[7]
        self.mm(bank[0:c.NQH, 0:384], RB[:], OH[:], r=["OH", "RB"], w=[("PS", 7)])
        self.cp(GREV[:], bank[0:c.NQH, 0:384], r=[("PS", 7)], w=["GREV"])
        gsc = self.gscr
        self.dma("sp", gsc[:], GREV[:], r=["GREV"], w=["gsc"])
        BREV = self.V("BREV")
        src = bass.AP(tensor=gsc.tensor, offset=0, ap=[[1, 128], [384, c.NQH], [1, 256]])
        self.dma("pool", BREV[:], src, r=["gsc"], w=["BREV"])
        brf = BREV[:, :, :].rearrange("p h k -> p (h k)")
        bif = BIAS[:, :, :].rearrange("p h k -> p (h k)")
        for j in range(c.NQH * 256 // 512):
            bi = j % 2
            self.mm(self.PS[bi][:, :], self.CSTB[:, 6, :], brf[:, j * 512:(j + 1) * 512], r=["BREV", "CSTB"], w=[("PS", bi)])
            self.cp(bif[:, j * 512:(j + 1) * 512], self.PS[bi][:, :], r=[("PS", bi)], w=["BIAS"])
        self.dma("pool", MASK0[:], self.din["mask0"][:], w=["MASK0"])
        self.dma("sp", SINKB[:], self.din["sinks"].partition_broadcast(128), w=["SINKB"])

    def proj_qkv(self):
        c = self.cfg
        XN, Q, KD2, VT = self.XN, self.V("Q"), self.V("KD2"), self.V("VT")
        KD = c.KD
        w_in = self.wd["w_in"].rearrange("(k q) n -> q k n", q=128)
        step = 0
        self.mset(KD2[:, :, 0:128], 0.0, w=[("KD2", g, "x") for g in range(c.NKV)])
        self.mset(VT[:, 0, :], 0.0, w=[("VT", 0)])
        for qc in range(c.QC):
            W, wreg = self.slab(w_in[:, :, qc * 128:(qc + 1) * 128])
            for ci, (a, b) in enumerate(c.CH):
                bi = step % 2
                step += 1
                bank = self.PS[bi]
                for k in range(KD):
                    self.mm(bank[:, 0:b - a], W[:, k, :], XN[:, k, a:b], start=(k == 0), stop=(k == KD - 1),
                            r=wreg + [("XN", k)], w=[("PS", bi)])
                self.cp(Q[:, qc, a:b], bank[:, 0:b - a], r=[("PS", bi)], w=self.colregs("Q", qc, a, b), eng="act")
        for g in range(c.NKV):
            W, wreg = self.slab(w_in[:, :, c.ATT + g * 64:c.ATT + (g + 1) * 64], dup64=True)
            for ci, (a, b) in enumerate(c.CH):
                bi = step % 2
                step += 1
                bank = self.PS[bi]
                for k in range(KD):
                    self.mm(bank[:, 0:b - a], W[:, k, :], XN[:, k, a:b], start=(k == 0), stop=(k == KD - 1),
                            r=wreg + [("XN", k)], w=[("PS", bi)])
                self.cp(KD2[:, g, 128 + a:128 + b], bank[:, 0:b - a], r=[("PS", bi)], w=self.colregs("KD2", g, a, b), eng="act")
                if b > c.NP + 1 - 128:
                    lo = max(a, c.NP + 1 - 128)
                    hi = min(b, c.NP + 1)
                    if lo < hi:
                        KST = self.KST
                        self.cp(KST[0:64, g, lo - (c.NP + 1 - 128):hi - (c.NP + 1 - 128)], bank[0:64, lo - a:hi - a],
                                r=[("PS", bi)], w=[("KST", g)])
        for g in range(c.NKV):
            self.dma("sp", self.dout["new_k"][:, g, :], self.KST[0:64, g, :], r=[("KST", g)])
        wv = []
        for vc in range(c.KC):
            W, wreg = self.slab(w_in[:, :, c.ATT + c.KVW + vc * 128:c.ATT + c.KVW + (vc + 1) * 128])
            wv.append((W, wreg))
        self.wv = wv
        for blk in range(c.NB):
            bi = 2 + blk % 2
            bank = self.PS[bi]
            a = 1 + blk * 128
            for vc in range(c.KC):
                W, wreg = wv[vc]
                for k in range(KD):
                    self.mm(bank[:, vc * 128:(vc + 1) * 128], XN[:, k, a:a + 128], W[:, k, :], start=(k == 0), stop=(k == KD - 1),
                            r=wreg + [("XN", k)], w=[("PS", bi)])
            self.cp(VT[:, 1 + blk, :], bank[:, 0:c.KVW], r=[("PS", bi)], w=[("VT", 1 + blk)], eng="act")
            if blk == c.NB - 1:
                self.cp(self.VST[:], bank[:, 0:c.KVW], r=[("PS", bi)], w=["VST"])
                self.dma("sp", self.dout["new_v"][:], self.VST[:], r=["VST"])
        if c.halves > 1:
            nk = c.NKV * 128
            lo = 128 + c.NP + 1 - 128
            self.dma("sp", self.kvx_in[:, 0:nk].rearrange("p (g t) -> p g t", g=c.NKV), KD2[:, :, lo:lo + 128],
                     r=[("KD2", g, c.NB - 1) for g in range(c.NKV)], w=["kvx_in_k"])
            self.dma("sp", self.kvx_in[:, nk:], VT[:, c.NB, :], r=[("VT", c.NB)], w=["kvx_in_v"])
            rg = self.replica_groups()
            kin, kout = self.kvx_in, self.kvx_out
            if self.no_cc:
                self.dma("sp", kout[0:128, :], kin, r=["kvx_in_k", "kvx_in_v"], w=["kvx_out"])
            else:
                self.p.cc(lambda e: e.collective_compute("AllGather", ALU.bypass, replica_groups=rg, ins=[kin], outs=[kout]),
                          r=["kvx_in_k", "kvx_in_v"], w=["kvx_out"])
            self.dma("sp", KD2[:, :, 1:129], self.kvx_out[0:128, 0:nk].rearrange("p (g t) -> p g t", g=c.NKV),
                     r=["kvx_out"], w=[("KD2", g, "x") for g in range(c.NKV)] + [("KD2", g, "h") for g in range(c.NKV)])
            self.dma("sp", VT[:, 0, :], self.kvx_out[0:128, nk:], r=["kvx_out"], w=[("VT", 0)])

    def attention_prompt(self):
        c = self.cfg
        Q, KD2, VT, BIAS, MASK0, SINKB = (self.V(n) for n in ("Q", "KD2", "VT", "BIAS", "MASK0", "SINKB"))
        S, E, PB, PT, MX = (self.V(n) for n in ("S", "E", "PB", "PT", "MX"))
        IDB = self.CSTB[:, 0, :]
        it = 0
        if self.debug:
            self.mset(S[:], 0.0, w=[("S", 0), ("S", 1)])
            self.mset(E[:], 0.0, w=[("E", s_) for s_ in range(4)])
            self.mset(MX[:], 0.0, w=["MX", "NMX", "RDEN"] + [("DEN", s_) for s_ in range(4)])
        order = list(range(1, c.NB)) + [0] if c.halves > 1 else list(range(c.NB))
        for blk in order:
            q0 = 1 + blk * 128
            k0 = 1 + blk * 128
            for g in range(c.NKV):
                pa, pb_ = 2 * (it % 2), 2 * (it % 2) + 1
                it += 1
                bA, bB = self.PS[pa], self.PS[pb_]
                for par, bank, bi in ((0, bA, pa), (1, bB, pb_)):
                    for i in range(2):
                        ch = 2 * g + i
                        self.mm(bank[:, i * 256:(i + 1) * 256], Q[64 * par:64 * par + 64, ch, q0:q0 + 128],
                                KD2[64 * par:64 * par + 64, g, k0:k0 + 256],
                                r=[("Q", ch, blk), ("KD2", g, blk), ("KD2", g, blk - 1 if blk > 0 else "x")], w=[("PS", bi)])
                    h0 = 4 * g + par
                    self.stt(S[:, 2 * par:2 * par + 2, 0:256], bank[:, :].rearrange("p (a b) -> p a b", a=2), 0.125,
                             BIAS[:, h0:h0 + 3:2, :], ALU.mult, ALU.add, r=[("PS", bi), "BIAS"], w=[("S", par)])
                    self.cp(S[:, 2 * par:2 * par + 2, 256], SINKB[:, h0:h0 + 3:2], r=["SINKB"], w=[("S", par)])
                if blk == 0:
                    self.tt(S[:, :, 0:128], S[:, :, 0:128], MASK0[:].unsqueeze(1).broadcast_to([128, 4, 128]), ALU.add,
                            r=[("S", 0), ("S", 1), "MASK0"], w=[("S", 0), ("S", 1)])
                self.red(MX[:, 0:4], S[:, :, 0:257], ALU.max, r=[("S", 0), ("S", 1)], w=["MX"])
                self.ts(MX[:, 4:8], MX[:, 0:4], -1.0, None, ALU.mult, r=["MX"], w=["NMX"])
                for s in range(4):
                    self.act(E[:, s, 0:257], S[:, s, 0:257], AF.Exp, bias=MX[:, 4 + s:5 + s], scale=1.0,
                             accum_out=MX[:, 8 + s:9 + s], r=[("S", 0), ("S", 1), "NMX"], w=[("E", s), ("DEN", s)])
                self.recip(MX[:, 12:16], MX[:, 8:12], r=[("DEN", s) for s in range(4)], w=["RDEN"])
                self.tt(PB[:, :, :], E[:, :, 0:256], MX[:, 12:16].unsqueeze(2).broadcast_to([128, 4, 256]), ALU.mult,
                        r=[("E", s) for s in range(4)] + ["RDEN"], w=["PB"])
                tb = 4 + (it % 2)
                tbank = self.PS[tb][:, :].bitcast(BF16).rearrange("p (a b) -> p a b", a=8)
                for s in range(4):
                    for kb in range(2):
                        self.tr(tbank[:, 2 * s + kb, :], PB[:, s, kb * 128:(kb + 1) * 128], IDB, r=["PB", "CSTB"], w=[("PS", tb)])
                self.cp(PT[:, :, :], tbank, r=[("PS", tb)], w=["PT"], eng="act")
                ob = 6 + (it % 2)
                obank = self.PS[ob]
                for s in range(4):
                    par, i = s // 2, s % 2
                    for kb in range(2):
                        self.mm(obank[64 * par:64 * par + 64, i * 128:(i + 1) * 128], VT[:, blk + kb, g * 64:(g + 1) * 64],
                                PT[:, 2 * s + kb, :], start=(kb == 0), stop=(kb == 1),
                                r=["PT", ("VT", blk + kb)], w=[("PS", ob)])
                self.cp(Q[:, 2 * g:2 * g + 2, q0:q0 + 128], obank[:, 0:256].rearrange("p (a b) -> p a b", a=2),
                        r=[("PS", ob)], w=[("Q", 2 * g, blk), ("Q", 2 * g + 1, blk)], eng="act")


def t5_bucket_np(dist, n_buckets=32, max_distance=128):
    n = np.maximum(dist, 0)
    max_exact = n_buckets // 2
    nf = np.maximum(n, 1).astype(np.float32)
    large = max_exact + (np.log(nf / max_exact) / np.float32(np.log(max_distance / max_exact))
                         * (n_buckets - max_exact)).astype(np.int32)
    large = np.minimum(large, n_buckets - 1)
    return np.where(n < max_exact, n, large)


def host_consts(cfg):
    out = {}
    cst = np.zeros((128, 7, 128), np.float32)
    cst[:, 6, :] = np.eye(128, dtype=np.float32)[::-1]
    cst[:, 0, :] = np.eye(128, dtype=np.float32)
    blk = np.zeros((128, 128), np.float32)
    blk[0:64, 0:64] = 1
    blk[64:, 64:] = 1
    cst[:, 1, :] = blk
    s_ = np.arange(128)[:, None]
    t_ = np.arange(128)[None, :]
    cst[:, 2, :] = (s_ < t_)
    cst[:, 3, :] = (s_ <= t_)
    cst[:, 4, :] = (s_ <= t_) * np.float32(-np.exp(-0.5))
    cst[:, 5, :] = (s_ < t_) * np.float32(-np.exp(-0.5))
    out["cst"] = cst
    oh = np.zeros((33, 384), np.float32)
    for m in range(384):
        d = 255 - m
        if 0 <= d <= 128 and m < 383:
            oh[int(t5_bucket_np(np.array(d))), m] = 1
        else:
            oh[32, m] = 1
    out["oh_rev"] = oh
    return out


class RwkvMixin:
    GN_EPS = 64e-5

    def rwkv_consts(self):
        c = self.cfg
        CF = self.CSTF
        M4, TRIL, IDH, IDHB, ONEF, W0R, LW = (self.V(n) for n in ("MASK4", "TRIL", "IDH", "IDHB", "ONEF", "W0R", "LW"))
        for q in range(4):
            self.cp(M4[0:64, q * 64:(q + 1) * 64], CF[0:64, 2 + (q % 2), 0:64], r=["CSTF"], w=["MASK4"])
        self.ts(TRIL[0:64, :], CF[0:64, 3, 0:64], -1.0, 1.0, ALU.mult, ALU.add, r=["CSTF"], w=["TRIL"])
        self.cp(IDH[0:64, :], CF[0:64, 0, 0:64], r=["CSTF"], w=["IDH"])
        self.cp(IDH[64:128, :], CF[64:128, 0, 64:128], r=["CSTF"], w=["IDH"])
        self.cp(IDHB[:], IDH[:], r=["IDH"], w=["IDHB"])
        self.mset(ONEF[:], 1.0, w=["ONEF"])
        SEL1 = self.V("SEL1")
        self.mset(SEL1[0:64, :], 0.0, w=["SEL1"])
        self.cp(SEL1[0:64, 64:128], CF[0:64, 0, 0:64], r=["CSTF"], w=["SEL1"])
        self.dma("sp", W0R[0:1, :], self.din["w0row"][:], w=["W0R"])
        self.dma("pool", LW[0:64, 0, :], self.din["w2"][:], w=["LW"])
        self.dma("pool", LW[64:128, 0, :], self.din["a2"][:], w=["LW"])
        self.dma("pool", LW[:, 1, :], self.din["g2"][:], w=["LW"])
        self.dma("sp", self.SSH[:], self.din["sshift"][:], w=["SSH"])
        if c.halves > 1:
            self.dma("sp", self.FLAG[:], self.din["flag"][:], w=["FLAG"])

    def blocks(self):
        c = self.cfg
        TB = min(self.TB, c.NP)
        out = [(1 + TB * i, TB, True) for i in range(c.NP // TB)]
        out.append((c.NP + 1, c.NS, False))
        return out

    def proj_cols(self, W, wreg, bank_i, a, wdt):
        KD = self.cfg.KD
        bank = self.PS[bank_i]
        for k in range(KD):
            self.mm(bank[:, 0:wdt + 1], W[:, k, :], self.XN[:, k, a - 1:a + wdt], start=(k == 0), stop=(k == KD - 1),
                    r=wreg + [("XN", k)], w=[("PS", bank_i)])
        return bank

    def rwkv_lora(self):
        c = self.cfg
        PV, LORA, PR, D = self.PVEC, self.V("LORA"), self.V("PR"), self.V("D")
        w_in = self.wd["w_in"].rearrange("(k q) n -> q k n", q=128)
        mu0 = self.pv["mu"]
        for li in range(2):
            chunk = 3 * c.RP + li
            col0 = c.ATT_COLS + chunk * 128
            W, wreg = self.slab(w_in[:, :, col0:col0 + 128])
            for (a, wdt, prompt) in self.blocks():
                bank = self.proj_cols(W, wreg, li, a, wdt)
                self.cp(PR[:, 0, 0:wdt + 1], bank[:, 0:wdt + 1], r=[("PS", li)], w=["PR"], eng="act")
                self.shift_out(PR, 0, chunk, a, wdt, prompt)
                if prompt:
                    self.tt(D[:, 0, 0:wdt], PR[:, 0, 0:wdt], PR[:, 0, 1:wdt + 1], ALU.subtract, r=["PR"], w=["D"])
                else:
                    self.tt(D[:, 0, 0:wdt], self.SSH[:, chunk, :], PR[:, 0, 1:wdt + 1], ALU.subtract, r=["PR", "SSH"], w=["D"])
                self.stt(D[:, 0, 0:wdt], D[:, 0, 0:wdt], PV[:, mu0 + chunk:mu0 + chunk + 1], PR[:, 0, 1:wdt + 1],
                         ALU.mult, ALU.add, r=["D", "PR", "PVEC"], w=["D"])
                if li == 0:
                    self.act(LORA[0:64, 0, a:a + wdt], D[0:64, 0, 0:wdt], AF.Tanh, r=["D"], w=["LORA"])
                    self.cp(LORA[64:128, 0, a:a + wdt], D[64:128, 0, 0:wdt], r=["D"], w=["LORA"])
                else:
                    self.act(LORA[:, 1, a:a + wdt], D[:, 0, 0:wdt], AF.Sigmoid, r=["D"], w=["LORA"])

    def shift_out(self, PR, x, chunk, a, wdt, prompt):
        c = self.cfg
        SHO = self.SHO
        if prompt and a + wdt == c.NP + 1:
            self.cp(SHO[:, chunk, 0:1], PR[:, x, wdt:wdt + 1], r=["PR"], w=["SHO"])
        if not prompt:
            self.cp(SHO[:, chunk, 1:1 + c.NS], PR[:, x, 1:1 + wdt], r=["PR"], w=["SHO"])

    def rwkv(self):
        c = self.cfg
        self.p.barrier()
        self.mset(self.V("MIXR")[:, :, 0:1], 0.0, w=[("MIXR", pr) for pr in range(c.RP)])
        self.rwkv_consts()
        self.rwkv_lora()
        for pr in range(c.RP):
            self.rwkv_pair(pr)
        self.dma("sp", self.dout["shift_o"][:], self.SHO[:], r=["SHO"])

    def rwkv_pair(self, pr):
        c = self.cfg
        w_in = self.wd["w_in"].rearrange("(k q) n -> q k n", q=128)
        slabs = []
        for x in range(3):
            col0 = c.ATT_COLS + x * c.RW + pr * 128
            slabs.append(self.slab(w_in[:, :, col0:col0 + 128]))
        H, HIN = self.V("H"), self.V("HIN")
        for (a, wdt, prompt) in self.blocks():
            self.rwkv_unit(pr, slabs, a, wdt, prompt)
        self.mset(H[:], 0.0, w=["H"])
        if c.halves > 1:
            self.rwkv_chain(False)
            self.dma("sp", self.hx_in[pr], H[:], r=["H"], w=[("hx_in", pr)])
            rg = self.replica_groups()
            hin, hout = self.hx_in[pr], self.hx_out[pr]
            if self.no_cc:
                self.dma("sp", hout[0:128, :], hin, r=[("hx_in", pr)], w=[("hx_out", pr)])
            else:
                self.p.cc(lambda e: e.collective_compute("AllGather", ALU.bypass, replica_groups=rg, ins=[hin], outs=[hout]),
                          r=[("hx_in", pr)], w=[("hx_out", pr)])
            self.dma("sp", HIN[:], self.hx_out[pr][0:128, :], r=[("hx_out", pr)], w=["HIN"])
            self.ts(H[:], HIN[:], self.FLAG[:, 0:1], None, ALU.mult, r=["HIN", "FLAG"], w=["H"])
        self.rwkv_chain(True)
        self.dma("sp", self.dout["wkv_p"][:, pr, :], H[:], r=["H"])
        self.rwkv_post(pr)

    def replica_groups(self):
        c = self.cfg
        return [[i * c.halves + j for j in range(c.halves)] for i in range(c.n_cores // c.halves)]

    def rwkv_unit(self, pr, slabs, a, wdt, prompt):
        c = self.cfg
        PV = self.PVEC
        pvc = lambda nm: PV[:, self.pv[nm] + pr:self.pv[nm] + pr + 1]
        V = self.V
        PR, D, XM, Aa, KK, T1, KP, NBb = (V(n) for n in ("PR", "D", "XM", "Aa", "KK", "T1", "KP", "NBb"))
        BV, Gg = V("BVP")[:, a:a + wdt], V("GGP")[:, a:a + wdt]
        RKb, SQb, LORA, LW = V("RKb"), V("SQb"), V("LORA"), V("LW")
        CF, CB = self.CSTF, self.CSTB
        mu0 = self.pv["mu"]
        w_ = slice(0, wdt)
        for x in range(3):
            W, wreg = slabs[x]
            bank = self.proj_cols(W, wreg, x, a, wdt)
            self.cp(PR[:, x, 0:wdt + 1], bank[:, 0:wdt + 1], r=[("PS", x)], w=["PR"], eng="act")
            self.shift_out(PR, x, x * c.RP + pr, a, wdt, prompt)
        if prompt:
            self.tt(D[:, :, w_], PR[:, :, 0:wdt], PR[:, :, 1:wdt + 1], ALU.subtract, r=["PR"], w=["D"])
        else:
            self.tt(D[:, :, w_], self.SSH[:, pr:pr + 2 * c.RP + 1:c.RP, :], PR[:, :, 1:wdt + 1], ALU.subtract, r=["PR", "SSH"], w=["D"])
        for x in range(3):
            ch = x * c.RP + pr
            self.stt(XM[:, x, w_], D[:, x, w_], PV[:, mu0 + ch:mu0 + ch + 1], PR[:, x, 1:wdt + 1], ALU.mult, ALU.add,
                     r=["D", "PR", "PVEC"], w=["XM"])
        XR, XK, XV = XM[:, 0, w_], XM[:, 1, w_], XM[:, 2, w_]
        b3 = self.PS[3]
        self.mm(b3[:, w_], LW[64:128, 0, pr * 128:(pr + 1) * 128], LORA[64:128, 0, a:a + wdt], r=["LW", "LORA"], w=[("PS", 3)])
        self.act(Aa[:, w_], b3[:, w_], AF.Sigmoid, bias=pvc("a0"), scale=1.0, r=[("PS", 3), "PVEC"], w=["Aa"])
        self.ts(KK[:, w_], XK, pvc("k_k"), None, ALU.mult, r=["XM", "PVEC"], w=["KK"])
        self.act(SQb[:, w_], KK[:, w_], AF.Square, r=["KK"], w=["SQb"])
        self.mm(b3[:, 256:256 + wdt], CB[:, 1, :], SQb[:, w_], r=["SQb", "CSTB"], w=[("PS", 3)])
        self.act(T1[:, w_], b3[:, 256:256 + wdt], AF.Sqrt, r=[("PS", 3)], w=["T1"])
        self.ts(T1[:, w_], T1[:, w_], 1e-12, None, ALU.max, r=["T1"], w=["T1"])
        self.recip(T1[:, w_], T1[:, w_], r=["T1"], w=["T1"])
        self.tt(KK[:, w_], KK[:, w_], T1[:, w_], ALU.mult, r=["KK", "T1"], w=["KK"])
        self.ts(T1[:, w_], Aa[:, w_], -1.0, pvc("k_a"), ALU.add, ALU.mult, r=["Aa", "PVEC", "KK"], w=["T1"])
        self.stt(KP[:, w_], T1[:, w_], 1.0, XK, ALU.add, ALU.mult, r=["T1", "XM"], w=["KP"])
        self.stt(NBb[:, w_], KK[:, w_], -1.0, Aa[:, w_], ALU.mult, ALU.mult, r=["KK", "Aa"], w=["NBb"])
        self.stt(RKb[:, w_], XR, pvc("r_k"), KP[:, w_], ALU.mult, ALU.mult, r=["XM", "KP", "PVEC"], w=["RKb"])
        self.mm(b3[:, w_], CB[:, 1, :], RKb[:, w_], r=["RKb", "CSTB"], w=[("PS", 3)])
        self.tt(BV, b3[:, w_], XV, ALU.mult, r=[("PS", 3), "XM"], w=["BVP"])
        self.mm(b3[:, 256:256 + wdt], LW[:, 1, pr * 128:(pr + 1) * 128], LORA[:, 1, a:a + wdt], r=["LW", "LORA"], w=[("PS", 3)])
        self.cp(Gg, b3[:, 256:256 + wdt], r=[("PS", 3)], w=["GGP"], eng="act")
        if prompt:
            self.rwkv_chunks(pr, a, wdt)
        else:
            self.rwkv_decode_step(pr, a, wdt)

    def rwkv_post(self, pr):
        c = self.cfg
        V = self.V
        PV = self.PVEC
        pvc = lambda nm: PV[:, self.pv[nm] + pr:self.pv[nm] + pr + 1]
        YP, BVP, GGP, DD, RS, MIXR = V("YP"), V("BVP"), V("GGP"), V("DD"), V("RS"), V("MIXR")
        CF = self.CSTF
        for (a, wdt, prompt) in self.blocks():
            w_ = slice(0, wdt)
            Y, BV, Gg = YP[:, a:a + wdt], BVP[:, a:a + wdt], GGP[:, a:a + wdt]
            b6 = self.PS[6]
            self.mm(b6[:, w_], CF[:, 1, :], Y, r=["YP", "CSTF"], w=[("PS", 6)])
            self.stt(DD[:, w_], b6[:, w_], -1.0 / 64, Y, ALU.mult, ALU.add, r=[("PS", 6), "YP"], w=["DD"])
            self.act(RS[:, w_], DD[:, w_], AF.Square, r=["DD"], w=["RS"])
            self.mm(b6[:, 256:256 + wdt], CF[:, 1, :], RS[:, w_], r=["RS", "CSTF"], w=[("PS", 6)])
            self.ts(RS[:, w_], b6[:, 256:256 + wdt], 1.0 / 64, self.GN_EPS, ALU.mult, ALU.add, r=[("PS", 6)], w=["RS"])
            self.act(RS[:, w_], RS[:, w_], AF.Sqrt, r=["RS"], w=["RS"])
            self.recip(RS[:, w_], RS[:, w_], r=["RS"], w=["RS"])
            self.tt(DD[:, w_], DD[:, w_], RS[:, w_], ALU.mult, r=["DD", "RS"], w=["DD"])
            self.stt(DD[:, w_], DD[:, w_], pvc("ln_w"), BV, ALU.mult, ALU.add, r=["DD", "BVP", "PVEC"], w=["DD"])
            self.stt(MIXR[:, pr, a:a + wdt], DD[:, w_], pvc("ln_b"), Gg, ALU.add, ALU.mult, r=["DD", "GGP", "PVEC"], w=[("MIXR", pr)])

    def rwkv_decode_step(self, pr, a, wdt):
        c = self.cfg
        V = self.V
        NS = wdt
        XM, KK, KP, NBb = V("XM"), V("KK"), V("KP"), V("NBb")
        Y = V("YP")[:, a:a + wdt]
        ST, T2, T3, DIAG, VI, SA, YI, SIGF, WDEC, SEL1 = (V(n) for n in ("ST", "T2", "T3", "DIAG", "VI", "SA", "YI", "SIGF", "WDEC", "SEL1"))
        LW, LORA, ONEF = V("LW"), V("LORA"), V("ONEF")
        CF = self.CSTF
        PV = self.PVEC
        w0c = PV[:, self.pv["w0"] + pr:self.pv["w0"] + pr + 1]
        b5 = self.PS[5]
        self.mm(b5[:, 0:NS], LW[0:64, 0, pr * 128:(pr + 1) * 128], LORA[0:64, 0, a:a + NS], r=["LW", "LORA"], w=[("PS", 5)])
        self.act(SIGF[:, 0:NS], b5[:, 0:NS], AF.Sigmoid, bias=w0c, scale=1.0, r=[("PS", 5), "PVEC"], w=["SIGF"])
        self.act(WDEC[:, 0:NS], SIGF[:, 0:NS], AF.Exp, scale=-float(np.exp(-0.5)), r=["SIGF"], w=["WDEC"])
        for n in range(NS):
            src = self.din["wkv_s"][n, 2 * pr:2 * pr + 2, :, :].rearrange("h i j -> i h j")
            self.dma("sp", ST[0:64, n, :, :], src, w=["ST"])
        for hh in range(2):
            self.mm(b5[0:64, 16 + hh * NS:16 + (hh + 1) * NS], CF[:, 0, 64 * hh:64 * hh + 64], XM[:, 2, 0:NS], r=["XM", "CSTF"], w=[("PS", 5)])
        self.cp(VI[0:64, :, :], b5[0:64, 16:16 + 2 * NS].rearrange("p (h n) -> p h n", h=2), r=[("PS", 5)], w=["VI"])
        srcs = (KK[:, 0:NS], WDEC[:, 0:NS], NBb[:, 0:NS], KP[:, 0:NS], XM[:, 0, 0:NS])
        regs = ("KK", "WDEC", "NBb", "KP", "XM")
        B = []
        for q, (sv, rg) in enumerate(zip(srcs, regs)):
            self.tt(DIAG[:, :, :], CF[:, 0, :].unsqueeze(1).broadcast_to([128, NS, 128]), sv.unsqueeze(2).broadcast_to([128, NS, 128]),
                    ALU.mult, r=["CSTF", rg], w=["DIAG"])
            bank = self.PS[q]
            for n in range(NS):
                self.mm(bank[0:64, n * 128:(n + 1) * 128], ONEF[:, 0:64], DIAG[:, n, :], r=["ONEF", "DIAG"], w=[("PS", q)])
            B.append(bank[0:64, 0:NS * 128].rearrange("p (n h j) -> p n h j", n=NS, h=2))
        KKB, WB, NBB, KPB, RB = B
        st = ST[0:64, :, :, :]
        t2, t3 = T2[0:64, :, :, :], T3[0:64, :, :, :]
        self.tt(t2, st, KKB, ALU.mult, r=["ST", ("PS", 0)], w=["T2"])
        self.red(SA[0:64, :, :], t2, ALU.add, r=["T2"], w=["SA"])
        self.tt(t2, st, WB, ALU.mult, r=["ST", ("PS", 1), "SA"], w=["T2"])
        self.tt(t3, NBB, SA[0:64, :, :].unsqueeze(3).broadcast_to([64, NS, 2, 64]), ALU.mult, r=[("PS", 2), "SA"], w=["T3"])
        self.tt(t2, t2, t3, ALU.add, r=["T2", "T3"], w=["T2"])
        self.tt(t3, KPB, VI[0:64, :, :].rearrange("p h n -> p n h").unsqueeze(3).broadcast_to([64, NS, 2, 64]), ALU.mult,
                r=[("PS", 3), "VI", "T2"], w=["T3"])
        self.tt(t2, t2, t3, ALU.add, r=["T2", "T3"], w=["T2"])
        for n in range(NS):
            dst = self.dout["wkv_s_o"][n, 2 * pr:2 * pr + 2, :, :].rearrange("h i j -> i h j")
            self.dma("sp", dst, T2[0:64, n, :, :], r=["T2"])
        self.tt(t3, t2, RB, ALU.mult, r=["T2", ("PS", 4)], w=["T3"])
        self.red(YI[0:64, :, :], t3, ALU.add, r=["T3"], w=["YI"])
        self.mm(b5[:, 32:32 + NS], CF[0:64, 0, :], YI[0:64, :, 0], start=True, stop=False, r=["YI", "CSTF"], w=[("PS", 5)])
        self.mm(b5[:, 32:32 + NS], SEL1[0:64, :], YI[0:64, :, 1], start=False, stop=True, r=["YI", "SEL1"], w=[("PS", 5)])
        self.cp(Y, b5[:, 32:32 + NS], r=[("PS", 5)], w=["YP"])

    def rwkv_chunks(self, pr, a, wdt):
        c = self.cfg
        V = self.V
        XM, KK, KP, NBb = V("XM"), V("KK"), V("KP"), V("NBb")
        c0 = (a - 1) // 64
        Y = V("YP")[:, a:a + wdt]
        W, WCC, KR, KH, NBH, KDE, NBDE, VB, RTF = (V(n) for n in ("W", "WCC", "KR", "KH", "NBH", "KDE", "NBDE", "VB", "RTF"))
        SIGT, TM, AM, NM, TT, MV, PZ = (V(n) for n in ("SIGT", "TM", "AM", "NM", "TT", "MV", "PZ"))
        DIAGW = V("DIAGW")
        NCq = wdt // 64
        PHIT, G, QT = (V(n)[:, c0:c0 + NCq, :] for n in ("PHITP", "GP", "QTP"))
        M4, TRIL, IDH, IDHB, ONEF, W0R, LW, LORA = (V(n) for n in ("MASK4", "TRIL", "IDH", "IDHB", "ONEF", "W0R", "LW", "LORA"))
        CF, CB = self.CSTF, self.CSTB
        NC_ = wdt // 64
        PSA = self.PSALL
        v4 = lambda ap: ap.rearrange("p (a b) -> p a b", a=NC_)
        b5 = self.PS[5].rearrange("p (a b) -> p a b", a=4)
        for cc in range(NC_):
            t0 = a + cc * 64
            self.mm(b5[0:64, cc, :], ONEF[0:1, 0:64], W0R[0:1, pr * 128:(pr + 1) * 128], start=True, stop=False,
                    r=["ONEF", "W0R"], w=[("PS", 5)])
            self.mm(b5[0:64, cc, :], LORA[0:64, 0, t0:t0 + 64], LW[0:64, 0, pr * 128:(pr + 1) * 128], start=False, stop=True,
                    r=["LORA", "LW"], w=[("PS", 5)])
        self.act(SIGT[0:64, 0:NC_, :], b5[0:64, 0:NC_, :], AF.Sigmoid, r=[("PS", 5)], w=["SIGT"])
        b4 = self.PS[4].rearrange("p (i a b) -> p i a b", i=2, a=4)
        for cc in range(NC_):
            for i in range(2):
                self.mm(b4[:, i, cc, :], SIGT[0:64, cc, :], CF[0:64, 4 + i, 0:64], r=["SIGT", "CSTF"], w=[("PS", 4)])
        self.act(W[:, 0, 0:NC_, :], b4[:, 1, 0:NC_, :], AF.Exp, r=[("PS", 4)], w=[("W", 0)])
        self.act(W[:, 1, 0:NC_, :], b4[:, 0, 0:NC_, :], AF.Exp, r=[("PS", 4)], w=[("W", 1)])
        self.act(W[:, 2, 0:NC_, :], b4[:, 0, 0:NC_, :], AF.Exp, scale=-1.0, r=[("PS", 4)], w=[("W", 2)])
        self.cp(WCC[:, 0:NC_], b4[:, 0, 0:NC_, 63], r=[("PS", 4)], w=["WCC"])
        for cc in range(NC_):
            self.act(W[:, 3, cc, :], b4[:, 0, cc, :], AF.Exp, bias=WCC[:, cc:cc + 1], scale=-1.0, r=[("PS", 4), "WCC"], w=[("W", 3)])
        n_ = slice(0, NC_)
        self.tt(KR[:, n_, 0, :], v4(KK[:, 0:wdt]), W[:, 0, n_, :], ALU.mult, r=["KK", ("W", 0)], w=["KR"])
        self.tt(RTF[:, n_, :], v4(XM[:, 0, 0:wdt]), W[:, 1, n_, :], ALU.mult, r=["XM", ("W", 1)], w=["RTF"])
        self.cp(KR[:, n_, 1, :], RTF[:, n_, :], r=["RTF"], w=["KR"], eng="act")
        self.tt(KH[:, n_, :], v4(KP[:, 0:wdt]), W[:, 2, n_, :], ALU.mult, r=["KP", ("W", 2)], w=["KH"])
        self.tt(NBH[:, n_, :], v4(NBb[:, 0:wdt]), W[:, 2, n_, :], ALU.mult, r=["NBb", ("W", 2)], w=["NBH"])
        self.tt(KDE[:, n_, :], v4(KP[:, 0:wdt]), W[:, 3, n_, :], ALU.mult, r=["KP", ("W", 3)], w=["KDE"])
        self.tt(NBDE[:, n_, :], v4(NBb[:, 0:wdt]), W[:, 3, n_, :], ALU.mult, r=["NBb", ("W", 3)], w=["NBDE"])
        self.cp(VB[:, n_, :], v4(XM[:, 2, 0:wdt]), r=["XM"], w=["VB"], eng="act")
        ptm = PSA[:, 3072:4096].bitcast(BF16).rearrange("p (a b) -> p a b", a=16)
        for kind, src in enumerate((None, KDE, NBDE, VB)):
            for cc in range(NC_):
                in_ = KR[:, cc, 0, :] if kind == 0 else src[:, cc, :]
                self.tr(ptm[0:64, kind * 4 + cc, :], in_, CB[:, 0, :], r=["KR", "KDE", "NBDE", "VB", "CSTB"], w=[("PS", 6), ("PS", 7)])
        self.cp(TM[0:64, :, 0:NC_, :], ptm[0:64, :, :].rearrange("p (k a) b -> p k a b", k=4)[:, :, 0:NC_, :],
                r=[("PS", 6), ("PS", 7)], w=["TM"], eng="act")
        for hh in range(2):
            rows = slice(64 * hh, 64 * hh + 64)
            pah = PSA[0:64, 1024 * hh:1024 * hh + 1024].rearrange("p (a b) -> p a b", a=4)
            pnh = self.PS[4 + hh][0:64, 0:256].rearrange("p (a b) -> p a b", a=4)
            for cc in range(NC_):
                krf = KR[rows, cc, :, :].rearrange("p a b -> p (a b)")
                self.mm(pah[:, cc, 0:128], KH[rows, cc, :], krf, r=["KH", "KR"], w=[("PS", 2 * hh), ("PS", 2 * hh + 1)])
                self.mm(pah[:, cc, 128:256], NBH[rows, cc, :], krf, r=["NBH", "KR"], w=[("PS", 2 * hh), ("PS", 2 * hh + 1)])
                self.mm(pnh[:, cc, :], KR[rows, cc, 0, :], NBH[rows, cc, :], r=["NBH", "KR"], w=[("PS", 4 + hh)])
            self.tt(AM[0:64, hh, n_, :], pah[:, n_, :], M4[0:64, :].unsqueeze(1).broadcast_to([64, NC_, 256]), ALU.mult,
                    r=[("PS", 2 * hh), ("PS", 2 * hh + 1), "MASK4"], w=["AM"])
            self.tt(NM[0:64, 0, 1, 4 * hh:4 * hh + NC_, :], pnh[:, n_, :], TRIL[0:64, :].unsqueeze(1).broadcast_to([64, NC_, 64]), ALU.mult,
                    r=[("PS", 4 + hh), "TRIL"], w=[("NM", 0)])
        E_ = [(hh, cc) for hh in range(2) for cc in range(NC_)]
        eidx = lambda hh, cc: 4 * hh + cc
        idb = IDHB[0:64, :]
        for hh in range(2):
            self.tt(TT[0:64, 0, 4 * hh:4 * hh + NC_, :], AM[0:64, hh, n_, 128:192], idb.unsqueeze(1).broadcast_to([64, NC_, 64]), ALU.add,
                    r=["AM", "IDHB"], w=[("TT", 0)])
        b6 = self.PS[6].rearrange("p (a b) -> p a b", a=8)
        b7 = self.PS[7].rearrange("p (a b) -> p a b", a=8)
        b5 = self.PS[5].rearrange("p (a b) -> p a b", a=8)
        for k in range(5):
            cur, nxt = k % 2, (k + 1) % 2
            for (hh, cc) in E_:
                e = eidx(hh, cc)
                ntk = AM[0:64, hh, cc, 128:192] if k == 0 else NM[0:64, cur, 0, e, :]
                nk = NM[0:64, cur, 1, e, :]
                self.mm(b6[0:64, e, :], nk, ntk, r=[("NM", cur), "AM"], w=[("PS", 6)])
                self.mm(b7[0:64, e, :], ntk, nk, r=[("NM", cur), "AM"], w=[("PS", 7)])
            self.cp(NM[0:64, nxt, 0, :, :], b6[0:64, :, :], r=[("PS", 6)], w=[("NM", nxt)], eng="act")
            self.cp(NM[0:64, nxt, 1, :, :], b7[0:64, :, :], r=[("PS", 7)], w=[("NM", nxt)], eng="act")
            for (hh, cc) in E_:
                e = eidx(hh, cc)
                self.mm(b5[0:64, e, :], idb, TT[0:64, cur, e, :], start=True, stop=False, r=[("TT", cur), "IDHB"], w=[("PS", 5)])
                self.mm(b5[0:64, e, :], NM[0:64, nxt, 1, e, :], TT[0:64, cur, e, :], start=False, stop=True,
                        r=[("TT", cur), ("NM", nxt)], w=[("PS", 5)])
            self.cp(TT[0:64, nxt, :, :], b5[0:64, :, :], r=[("PS", 5)], w=[("TT", nxt)])
        TTf = TT[0:64, 1]
        b4 = self.PS[4].rearrange("p (a b) -> p a b", a=8)
        for (hh, cc) in E_:
            e = eidx(hh, cc)
            self.mm(b4[0:64, e, :], AM[0:64, hh, cc, 0:64], TM[0:64, 3, cc, 64 * hh:64 * hh + 64], r=["AM", "TM"], w=[("PS", 4)])
        self.cp(MV[0:64, :, :], b4[0:64, :, :], r=[("PS", 4)], w=["MV"], eng="act")
        pz = PSA[0:64, 0:1024].rearrange("p (a b) -> p a b", a=8)
        for (hh, cc) in E_:
            e = eidx(hh, cc)
            self.mm(pz[:, e, 0:64], TTf[:, e, :], TM[0:64, 0, cc, 64 * hh:64 * hh + 64], r=[("TT", 1), "TM"], w=[("PS", 0), ("PS", 1)])
            self.mm(pz[:, e, 64:128], TTf[:, e, :], MV[0:64, e, :], r=[("TT", 1), "MV"], w=[("PS", 0), ("PS", 1)])
        self.cp(PZ[0:64, :, :], pz, r=[("PS", 0), ("PS", 1)], w=["PZ"])
        php = PSA[:, 1024:1280].rearrange("p (a b) -> p a b", a=4)
        gp = PSA[:, 1280:1536].rearrange("p (a b) -> p a b", a=4)
        qtp = PSA[:, 1536:1792].rearrange("p (a b) -> p a b", a=4)
        y0p = PSA[:, 1792:2048].rearrange("p (a b) -> p a b", a=4)
        for (hh, cc) in E_:
            e = eidx(hh, cc)
            o = slice(64 * hh, 64 * hh + 64)
            hs = slice(64 * hh, 64 * hh + 64)
            P1, Z = PZ[0:64, e, 0:64], PZ[0:64, e, 64:128]
            self.mm(php[o, cc, :], P1, TM[0:64, 2, cc, hs], r=["PZ", "TM"], w=[("PS", 2)])
            self.mm(gp[o, cc, :], TM[0:64, 1, cc, hs], TM[0:64, 3, cc, hs], start=True, stop=False, r=["TM"], w=[("PS", 2)])
            self.mm(gp[o, cc, :], TM[0:64, 2, cc, hs], Z, start=False, stop=True, r=["TM", "PZ"], w=[("PS", 2)])
            self.mm(qtp[o, cc, :], P1, AM[0:64, hh, cc, 192:256], r=["PZ", "AM"], w=[("PS", 3)])
            self.mm(y0p[o, cc, :], TM[0:64, 3, cc, hs], AM[0:64, hh, cc, 64:128], start=True, stop=False, r=["TM", "AM"], w=[("PS", 3)])
            self.mm(y0p[o, cc, :], Z, AM[0:64, hh, cc, 192:256], start=False, stop=True, r=["PZ", "AM"], w=[("PS", 3)])
        self.tt(DIAGW[:, n_, :], IDH[:, :].unsqueeze(1).broadcast_to([128, NC_, 64]),
                W[:, 1, n_, 63:64].broadcast_to([128, NC_, 64]), ALU.mult, r=["IDH", ("W", 1)], w=["DIAGW"])
        self.tt(PHIT[:, n_, :], php[:, n_, :], DIAGW[:, n_, :], ALU.add, r=[("PS", 2), "DIAGW"], w=["PHITP"])
        self.cp(G[:, n_, :], gp[:, n_, :], r=[("PS", 2)], w=["GP"], eng="act")
        self.tt(QT[:, n_, :], qtp[:, n_, :], RTF[:, n_, :], ALU.add, r=[("PS", 3), "RTF"], w=["QTP"])
        self.cp(v4(Y), y0p[:, n_, :], r=[("PS", 3)], w=["YP"], eng="act")

    def rwkv_chain(self, emit_y):
        c = self.cfg
        V = self.V
        PHITP, GP, QTP, H, YP = V("PHITP"), V("GP"), V("QTP"), V("H"), V("YP")
        for blk0 in range(0, c.NCH, 4):
            nb = min(4, c.NCH - blk0)
            psh = [self.PS[4 + hh].rearrange("p (a i b) -> p a i b", a=4, i=2) for hh in range(2)]
            for cc in range(nb):
                ch = blk0 + cc
                for hh in range(2):
                    o = slice(64 * hh, 64 * hh + 64)
                    if emit_y:
                        self.mm(psh[hh][o, cc, 0, :], H[o, :], QTP[o, ch, :], r=["H", "QTP"], w=[("PS", 4 + hh)])
                    self.mm(psh[hh][o, cc, 1, :], PHITP[o, ch, :], H[o, :], r=["H", "PHITP"], w=[("PS", 4 + hh)])
                for hh in range(2):
                    o = slice(64 * hh, 64 * hh + 64)
                    self.tt(H[o, :], psh[hh][o, cc, 1, :], GP[o, ch, :], ALU.add, r=[("PS", 4 + hh), "GP"], w=["H"])
            if emit_y:
                a = 1 + blk0 * 64
                yv = YP[:, a:a + nb * 64].rearrange("p (a b) -> p a b", a=nb)
                for hh in range(2):
                    o = slice(64 * hh, 64 * hh + 64)
                    self.tt(yv[o, :, :], yv[o, :, :], psh[hh][o, 0:nb, 0, :], ALU.add, r=[("PS", 4 + hh), "YP"], w=["YP"])


class DecodeAttMixin:
    def attention_decode(self):
        c = self.cfg
        V = self.V
        NS, NKV, NQH = c.NS, c.NKV, c.NQH
        Q, KD2 = V("Q"), V("KD2")
        CKD, CVD, KTD, QP, SD, ED, PD, PTD, PT1, VNS, VN0, VND, MXD, SINKP = (V(n) for n in (
            "CKD", "CVD", "KTD", "QP", "SD", "ED", "PD", "PTD", "PT1", "VNS", "VN0", "VND", "MXD", "SINKP"))
        CB = self.CSTB
        KD = c.KD
        col0 = c.NP + 1
        GREV = self.GREV
        self.dma("sp", SINKP[0:NQH, 0:1], self.din["sinks"].rearrange("(h o) -> h o", o=1), w=["SINKP"])
        b2 = self.PS[2]
        for vc in range(c.KC):
            W, wreg = self.wv[vc]
            for k in range(KD):
                self.mm(b2[0:NS, vc * 128:(vc + 1) * 128], self.XN[:, k, col0:col0 + NS], W[:, k, :], start=(k == 0), stop=(k == KD - 1),
                        r=wreg + [("XN", k)], w=[("PS", 2)])
        self.cp(VNS[0:NS, :], b2[0:NS, 0:c.KVW], r=[("PS", 2)], w=["VNS"])
        self.dma("sp", self.dout["new_v_s"][:, :], VNS[0:NS, :], r=["VNS"])
        for n in range(NS):
            self.dma("sp", VN0[0:1, n, :], VNS[n:n + 1, :], r=["VNS"], w=["VN0"])
        vn0 = VN0[0:1, :, :].rearrange("p n (g d) -> p n g d", g=NKV)
        vnd = VND[0:1, :, :, :].rearrange("p n g (t d) -> p n g t d", t=2)
        for t in range(2):
            self.cp(vnd[:, :, :, t, :], vn0, r=["VN0"], w=["VND"])
        self.cp(self.KSS[:, :, :], KD2[0:64, :, 128 + col0:128 + col0 + NS], r=[("KD2", g, "s") for g in range(NKV)], w=["KSS"])
        self.dma("sp", self.dout["new_k_s"][:, :, :], self.KSS[:, :, :], r=["KSS"])
        self.dma("sp", self.dout["kc_s"][:, :, :], self.din["cache_k"][:, 1:128, :])
        self.dma("sp", self.dout["vc_s"][:, :, :], self.din["cache_v"][:, 1:128, :])
        self.mset(QP[:], 0.0, w=["QP"])
        for n in range(NS):
            for g in range(NKV):
                for par in range(2):
                    h0 = 4 * g + par
                    self.cp(QP[64 * par:64 * par + 64, n, g, h0:h0 + 3:2], Q[64 * par:64 * par + 64, 2 * g:2 * g + 2, col0 + n],
                            r=[("Q", 2 * g, "s"), ("Q", 2 * g + 1, "s")], w=["QP"])
        self.mset(SD[:], 0.0, w=["SD"])
        for n in range(NS):
            ck = self.din["cache_k"][n].rearrange("t (g d) -> t g d", g=NKV)
            cv = self.din["cache_v"][n].rearrange("t (g d) -> t g d", g=NKV)
            ckd = CKD[:, :, :].rearrange("p g (t d) -> p g t d", t=2)
            cvd = CVD[:, :, :].rearrange("p g (t d) -> p g t d", t=2)
            for t in range(2):
                self.dma("pool", ckd[:, :, t, :], ck, w=["CKD"])
                self.dma("pool", cvd[:, :, t, :], cv, w=["CVD"])
            tb = self.PS[4][:, :].bitcast(BF16).rearrange("p (g k) -> p g k", g=NKV)
            for g in range(NKV):
                self.tr(tb[:, g, 0:128], CKD[:, g, :], CB[:, 0, :], r=["CKD", "CSTB"], w=[("PS", 4)])
            self.cp(KTD[:, :, 0:128], tb[:, :, 0:128], r=[("PS", 4)], w=["KTD"], eng="act")
            self.cp(KTD[:, :, 128:129], KD2[:, :, 128 + col0 + n:128 + col0 + n + 1], r=[("KD2", g, "s") for g in range(NKV)], w=["KTD"])
            b0 = self.PS[0]
            for g in range(NKV):
                self.mm(b0[0:NQH, 0:129], QP[:, n, g, :], KTD[:, g, 0:129], start=(g == 0), stop=(g == NKV - 1),
                        r=["QP", "KTD"], w=[("PS", 0)])
            self.stt(SD[0:NQH, 0:129], b0[0:NQH, 0:129], 0.125, GREV[:, 127:256], ALU.mult, ALU.add, r=[("PS", 0), "GREV"], w=["SD"])
            self.cp(SD[0:NQH, 129:130], SINKP[0:NQH, 0:1], r=["SINKP"], w=["SD"])
            self.red(MXD[0:NQH, 0:1], SD[0:NQH, 0:130], ALU.max, r=["SD"], w=["MXD"])
            self.ts(MXD[0:NQH, 1:2], MXD[0:NQH, 0:1], -1.0, None, ALU.mult, r=["MXD"], w=["MXD"])
            self.act(ED[0:NQH, 0:130], SD[0:NQH, 0:130], AF.Exp, bias=MXD[0:NQH, 1:2], scale=1.0, accum_out=MXD[0:NQH, 2:3],
                     r=["SD", "MXD"], w=["ED", "MXD"])
            self.recip(MXD[0:NQH, 3:4], MXD[0:NQH, 2:3], r=["MXD"], w=["MXD"])
            self.ts(PD[0:NQH, 0:130], ED[0:NQH, 0:130], MXD[0:NQH, 3:4], None, ALU.mult, r=["ED", "MXD"], w=["PD"])
            tp = self.PS[5][:, :].bitcast(BF16)
            self.tr(tp[:, 0:NQH], PD[0:NQH, 0:128], CB[0:NQH, 0, 0:NQH], r=["PD", "CSTB"], w=[("PS", 5)])
            self.tr(tp[0:1, 64:64 + NQH], PD[0:NQH, 128:129], CB[0:NQH, 0, 0:NQH], r=["PD", "CSTB"], w=[("PS", 5)])
            self.cp(PTD[:, 0:NQH], tp[:, 0:NQH], r=[("PS", 5)], w=["PTD"], eng="act")
            self.cp(PT1[0:1, 0:NQH], tp[0:1, 64:64 + NQH], r=[("PS", 5)], w=["PT1"], eng="act")
            b1 = self.PS[1]
            for g in range(NKV):
                self.mm(b1[:, 4 * g:4 * g + 4], CVD[:, g, :], PTD[:, 4 * g:4 * g + 4], start=True, stop=False,
                        r=["CVD", "PTD"], w=[("PS", 1)])
                self.mm(b1[:, 4 * g:4 * g + 4], VND[0:1, n, g, :], PT1[0:1, 4 * g:4 * g + 4], start=False, stop=True,
                        r=["VND", "PT1"], w=[("PS", 1)])
            for g in range(NKV):
                for par in range(2):
                    h0 = 4 * g + par
                    self.cp(Q[64 * par:64 * par + 64, 2 * g:2 * g + 2, col0 + n], b1[64 * par:64 * par + 64, h0:h0 + 3:2],
                            r=[("PS", 1), "QP"], w=[("Q", 2 * g, "s"), ("Q", 2 * g + 1, "s")], eng="act")

    def w_out_phase(self):
        c = self.cfg
        KD = c.KD
        X, Q, MIXR = self.X, self.V("Q"), self.V("MIXR")
        self.p.barrier()
        self.dma("sp", X[:], self.xpark[:], r=["xpark"], w=[("X", k) for k in range(KD)])
        w_out = self.wd["w_out"].rearrange("(k q) n -> q k n", q=128)
        step = 0
        for o in range(KD):
            W, wreg = self.slab(w_out[:, :, o * 128:(o + 1) * 128])
            for ci, (a, b) in enumerate(c.CH):
                bi = step % 2
                step += 1
                bank = self.PS[bi]
                for k in range(KD):
                    if k < c.QC:
                        rhs, rr = Q[:, k, a:b], self.allcols("Q", k)
                    else:
                        rhs, rr = MIXR[:, k - c.QC, a:b], [("MIXR", k - c.QC)]
                    self.mm(bank[:, 0:b - a], W[:, k, :], rhs, start=(k == 0), stop=(k == KD - 1), r=wreg + rr, w=[("PS", bi)])
                self.tt(X[:, o, a:b], bank[:, 0:b - a], X[:, o, a:b], ALU.add, r=[("PS", bi), ("X", o)], w=[("X", o)])


class FullBuilder(DecodeAttMixin, RwkvMixin, MixerBuilder):
    def declare_io(self):
        MixerBuilder.declare_io(self)
        c = self.cfg
        self.inp("w0row", [1, c.RW])
        self.inp("w2", [64, c.RW])
        self.inp("a2", [64, c.RW])
        self.inp("g2", [128, c.RW])
        self.inp("sshift", [128, c.SC, c.NS])
        self.outp("shift_o", [128, c.SC, 1 + c.NS])
        self.outp("wkv_p", [128, c.RP, 64])
        self.inp("wkv_s", [c.NS, c.RH, 64, 64])
        if c.halves > 1:
            self.inp("flag", [128, 1])
            self.hx_in = self.nc.dram_tensor("hx_in", [c.RP, 128, 64], F32, kind="Internal").ap()
            self.hx_out = self.nc.dram_tensor("hx_out", [c.RP, 128 * c.halves, 64], F32, kind="Internal").ap()
            kw = c.NKV * 128 + c.KVW
            self.kvx_in = self.nc.dram_tensor("kvx_in", [128, kw], BF16, kind="Internal").ap()
            self.kvx_out = self.nc.dram_tensor("kvx_out", [128 * c.halves, kw], BF16, kind="Internal").ap()
        self.inp("cache_k", [c.NS, 128, c.KVW])
        self.inp("cache_v", [c.NS, 128, c.KVW])
        self.outp("new_v_s", [c.NS, c.KVW])
        self.outp("new_k_s", [64, c.NKV, c.NS])
        self.outp("kc_s", [c.NS, 127, c.KVW])
        self.outp("vc_s", [c.NS, 127, c.KVW])
        self.outp("wkv_s_o", [c.NS, c.RH, 64, 64])
        if self.debug:
            self.outp("dbg_mixr", [128, c.RP, c.NT], BF16)

    def alloc_extra(self, sb):
        MixerBuilder.alloc_extra(self, sb)
        c = self.cfg
        self.SSH = sb("SSH", [128, c.SC, c.NS], F32)
        self.SHO = sb("SHO", [128, c.SC, 1 + c.NS], F32)
        self.KSS = sb("KSS", [64, c.NKV, c.NS], F32)
        self.FLAG = sb("FLAG", [128, 1], F32)

    def mixer(self):
        c = self.cfg
        self.slab_i = 0
        if "att" in self.parts:
            self.attention_inputs()
            self.proj_qkv()
            self.attention_prompt()
            self.attention_decode()
            if self.debug:
                self.dma("sp", self.dout["dbg_q"][:], self.V("Q")[:], r=[r_ for ch in range(c.QC) for r_ in self.allcols("Q", ch)])
        if "rwkv" in self.parts:
            self.rwkv()
            if self.debug:
                self.dma("sp", self.dout["dbg_mixr"][:, :, 1:c.NT], self.V("MIXR")[:, :, 1:c.NT], r=[("MIXR", pr) for pr in range(c.RP)])
        if "wout" in self.parts:
            self.w_out_phase()

    parts = ("att", "rwkv", "wout")


def _fm(vec, nchunks):
    return np.ascontiguousarray(np.asarray(vec, np.float32).reshape(nchunks, 128).T)


def prep_inputs(cfg, builder, inp):
    c = cfg
    pv = builder.pv
    pvec = np.zeros((128, pv["_n"]), np.float32)

    def setp(name, vec, n):
        pvec[:, pv[name]:pv[name] + n] = _fm(np.asarray(vec).reshape(-1), n)

    setp("n1", inp["ffn1_norm"][0], c.KD)
    setp("nm", inp["mix_norm"][0], c.KD)
    setp("n2", inp["ffn2_norm"][0], c.KD)
    setp("nf", inp["final_norm"], c.KD)
    setp("mu", inp["shift_mu"][0], c.SC)
    for nm, key in (("w0", "decay_w0"), ("a0", "aaa_a0"), ("k_k", "key_k"), ("k_a", "key_a"), ("r_k", "bonus_r_k"),
                    ("ln_w", "ln_x_w"), ("ln_b", "ln_x_b")):
        setp(nm, inp[key][0], c.RP)
    hc = host_consts(c)
    rb = np.concatenate([np.asarray(inp["rel_bias"], np.float32), np.full((1, c.NQH), -30000, np.float32)], 0)
    shared = {
        "pvec": pvec, "cst": hc["cst"], "oh_rev": hc["oh_rev"], "rb_ext": rb,
        "sinks": np.ascontiguousarray(inp["attn_sinks"][0], np.float32),
        "f1g": inp["ffn1_w_gate"][0], "f1u": inp["ffn1_w_up"][0], "f1d": inp["ffn1_w_down"][0],
        "w_in": inp["w_in"][0], "w_out": inp["w_out"][0],
        "f2g": inp["ffn2_w_gate"][0], "f2u": inp["ffn2_w_up"][0], "f2d": inp["ffn2_w_down"][0],
        "w0row": np.ascontiguousarray(inp["decay_w0"][0][None, :]), "w2": inp["decay_w2"][0], "a2": inp["aaa_a2"][0],
        "g2": inp["gate_g2"][0],
    }
    shared = {k: np.ascontiguousarray(v, dtype=np.float32) for k, v in shared.items()}
    maps = []
    for core in range(c.n_cores):
        b, half = core // c.halves, core % c.halves
        t0 = half * c.NP
        xp = inp["x_prompt"][b]
        halo = xp[t0 - 1:t0] if half > 0 else np.zeros((1, c.D), np.float32)
        x = np.concatenate([halo, xp[t0:t0 + c.NP], inp["x_sample"][core * c.NS:(core + 1) * c.NS, 0]], 0)
        xT = np.ascontiguousarray(x.T.reshape(c.KD, 128, c.NT).transpose(1, 0, 2), dtype=np.float32)
        ss = inp["state_shift"][0][core * c.NS:(core + 1) * c.NS]
        sshift = np.ascontiguousarray(ss.T.reshape(c.SC, 128, c.NS).transpose(1, 0, 2), dtype=np.float32)
        m = dict(shared)
        if c.halves > 1:
            m["flag"] = np.full((128, 1), 0.0 if half == 0 else 1.0, np.float32)
        m.update({
            "xT": xT, "sshift": sshift,
            "mask0": np.full((128, 128), -30000.0 if half == 0 else 0.0, np.float32),
            "wkv_s": np.ascontiguousarray(inp["state_wkv"][0][core * c.NS:(core + 1) * c.NS], dtype=np.float32),
            "cache_k": np.ascontiguousarray(inp["cache_k"][0][core * c.NS:(core + 1) * c.NS].reshape(c.NS, 128, c.KVW), dtype=np.float32),
            "cache_v": np.ascontiguousarray(inp["cache_v"][0][core * c.NS:(core + 1) * c.NS].reshape(c.NS, 128, c.KVW), dtype=np.float32),
        })
        maps.append(m)
    return maps


def assemble(cfg, res, batch):
    c = cfg
    S = c.NP * c.halves
    DB = c.n_cores * c.NS
    y_p = np.zeros((batch, S, c.D), np.float32)
    y_s = np.zeros((DB, 1, c.D), np.float32)
    nk_p = np.zeros((1, batch, 128, c.NKV, 64), np.float32)
    nv_p = np.zeros((1, batch, 128, c.NKV, 64), np.float32)
    wkv_p = np.zeros((1, batch, c.RH, 64, 64), np.float32)
    sh_p = np.zeros((1, batch, c.SHIFT_COLS), np.float32)
    nk_s = np.zeros((1, DB, 128, c.NKV, 64), np.float32)
    nv_s = np.zeros((1, DB, 128, c.NKV, 64), np.float32)
    wkv_s = np.zeros((1, DB, c.RH, 64, 64), np.float32)
    sh_s = np.zeros((1, DB, c.SHIFT_COLS), np.float32)
    for core in range(c.n_cores):
        r = res[core]
        b, half = core // c.halves, core % c.halves
        y = np.asarray(r["yT"]).transpose(1, 0, 2).reshape(c.D, c.NP + c.NS).T
        y_p[b, half * c.NP:(half + 1) * c.NP] = y[:c.NP]
        y_s[core * c.NS:(core + 1) * c.NS, 0] = y[c.NP:]
        sho = np.asarray(r["shift_o"])
        shf = sho.transpose(2, 1, 0).reshape(1 + c.NS, c.SHIFT_COLS)
        if half == c.halves - 1:
            nk_p[0, b] = np.asarray(r["new_k"]).transpose(2, 1, 0)
            nv_p[0, b] = np.asarray(r["new_v"]).reshape(128, c.NKV, 64)
            wk = np.asarray(r["wkv_p"])
            wkv_p[0, b] = wk.reshape(2, 64, c.RP, 64).transpose(2, 0, 3, 1).reshape(c.RH, 64, 64)
            sh_p[0, b] = shf[0]
        sl = slice(core * c.NS, (core + 1) * c.NS)
        nk_s[0, sl, :127] = np.asarray(r["kc_s"]).reshape(c.NS, 127, c.NKV, 64)
        nk_s[0, sl, 127] = np.asarray(r["new_k_s"]).transpose(2, 1, 0)
        nv_s[0, sl, :127] = np.asarray(r["vc_s"]).reshape(c.NS, 127, c.NKV, 64)
        nv_s[0, sl, 127] = np.asarray(r["new_v_s"]).reshape(c.NS, c.NKV, 64)
        wkv_s[0, sl] = np.asarray(r["wkv_s_o"])
        sh_s[0, sl] = shf[1:]
    return (y_p, y_s, nk_p, nv_p, wkv_p, sh_p, nk_s, nv_s, wkv_s, sh_s)


def kernel(**inputs):
    inp = {k: np.asarray(v) for k, v in inputs.items()}
    cfg = Cfg(D=2048, DFF=5504, NP=1024, NS=4, n_cores=8, halves=2, GS=8)
    b = FullBuilder(cfg)
    nc = b.build()
    maps = prep_inputs(cfg, b, inp)
    res = run_bass_kernel_spmd(nc, maps, core_ids=list(range(cfg.n_cores)))
    return assemble(cfg, res.results, inp["x_prompt"].shape[0])
```

```python
import numpy as np
import concourse.bass as bass
import concourse.mybir as mybir
from concourse.bass_utils import run_bass_kernel_spmd

F32 = mybir.dt.float32
BF16 = mybir.dt.bfloat16
I32 = mybir.dt.int32
AF = mybir.ActivationFunctionType
ALU = mybir.AluOpType
AX = mybir.AxisListType

ENGS = ("pe", "act", "dve", "pool", "sp")
DMAQ = ("sp", "act", "pool")
NSEM_DMA = 12
SEG = 6000


class Prog:
    def __init__(self):
        self.ops = {e: [] for e in ENGS}
        self.lastw = {}
        self.lastr = {}
        self.ndma = {q: 0 for q in DMAQ}
        self.pending = {e: [] for e in ENGS}

    def _deps(self, eng, r, w):
        deps = list(self.pending[eng])
        self.pending[eng] = []
        for reg in r:
            lw = self.lastw.get(reg)
            if lw is not None:
                deps.append(lw)
        for reg in w:
            lw = self.lastw.get(reg)
            if lw is not None and not (lw[0] == "eng" and lw[1] == eng and eng == "pe"):
                deps.append(lw)
            for k, ref in self.lastr.get(reg, {}).items():
                if not (ref[0] == "eng" and ref[1] == eng and eng == "pe"):
                    deps.append(ref)
        return deps

    def _mark(self, ref, r, w):
        for reg in r:
            key = (ref[0], ref[1]) if ref[0] == "eng" else ("dma", ref[1], ref[2] % NSEM_DMA)
            if ref[0] == "cc":
                key = ("cc", ref[2])
            self.lastr.setdefault(reg, {})[key] = ref
        for reg in w:
            self.lastw[reg] = ref
            self.lastr[reg] = {}

    @staticmethod
    def _excl(r, w):
        extra = [x for x in r if isinstance(x, tuple) and x[0] == "PS" and x not in w]
        return list(r), list(w) + extra

    def op(self, eng, fn, r=(), w=()):
        r, w = self._excl(r, w)
        deps = self._deps(eng, r, w)
        idx = len(self.ops[eng])
        self.ops[eng].append({"fn": fn, "deps": deps, "sig": False, "dma": None})
        self._mark(("eng", eng, idx), r, w)
        return ("eng", eng, idx)

    def dma(self, q, fn, r=(), w=()):
        deps = self._deps(q, r, w)
        k = self.ndma[q]
        self.ndma[q] += 1
        if k >= NSEM_DMA:
            deps.append(("dma", q, k - NSEM_DMA))
        idx = len(self.ops[q])
        self.ops[q].append({"fn": fn, "deps": deps, "sig": False, "dma": k})
        self._mark(("dma", q, k), r, w)
        return ("dma", q, k)

    def cc(self, fn, r=(), w=()):
        deps = self._deps("pool", r, w)
        k = getattr(self, "ncc", 0)
        self.ncc = k + 1
        self.ops["pool"].append({"fn": fn, "deps": deps, "sig": False, "dma": None, "cc": k})
        self._mark(("cc", "pool", k), r, w)
        return ("cc", "pool", k)

    def barrier(self):
        refs = []
        for e in ENGS:
            if self.ops[e]:
                last = len(self.ops[e]) - 1
                j = last
                while j >= 0 and (self.ops[e][j]["dma"] is not None or self.ops[e][j].get("cc") is not None):
                    j -= 1
                if j >= 0:
                    refs.append(("eng", e, j))
        for q in DMAQ:
            n = self.ndma[q]
            for k in range(max(0, n - NSEM_DMA), n):
                refs.append(("dma", q, k))
        for k in range(getattr(self, "ncc", 0)):
            refs.append(("cc", "pool", k))
        for e in ENGS:
            self.pending[e] = list(refs)

    def emit(self, nc, stack):
        for e in ENGS:
            for o in self.ops[e]:
                for d in o["deps"]:
                    if d[0] == "eng":
                        self.ops[d[1]][d[2]]["sig"] = True
        sigval = {}
        nsig = {}
        for e in ENGS:
            c = 0
            for i, o in enumerate(self.ops[e]):
                if o["sig"]:
                    c += 1
                    sigval[(e, i)] = c
            nsig[e] = c
        esem = {}
        for e in ENGS:
            nseg = (nsig[e] + SEG - 1) // SEG
            esem[e] = [stack.enter_context(nc.semaphore(f"s_{e}_{j}")) for j in range(max(1, nseg))]
        dsem = {q: [stack.enter_context(nc.semaphore(f"d_{q}_{j}")) for j in range(NSEM_DMA)]
                for q in DMAQ if self.ndma[q] > 0}
        csem = [stack.enter_context(nc.semaphore(f"cc_{j}")) for j in range(getattr(self, "ncc", 0))]
        block = stack.enter_context(nc.Block())
        prog = self

        def emit_engine(e, eng):
            waited = {}
            for i, o in enumerate(prog.ops[e]):
                for d in o["deps"]:
                    if d[0] == "eng":
                        if d[1] == e and d[2] >= i:
                            continue
                        v = sigval[(d[1], d[2])]
                        key = ("eng", d[1])
                        if waited.get(key, 0) >= v:
                            continue
                        waited[key] = v
                        seg, val = (v - 1) // SEG, (v - 1) % SEG + 1
                        eng.wait_ge(esem[d[1]][seg], val)
                    elif d[0] == "cc":
                        key = ("cc", d[2])
                        if waited.get(key, 0) >= 1:
                            continue
                        waited[key] = 1
                        eng.wait_ge(csem[d[2]], 1)
                    else:
                        _, q, k = d
                        key = ("dma", q, k % NSEM_DMA)
                        v = 16 * (k // NSEM_DMA + 1)
                        if waited.get(key, 0) >= v:
                            continue
                        waited[key] = v
                        eng.wait_ge(dsem[q][k % NSEM_DMA], v)
                ins = o["fn"](eng)
                if o.get("cc") is not None:
                    ins.then_inc(csem[o["cc"]])
                elif o["dma"] is not None:
                    ins.then_inc(dsem[e][o["dma"] % NSEM_DMA], 16)
                elif o["sig"]:
                    v = sigval[(e, i)]
                    ins.then_inc(esem[e][(v - 1) // SEG], 1)
            if e in DMAQ and prog.ndma[e] > 0:
                n = prog.ndma[e]
                for j in range(NSEM_DMA):
                    cnt = (n - j + NSEM_DMA - 1) // NSEM_DMA if n > j else 0
                    if cnt > 0 and waited.get(("dma", e, j), 0) < 16 * cnt:
                        eng.wait_ge(dsem[e][j], 16 * cnt)

        @block.tensor
        def _(t):
            emit_engine("pe", t)

        @block.scalar
        def _(s):
            emit_engine("act", s)

        @block.vector
        def _(v):
            emit_engine("dve", v)

        @block.gpsimd
        def _(g):
            emit_engine("pool", g)

        @block.sync
        def _(s):
            emit_engine("sp", s)


class Cfg:
    def __init__(self, D=2048, DFF=5504, NP=1024, NS=4, n_cores=8, halves=2, GS=8):
        self.D, self.DFF, self.NP, self.NS = D, DFF, NP, NS
        self.n_cores, self.halves, self.GS = n_cores, halves, GS
        self.KD = D // 128
        self.ATT = D // 2
        self.QC = self.ATT // 128
        self.NQH = self.ATT // 64
        self.NKV = self.NQH // 4
        self.KVW = self.NKV * 64
        self.KC = self.KVW // 128
        self.RW = D - self.ATT
        self.RH = self.RW // 64
        self.RP = self.RW // 128
        self.NF = (DFF + 127) // 128
        assert DFF % 128 == 0
        self.NT = NP + NS + 1
        self.NTP = (self.NT + 1) // 2 * 2
        self.ATT_COLS = self.ATT + 2 * self.KVW
        self.SHIFT_COLS = 3 * self.RW + 256
        self.IN_COLS = self.ATT_COLS + self.SHIFT_COLS
        self.SC = self.SHIFT_COLS // 128
        self.NB = NP // 128
        self.NCH = NP // 64
        n = (self.NT + 511) // 512
        base, rem = divmod(self.NT, n)
        self.CH = []
        s = 0
        for i in range(n):
            w = base + (1 if i < rem else 0)
            self.CH.append((s, s + w))
            s += w


def _groups(n, gs):
    out, s = [], 0
    while s < n:
        out.append((s, min(n, s + gs)))
        s += gs
    return out


class Builder:
    def __init__(self, cfg, stages=("ffn1", "mixer", "ffn2")):
        self.cfg = cfg
        self.stages = stages
        self.p = Prog()
        self.nc = bass.Bass("TRN2", target_bir_lowering=False)
        self.din = {}
        self.dout = {}
        self.psum_rr = 0

    def inp(self, name, shape, dt=F32):
        self.din[name] = self.nc.dram_tensor(name, list(shape), dt, kind="ExternalInput").ap()
        return self.din[name]

    def outp(self, name, shape, dt=F32):
        self.dout[name] = self.nc.dram_tensor(name, list(shape), dt, kind="ExternalOutput").ap()
        return self.dout[name]

    def pvec_layout(self):
        c = self.cfg
        off = {}
        o = 0
        for nm, n in (("n1", c.KD), ("nm", c.KD), ("n2", c.KD), ("nf", c.KD), ("mu", c.SC),
                      ("w0", c.RP), ("a0", c.RP), ("k_k", c.RP), ("k_a", c.RP), ("r_k", c.RP),
                      ("ln_w", c.RP), ("ln_b", c.RP)):
            off[nm] = o
            o += n
        off["_n"] = o
        return off

    def build(self):
        import contextlib
        c, nc, p = self.cfg, self.nc, self.p
        KD, NT, NF = c.KD, c.NT, c.NF
        with contextlib.ExitStack() as st:
            self.st = st
            sb = lambda name, shape, dt: st.enter_context(nc.sbuf_tensor(name, list(shape), dt))
            xT = self.inp("xT", [128, KD, NT])
            pv = self.pvec_layout()
            self.pv = pv
            pvec_d = self.inp("pvec", [128, pv["_n"]])
            cst_d = self.inp("cst", [128, 7, 128])
            wd = {}
            for nm, shp in (("f1g", [c.D, c.DFF]), ("f1u", [c.D, c.DFF]), ("f1d", [c.DFF, c.D]),
                            ("w_in", [c.D, c.IN_COLS]), ("w_out", [c.D, c.D]),
                            ("f2g", [c.D, c.DFF]), ("f2u", [c.D, c.DFF]), ("f2d", [c.DFF, c.D])):
                wd[nm] = self.inp(nm, shp)
            self.wd = wd
            yT = self.outp("yT", [128, KD, c.NP + c.NS])
            self.declare_io()

            XN = sb("XN", [128, KD, NT], BF16)
            PVEC = sb("PVEC", [128, pv["_n"]], F32)
            CSTF = sb("CSTF", [128, 7, 128], F32)
            CSTB = sb("CSTB", [128, 7, 128], BF16)
            ONESB = sb("ONESB", [128, 128], BF16)
            RSTD = sb("RSTD", [128, 512], F32)
            SQ = sb("SQ", [128, 2, 512], BF16)
            self.XN, self.PVEC, self.CSTF, self.CSTB, self.ONESB = XN, PVEC, CSTF, CSTB, ONESB
            self.RSTD, self.SQ = RSTD, SQ
            self.SG = sb("SG", [128, 2, 512], F32)
            self.alloc_extra(sb)
            SCR_BYTES = self.scratch_bytes()
            SCR = sb("SCR", [128, SCR_BYTES // 4], F32)
            self.SCR = SCR
            self.XB = (KD * NT * 4 + 63) // 64 * 64
            X = self.carve(0, [KD, NT], F32)
            self.X = X
            self.xpark = nc.dram_tensor("xpark", [128, KD, NT], F32, kind="Internal").ap()
            self.PSALL = st.enter_context(nc.psum_tensor("psall", [128, 4096], F32))
            self.PS = [self.PSALL[:, 512 * i:512 * (i + 1)] for i in range(8)]

            self.dma("sp", X[:], xT[:], w=[("X", k) for k in range(KD)])
            self.dma("sp", PVEC[:], pvec_d[:], w=["PVEC"])
            self.dma("sp", CSTF[:], cst_d[:], w=["CSTF"])
            self.cp(CSTB[:], CSTF[:], r=["CSTF"], w=["CSTB"])
            self.mset(ONESB[:], 1.0, w=["ONESB"])

            if "ffn1" in self.stages:
                self.rmsnorm("n1")
                self.ffn("f1g", "f1u", "f1d")
            if "mixer" in self.stages:
                self.rmsnorm("nm")
                self.dma("sp", self.xpark[:], X[:], r=[("X", k) for k in range(KD)], w=["xpark"])
                p.barrier()
                self.mixer()
                p.barrier()
            if "ffn2" in self.stages:
                self.rmsnorm("n2")
                self.ffn("f2g", "f2u", "f2d")
            self.rmsnorm("nf", final=True)
            self.dma("sp", yT[:], X[:, :, 1:NT], r=[("X", k) for k in range(KD)])
            self.extra_outputs()
            p.emit(nc, st)
        return nc

    def extra_outputs(self):
        pass

    def declare_io(self):
        pass

    def alloc_extra(self, sb):
        pass

    def mixer(self):
        pass

    def carve(self, off_bytes, shape, dt):
        n = int(np.prod(shape))
        esz = 4 if dt == F32 else 2
        assert off_bytes % 4 == 0
        nb = n * esz
        assert nb % 4 == 0
        assert off_bytes + nb <= self.SCR.shape[1] * 4, (off_bytes, nb, self.SCR.shape)
        v = self.SCR[:, off_bytes // 4:(off_bytes + nb) // 4]
        if dt != F32:
            v = v.bitcast(dt)
        if len(shape) == 1:
            return v
        names = "abcdefg"[:len(shape)]
        pat = "p (" + " ".join(names) + ") -> p " + " ".join(names)
        return v.rearrange(pat, **{n: int(sz) for n, sz in zip(names[:-1], shape[:-1])})

    def scratch_bytes(self):
        c = self.cfg
        ffn = (c.GS * c.NTP * 2 + 63) // 64 * 64 + c.GS * c.D * 2 + 4 * 2 * c.KD * 128 * 2
        ffn = (ffn + 63) // 64 * 64
        xb = (c.KD * c.NT * 4 + 63) // 64 * 64
        return max(xb + ffn, self.mixer_bytes())

    def mixer_bytes(self):
        return 0

    def ps(self):
        b = self.PS[self.psum_rr % 8]
        self.psum_rr += 1
        return b

    def mm(self, out, lhsT, rhs, start=True, stop=True, r=(), w=()):
        return self.p.op("pe", lambda e: e.matmul(out, lhsT=lhsT, rhs=rhs, start=start, stop=stop), r, w)

    def tr(self, out, in_, ident, r=(), w=()):
        return self.p.op("pe", lambda e: e.transpose(out, in_, ident), r, w)

    def act(self, out, in_, func, bias=None, scale=None, r=(), w=(), accum_out=None):
        kw = {}
        if bias is not None:
            kw["bias"] = bias
        if scale is not None:
            kw["scale"] = scale
        if accum_out is not None:
            kw["accum_out"] = accum_out
        return self.p.op("act", lambda e: e.activation(out=out, in_=in_, func=func, **kw), r, w)

    def ts(self, out, in0, s1, s2, op0, op1=None, r=(), w=(), eng="dve"):
        if op1 is None:
            return self.p.op(eng, lambda e: e.tensor_scalar(out=out, in0=in0, scalar1=s1, scalar2=None, op0=op0), r, w)
        return self.p.op(eng, lambda e: e.tensor_scalar(out=out, in0=in0, scalar1=s1, scalar2=s2, op0=op0, op1=op1), r, w)

    def stt(self, out, in0, scalar, in1, op0, op1, r=(), w=()):
        return self.p.op("dve", lambda e: e.scalar_tensor_tensor(out=out, in0=in0, scalar=scalar, in1=in1, op0=op0, op1=op1), r, w)

    def tt(self, out, in0, in1, op, r=(), w=(), eng="dve"):
        return self.p.op(eng, lambda e: e.tensor_tensor(out=out, in0=in0, in1=in1, op=op), r, w)

    def cp(self, out, in_, r=(), w=(), eng="dve"):
        if eng == "act":
            return self.p.op("act", lambda e: e.copy(out=out, in_=in_), r, w)
        return self.p.op(eng, lambda e: e.tensor_copy(out=out, in_=in_), r, w)

    def red(self, out, in_, op, r=(), w=()):
        return self.p.op("dve", lambda e: e.tensor_reduce(out=out, in_=in_, axis=AX.X, op=op), r, w)

    def recip(self, out, in_, r=(), w=()):
        return self.p.op("dve", lambda e: e.reciprocal(out=out, in_=in_), r, w)

    def rsqrt_pool(self, out, in_, r=(), w=()):
        self.ts(out, in_, 1e-24, None, ALU.max, r=r, w=w)
        self.act(out, out, AF.Ln, r=w, w=w)
        return self.act(out, out, AF.Exp, scale=-0.5, r=w, w=w)

    def mset(self, ap, val, r=(), w=(), eng="dve"):
        return self.p.op(eng, lambda e: e.memset(ap, val), r, w)

    def dma(self, q, out, in_, r=(), w=()):
        return self.p.dma(q, lambda e: e.dma_start(out=out, in_=in_), r, w)

    def rmsnorm(self, gname, final=False):
        c = self.cfg
        X, XN, PVEC, ONESB, RSTD, SQ = self.X, self.XN, self.PVEC, self.ONESB, self.RSTD, self.SQ
        KD = c.KD
        goff = self.pv[gname]
        for ci, (a, b) in enumerate(c.CH):
            w = b - a
            bi = 6 + (ci % 2)
            bank = self.PS[bi]
            breg = ("PS", bi)
            for k in range(KD):
                sl = k % 2
                self.act(SQ[:, sl, 0:w], X[:, k, a:b], AF.Square, r=[("X", k)], w=[("SQ", sl)])
                self.mm(bank[:, 0:w], ONESB[:], SQ[:, sl, 0:w], start=(k == 0), stop=(k == KD - 1),
                        r=[("SQ", sl), "ONESB"], w=[breg])
            self.ts(RSTD[:, 0:w], bank[:, 0:w], 1.0 / c.D, 1e-5, ALU.mult, ALU.add, r=[breg], w=["RSTD"])
            self.rsqrt_pool(RSTD[:, 0:w], RSTD[:, 0:w], r=["RSTD"], w=["RSTD"])
            for k in range(KD):
                dst = X if final else XN
                dreg = ("X", k) if final else ("XN", k)
                self.stt(dst[:, k, a:b], X[:, k, a:b], PVEC[:, goff + k:goff + k + 1], RSTD[:, 0:w],
                         ALU.mult, ALU.mult, r=[("X", k), "RSTD", "PVEC"], w=[dreg])

    def ffn(self, gname, uname, dname):
        c = self.cfg
        X, XN, SG = self.X, self.XN, self.SG
        KD, NT, GS = c.KD, c.NT, c.GS
        HB = self.carve(self.XB, [GS, c.NTP], BF16)
        o1 = self.XB + (GS * c.NTP * 2 + 63) // 64 * 64
        WD = self.carve(o1, [GS, c.D], BF16)
        o2 = o1 + GS * c.D * 2
        NSLOT = 4
        WGU = self.carve(o2, [NSLOT * 2, KD, 128], BF16)
        wg_d = self.wd[gname].rearrange("(k q) n -> q k n", q=128)
        wu_d = self.wd[uname].rearrange("(k q) n -> q k n", q=128)
        wdn_d = self.wd[dname].rearrange("(j q) n -> q j n", q=128)
        step = 0
        slot_i = 0
        for (g0, g1) in _groups(c.NF, GS):
            ng = g1 - g0
            self.dma("pool", WD[:, 0:ng, :], wdn_d[:, g0:g1, :], w=[("WD", j) for j in range(ng)])
            for j in range(g0, g1):
                sl = slot_i % NSLOT
                slot_i += 1
                self.dma("pool", WGU[:, 2 * sl, :, :], wg_d[:, :, j * 128:(j + 1) * 128], w=[("WGU", 2 * sl)])
                self.dma("pool", WGU[:, 2 * sl + 1, :, :], wu_d[:, :, j * 128:(j + 1) * 128], w=[("WGU", 2 * sl + 1)])
                for ci, (a, b) in enumerate(c.CH):
                    w = b - a
                    bi = (step % 2) * 2
                    sgs = step % 2
                    step += 1
                    bg, bu = self.PS[bi], self.PS[bi + 1]
                    for k in range(KD):
                        self.mm(bg[:, 0:w], WGU[:, 2 * sl, k, :], XN[:, k, a:b], start=(k == 0), stop=(k == KD - 1),
                                r=[("WGU", 2 * sl), ("XN", k)], w=[("PS", bi)])
                    for k in range(KD):
                        self.mm(bu[:, 0:w], WGU[:, 2 * sl + 1, k, :], XN[:, k, a:b], start=(k == 0), stop=(k == KD - 1),
                                r=[("WGU", 2 * sl + 1), ("XN", k)], w=[("PS", bi + 1)])
                    self.act(SG[:, sgs, 0:w], bg[:, 0:w], AF.Silu, r=[("PS", bi)], w=[("SG", sgs)])
                    self.tt(HB[:, j - g0, a:b], bu[:, 0:w], SG[:, sgs, 0:w], ALU.mult,
                            r=[("PS", bi + 1), ("SG", sgs)], w=[("HB", j - g0, ci)])
            dstep = 0
            for o in range(KD):
                for ci, (a, b) in enumerate(c.CH):
                    w = b - a
                    bi = 4 + (dstep % 2)
                    dstep += 1
                    bd = self.PS[bi]
                    for jj in range(ng):
                        self.mm(bd[:, 0:w], WD[:, jj, o * 128:(o + 1) * 128], HB[:, jj, a:b], start=(jj == 0), stop=(jj == ng - 1),
                                r=[("WD", jj), ("HB", jj, ci)], w=[("PS", bi)])
                    self.stt(X[:, o, a:b], bd[:, 0:w], 0.5, X[:, o, a:b], ALU.mult, ALU.add,
                             r=[("PS", bi), ("X", o)], w=[("X", o)])


class MixerBuilder(Builder):
    TB = 256
    NW = 4

    def layout(self):
        c = self.cfg
        L = {}
        xb = (c.KD * c.NT * 4 + 63) // 64 * 64
        o = 0

        def add(name, shape, dt):
            nonlocal o
            esz = 4 if dt == F32 else 2
            L[name] = (o, shape, dt)
            o += (int(np.prod(shape)) * esz + 63) // 64 * 64

        add("KD2", [c.NKV, 128 + c.NTP], BF16)
        add("VT", [c.NB + 1, c.KVW], BF16)
        add("BIAS", [c.NQH, 256], BF16)
        add("BREV", [c.NQH, 256], BF16)
        add("MASK0", [128], BF16)
        add("SINKB", [c.NQH], F32)
        add("S", [4, 260], F32)
        add("E", [4, 260], F32)
        add("PB", [4, 256], BF16)
        add("PT", [8, 128], BF16)
        add("MX", [16], F32)
        add("CKD", [c.NKV, 128], BF16)
        add("CVD", [c.NKV, 128], BF16)
        add("KTD", [c.NKV, 130], BF16)
        add("QP", [c.NS, c.NKV, c.NQH], BF16)
        add("SD", [132], F32)
        add("ED", [132], F32)
        add("PD", [132], BF16)
        add("PTD", [c.NQH + 16], BF16)
        add("PT1", [c.NQH + 16], BF16)
        add("VNS", [c.KVW], F32)
        add("VN0", [c.NS, c.KVW], F32)
        add("VND", [c.NS, c.NKV, 128], BF16)
        add("MXD", [8], F32)
        add("SINKP", [2], F32)
        oa1 = o
        o = 0
        TB = self.TB
        for nm in ("PR",):
            add(nm, [3, TB + 1], F32)
        add("D", [3, TB], F32)
        add("XM", [3, TB], F32)
        for nm in ("Aa", "KK", "T1", "KP", "NBb", "DD", "RS"):
            add(nm, [TB], F32)
        for nm in ("YP", "BVP", "GGP"):
            add(nm, [c.NT], F32)
        for nm in ("PHITP", "GP", "QTP"):
            add(nm, [c.NCH, 64], F32)
        add("HIN", [64], F32)
        for nm in ("RKb", "SQb"):
            add(nm, [TB], BF16)
        add("W", [4, 4, 64], F32)
        add("WCC", [4], F32)
        add("KR", [4, 2, 64], BF16)
        for nm in ("KH", "NBH", "KDE", "NBDE", "VB"):
            add(nm, [4, 64], BF16)
        add("RTF", [4, 64], F32)
        add("SIGT", [4, 128], F32)
        add("TM", [4, 4, 128], BF16)
        add("AM", [2, 4, 256], BF16)
        add("NM", [2, 2, 8, 64], BF16)
        add("TT", [2, 8, 64], BF16)
        add("MV", [8, 64], BF16)
        add("PZ", [8, 128], BF16)
        add("H", [64], F32)
        add("DIAGW", [4, 64], F32)
        add("MASK4", [256], F32)
        add("TRIL", [64], F32)
        add("IDH", [64], F32)
        add("IDHB", [64], BF16)
        add("ST", [c.NS, 2, 64], F32)
        add("T2", [c.NS, 2, 64], F32)
        add("T3", [c.NS, 2, 64], F32)
        add("DIAG", [c.NS, 128], F32)
        add("VI", [2, c.NS], F32)
        add("SA", [c.NS, 2], F32)
        add("YI", [c.NS, 2], F32)
        add("SIGF", [c.NS], F32)
        add("WDEC", [c.NS], F32)
        add("SEL1", [128], F32)
        o = max(xb, oa1, o)
        add("WR", [self.NW, c.KD, 128], BF16)
        add("Q", [c.QC, c.NTP], BF16)
        add("MIXR", [c.RP, c.NTP], BF16)
        add("LORA", [3, c.NTP], BF16)
        add("LW", [2, c.RW], BF16)
        add("W0R", [c.RW], F32)
        add("ONEF", [128], F32)
        self.L = L
        return o

    def mixer_bytes(self):
        return self.layout()

    def declare_io(self):
        c = self.cfg
        self.inp("oh_rev", [33, 384])
        self.inp("rb_ext", [33, c.NQH])
        self.inp("mask0", [128, 128])
        self.inp("sinks", [c.NQH])
        self.outp("new_k", [64, c.NKV, 128])
        self.outp("new_v", [128, c.KVW])
        self.gscr = self.nc.dram_tensor("gscr", [c.NQH, 384], F32, kind="Internal").ap()
        if self.debug:
            self.outp("dbg_q", [128, c.QC, c.NTP], BF16)

    def alloc_extra(self, sb):
        c = self.cfg
        self.KST = sb("KST", [64, c.NKV, 128], F32)
        self.VST = sb("VST", [128, c.KVW], F32)

    debug = False
    no_cc = False

    def dbg(self, name, ap, regs, dt=F32):
        if not self.debug:
            return
        d = self.nc.dram_tensor("dbg_" + name, list(ap.shape), dt, kind="ExternalOutput").ap()
        self.dma("sp", d, ap, r=regs)

    def V(self, name):
        off, shape, dt = self.L[name]
        return self.carve(off, shape, dt)

    def colregs(self, name, c_, a, b):
        cfg = self.cfg
        regs = set()
        for col in (a, b - 1):
            pass
        if a == 0:
            regs.add((name, c_, "h"))
        lo, hi = max(a, 1), min(b, cfg.NP + 1)
        if lo < hi:
            for blk in range((lo - 1) // 128, (hi - 2) // 128 + 1):
                regs.add((name, c_, blk))
        if b > cfg.NP + 1:
            regs.add((name, c_, "s"))
        return list(regs)

    def allcols(self, name, c_):
        return [(name, c_, "h")] + [(name, c_, b) for b in range(self.cfg.NB)] + [(name, c_, "s")]

    def slab(self, src_ap, dup64=False):
        WR = self.V("WR")
        sl = self.slab_i % self.NW
        self.slab_i += 1
        reg = ("WR", sl)
        if dup64:
            self.dma("pool", WR[:, sl, :, 0:64], src_ap, w=[reg])
            self.dma("pool", WR[:, sl, :, 64:128], src_ap, w=[])
            self.p.lastw[reg] = ("dma", "pool", self.p.ndma["pool"] - 1)
            self.p.lastw[("WRa", sl)] = ("dma", "pool", self.p.ndma["pool"] - 2)
            return WR[:, sl], [reg, ("WRa", sl)]
        self.dma("pool", WR[:, sl], src_ap, w=[reg])
        return WR[:, sl], [reg]

    def mixer(self):
        c = self.cfg
        self.slab_i = 0
        self.attention_inputs()
        self.proj_qkv()
        self.attention_prompt()
        if self.debug:
            self.dma("sp", self.dout["dbg_q"][:], self.V("Q")[:], r=[r_ for ch in range(c.QC) for r_ in self.allcols("Q", ch)])
            self.dbg("S", self.V("S"), [("S", 0), ("S", 1)])
            self.dbg("E", self.V("E"), [("E", s_) for s_ in range(4)])
            self.dbg("MX", self.V("MX"), ["MX", "NMX", "RDEN"])
            self.dbg("BIAS", self.V("BIAS"), ["BIAS"], BF16)
            self.dbg("KD2", self.V("KD2"), [], BF16)

    def attention_inputs(self):
        c = self.cfg
        BIAS, MASK0, SINKB = self.V("BIAS"), self.V("MASK0"), self.V("SINKB")
        oh = self.din["oh_rev"]
        rb = self.din["rb_ext"]
        OH = self.st.enter_context(self.nc.sbuf_tensor("OH", [33, 384], F32))
        RB = self.st.enter_context(self.nc.sbuf_tensor("RB", [33, c.NQH], F32))
        GREV = self.st.enter_context(self.nc.sbuf_tensor("GREV", [c.NQH, 384], F32))
        self.GREV = GREV
        self.dma("sp", OH[:], oh[:], w=["OH"])
        self.dma("sp", RB[:], rb[:], w=["RB"])
        bank = self.PS[7]
        self.mm(bank[0:c.NQH, 0:384], RB[:], OH[:], r=["OH", "RB"], w=[("PS", 7)])
        self.cp(GREV[:], bank[0:c.NQH, 0:384], r=[("PS", 7)], w=["GREV"])
        gsc = self.gscr
        self.dma("sp", gsc[:], GREV[:], r=["GREV"], w=["gsc"])
        BREV = self.V("BREV")
        src = bass.AP(tensor=gsc.tensor, offset=0, ap=[[1, 128], [384, c.NQH], [1, 256]])
        self.dma("pool", BREV[:], src, r=["gsc"], w=["BREV"])
        brf = BREV[:, :, :].rearrange("p h k -> p (h k)")
        bif = BIAS[:, :, :].rearrange("p h k -> p (h k)")
        for j in range(c.NQH * 256 // 512):
            bi = j % 2
            self.mm(self.PS[bi][:, :], self.CSTB[:, 6, :], brf[:, j * 512:(j + 1) * 512], r=["BREV", "CSTB"], w=[("PS", bi)])
            self.cp(bif[:, j * 512:(j + 1) * 512], self.PS[bi][:, :], r=[("PS", bi)], w=["BIAS"])
        self.dma("pool", MASK0[:], self.din["mask0"][:], w=["MASK0"])
        self.dma("sp", SINKB[:], self.din["sinks"].partition_broadcast(128), w=["SINKB"])

    def proj_qkv(self):
        c = self.cfg
        XN, Q, KD2, VT = self.XN, self.V("Q"), self.V("KD2"), self.V("VT")
        KD = c.KD
        w_in = self.wd["w_in"].rearrange("(k q) n -> q k n", q=128)
        step = 0
        self.mset(KD2[:, :, 0:128], 0.0, w=[("KD2", g, "x") for g in range(c.NKV)])
        self.mset(VT[:, 0, :], 0.0, w=[("VT", 0)])
        for qc in range(c.QC):
            W, wreg = self.slab(w_in[:, :, qc * 128:(qc + 1) * 128])
            for ci, (a, b) in enumerate(c.CH):
                bi = step % 2
                step += 1
                bank = self.PS[bi]
                for k in range(KD):
                    self.mm(bank[:, 0:b - a], W[:, k, :], XN[:, k, a:b], start=(k == 0), stop=(k == KD - 1),
                            r=wreg + [("XN", k)], w=[("PS", bi)])
                self.cp(Q[:, qc, a:b], bank[:, 0:b - a], r=[("PS", bi)], w=self.colregs("Q", qc, a, b), eng="act")
        for g in range(c.NKV):
            W, wreg = self.slab(w_in[:, :, c.ATT + g * 64:c.ATT + (g + 1) * 64], dup64=True)
            for ci, (a, b) in enumerate(c.CH):
                bi = step % 2
                step += 1
                bank = self.PS[bi]
                for k in range(KD):
                    self.mm(bank[:, 0:b - a], W[:, k, :], XN[:, k, a:b], start=(k == 0), stop=(k == KD - 1),
                            r=wreg + [("XN", k)], w=[("PS", bi)])
                self.cp(KD2[:, g, 128 + a:128 + b], bank[:, 0:b - a], r=[("PS", bi)], w=self.colregs("KD2", g, a, b), eng="act")
                if b > c.NP + 1 - 128:
                    lo = max(a, c.NP + 1 - 128)
                    hi = min(b, c.NP + 1)
                    if lo < hi:
                        KST = self.KST
                        self.cp(KST[0:64, g, lo - (c.NP + 1 - 128):hi - (c.NP + 1 - 128)], bank[0:64, lo - a:hi - a],
                                r=[("PS", bi)], w=[("KST", g)])
        for g in range(c.NKV):
            self.dma("sp", self.dout["new_k"][:, g, :], self.KST[0:64, g, :], r=[("KST", g)])
        wv = []
        for vc in range(c.KC):
            W, wreg = self.slab(w_in[:, :, c.ATT + c.KVW + vc * 128:c.ATT + c.KVW + (vc + 1) * 128])
            wv.append((W, wreg))
        self.wv = wv
        for blk in range(c.NB):
            bi = 2 + blk % 2
            bank = self.PS[bi]
            a = 1 + blk * 128
            for vc in range(c.KC):
                W, wreg = wv[vc]
                for k in range(KD):
                    self.mm(bank[:, vc * 128:(vc + 1) * 128], XN[:, k, a:a + 128], W[:, k, :], start=(k == 0), stop=(k == KD - 1),
                            r=wreg + [("XN", k)], w=[("PS", bi)])
            self.cp(VT[:, 1 + blk, :], bank[:, 0:c.KVW], r=[("PS", bi)], w=[("VT", 1 + blk)], eng="act")
            if blk == c.NB - 1:
                self.cp(self.VST[:], bank[:, 0:c.KVW], r=[("PS", bi)], w=["VST"])
                self.dma("sp", self.dout["new_v"][:], self.VST[:], r=["VST"])
        if c.halves > 1:
            nk = c.NKV * 128
            lo = 128 + c.NP + 1 - 128
            self.dma("sp", self.kvx_in[:, 0:nk].rearrange("p (g t) -> p g t", g=c.NKV), KD2[:, :, lo:lo + 128],
                     r=[("KD2", g, c.NB - 1) for g in range(c.NKV)], w=["kvx_in_k"])
            self.dma("sp", self.kvx_in[:, nk:], VT[:, c.NB, :], r=[("VT", c.NB)], w=["kvx_in_v"])
            rg = self.replica_groups()
            kin, kout = self.kvx_in, self.kvx_out
            if self.no_cc:
                self.dma("sp", kout[0:128, :], kin, r=["kvx_in_k", "kvx_in_v"], w=["kvx_out"])
            else:
                self.p.cc(lambda e: e.collective_compute("AllGather", ALU.bypass, replica_groups=rg, ins=[kin], outs=[kout]),
                          r=["kvx_in_k", "kvx_in_v"], w=["kvx_out"])
            self.dma("sp", KD2[:, :, 1:129], self.kvx_out[0:128, 0:nk].rearrange("p (g t) -> p g t", g=c.NKV),
                     r=["kvx_out"], w=[("KD2", g, "x") for g in range(c.NKV)] + [("KD2", g, "h") for g in range(c.NKV)])
            self.dma("sp", VT[:, 0, :], self.kvx_out[0:128, nk:], r=["kvx_out"], w=[("VT", 0)])

    def attention_prompt(self):
        c = self.cfg
        Q, KD2, VT, BIAS, MASK0, SINKB = (self.V(n) for n in ("Q", "KD2", "VT", "BIAS", "MASK0", "SINKB"))
        S, E, PB, PT, MX = (self.V(n) for n in ("S", "E", "PB", "PT", "MX"))
        IDB = self.CSTB[:, 0, :]
        it = 0
        if self.debug:
            self.mset(S[:], 0.0, w=[("S", 0), ("S", 1)])
            self.mset(E[:], 0.0, w=[("E", s_) for s_ in range(4)])
            self.mset(MX[:], 0.0, w=["MX", "NMX", "RDEN"] + [("DEN", s_) for s_ in range(4)])
        order = list(range(1, c.NB)) + [0] if c.halves > 1 else list(range(c.NB))
        for blk in order:
            q0 = 1 + blk * 128
            k0 = 1 + blk * 128
            for g in range(c.NKV):
                pa, pb_ = 2 * (it % 2), 2 * (it % 2) + 1
                it += 1
                bA, bB = self.PS[pa], self.PS[pb_]
                for par, bank, bi in ((0, bA, pa), (1, bB, pb_)):
                    for i in range(2):
                        ch = 2 * g + i
                        self.mm(bank[:, i * 256:(i + 1) * 256], Q[64 * par:64 * par + 64, ch, q0:q0 + 128],
                                KD2[64 * par:64 * par + 64, g, k0:k0 + 256],
                                r=[("Q", ch, blk), ("KD2", g, blk), ("KD2", g, blk - 1 if blk > 0 else "x")], w=[("PS", bi)])
                    h0 = 4 * g + par
                    self.stt(S[:, 2 * par:2 * par + 2, 0:256], bank[:, :].rearrange("p (a b) -> p a b", a=2), 0.125,
                             BIAS[:, h0:h0 + 3:2, :], ALU.mult, ALU.add, r=[("PS", bi), "BIAS"], w=[("S", par)])
                    self.cp(S[:, 2 * par:2 * par + 2, 256], SINKB[:, h0:h0 + 3:2], r=["SINKB"], w=[("S", par)])
                if blk == 0:
                    self.tt(S[:, :, 0:128], S[:, :, 0:128], MASK0[:].unsqueeze(1).broadcast_to([128, 4, 128]), ALU.add,
                            r=[("S", 0), ("S", 1), "MASK0"], w=[("S", 0), ("S", 1)])
                self.red(MX[:, 0:4], S[:, :, 0:257], ALU.max, r=[("S", 0), ("S", 1)], w=["MX"])
                self.ts(MX[:, 4:8], MX[:, 0:4], -1.0, None, ALU.mult, r=["MX"], w=["NMX"])
                for s in range(4):
                    self.act(E[:, s, 0:257], S[:, s, 0:257], AF.Exp, bias=MX[:, 4 + s:5 + s], scale=1.0,
                             accum_out=MX[:, 8 + s:9 + s], r=[("S", 0), ("S", 1), "NMX"], w=[("E", s), ("DEN", s)])
                self.recip(MX[:, 12:16], MX[:, 8:12], r=[("DEN", s) for s in range(4)], w=["RDEN"])
                self.tt(PB[:, :, :], E[:, :, 0:256], MX[:, 12:16].unsqueeze(2).broadcast_to([128, 4, 256]), ALU.mult,
                        r=[("E", s) for s in range(4)] + ["RDEN"], w=["PB"])
                tb = 4 + (it % 2)
                tbank = self.PS[tb][:, :].bitcast(BF16).rearrange("p (a b) -> p a b", a=8)
                for s in range(4):
                    for kb in range(2):
                        self.tr(tbank[:, 2 * s + kb, :], PB[:, s, kb * 128:(kb + 1) * 128], IDB, r=["PB", "CSTB"], w=[("PS", tb)])
                self.cp(PT[:, :, :], tbank, r=[("PS", tb)], w=["PT"], eng="act")
                ob = 6 + (it % 2)
                obank = self.PS[ob]
                for s in range(4):
                    par, i = s // 2, s % 2
                    for kb in range(2):
                        self.mm(obank[64 * par:64 * par + 64, i * 128:(i + 1) * 128], VT[:, blk + kb, g * 64:(g + 1) * 64],
                                PT[:, 2 * s + kb, :], start=(kb == 0), stop=(kb == 1),
                                r=["PT", ("VT", blk + kb)], w=[("PS", ob)])
                self.cp(Q[:, 2 * g:2 * g + 2, q0:q0 + 128], obank[:, 0:256].rearrange("p (a b) -> p a b", a=2),
                        r=[("PS", ob)], w=[("Q", 2 * g, blk), ("Q", 2 * g + 1, blk)], eng="act")


def t5_bucket_np(dist, n_buckets=32, max_distance=128):
    n = np.maximum(dist, 0)
    max_exact = n_buckets // 2
    nf = np.maximum(n, 1).astype(np.float32)
    large = max_exact + (np.log(nf / max_exact) / np.float32(np.log(max_distance / max_exact))
                         * (n_buckets - max_exact)).astype(np.int32)
    large = np.minimum(large, n_buckets - 1)
    return np.where(n < max_exact, n, large)


def host_consts(cfg):
    out = {}
    cst = np.zeros((128, 7, 128), np.float32)
    cst[:, 6, :] = np.eye(128, dtype=np.float32)[::-1]
    cst[:, 0, :] = np.eye(128, dtype=np.float32)
    blk = np.zeros((128, 128), np.float32)
    blk[0:64, 0:64] = 1
    blk[64:, 64:] = 1
    cst[:, 1, :] = blk
    s_ = np.arange(128)[:, None]
    t_ = np.arange(128)[None, :]
    cst[:, 2, :] = (s_ < t_)
    cst[:, 3, :] = (s_ <= t_)
    cst[:, 4, :] = (s_ <= t_) * np.float32(-np.exp(-0.5))
    cst[:, 5, :] = (s_ < t_) * np.float32(-np.exp(-0.5))
    out["cst"] = cst
    oh = np.zeros((33, 384), np.float32)
    for m in range(384):
        d = 255 - m
        if 0 <= d <= 128 and m < 383:
            oh[int(t5_bucket_np(np.array(d))), m] = 1
        else:
            oh[32, m] = 1
    out["oh_rev"] = oh
    return out


class RwkvMixin:
    GN_EPS = 64e-5

    def rwkv_consts(self):
        c = self.cfg
        CF = self.CSTF
        M4, TRIL, IDH, IDHB, ONEF, W0R, LW = (self.V(n) for n in ("MASK4", "TRIL", "IDH", "IDHB", "ONEF", "W0R", "LW"))
        for q in range(4):
            self.cp(M4[0:64, q * 64:(q + 1) * 64], CF[0:64, 2 + (q % 2), 0:64], r=["CSTF"], w=["MASK4"])
        self.ts(TRIL[0:64, :], CF[0:64, 3, 0:64], -1.0, 1.0, ALU.mult, ALU.add, r=["CSTF"], w=["TRIL"])
        self.cp(IDH[0:64, :], CF[0:64, 0, 0:64], r=["CSTF"], w=["IDH"])
        self.cp(IDH[64:128, :], CF[64:128, 0, 64:128], r=["CSTF"], w=["IDH"])
        self.cp(IDHB[:], IDH[:], r=["IDH"], w=["IDHB"])
        self.mset(ONEF[:], 1.0, w=["ONEF"])
        SEL1 = self.V("SEL1")
        self.mset(SEL1[0:64, :], 0.0, w=["SEL1"])
        self.cp(SEL1[0:64, 64:128], CF[0:64, 0, 0:64], r=["CSTF"], w=["SEL1"])
        self.dma("sp", W0R[0:1, :], self.din["w0row"][:], w=["W0R"])
        self.dma("pool", LW[0:64, 0, :], self.din["w2"][:], w=["LW"])
        self.dma("pool", LW[64:128, 0, :], self.din["a2"][:], w=["LW"])
        self.dma("pool", LW[:, 1, :], self.din["g2"][:], w=["LW"])
        self.dma("sp", self.SSH[:], self.din["sshift"][:], w=["SSH"])
        if c.halves > 1:
            self.dma("sp", self.FLAG[:], self.din["flag"][:], w=["FLAG"])

    def blocks(self):
        c = self.cfg
        TB = min(self.TB, c.NP)
        out = [(1 + TB * i, TB, True) for i in range(c.NP // TB)]
        out.append((c.NP + 1, c.NS, False))
        return out

    def proj_cols(self, W, wreg, bank_i, a, wdt):
        KD = self.cfg.KD
        bank = self.PS[bank_i]
        for k in range(KD):
            self.mm(bank[:, 0:wdt + 1], W[:, k, :], self.XN[:, k, a - 1:a + wdt], start=(k == 0), stop=(k == KD - 1),
                    r=wreg + [("XN", k)], w=[("PS", bank_i)])
        return bank

    def rwkv_lora(self):
        c = self.cfg
        PV, LORA, PR, D = self.PVEC, self.V("LORA"), self.V("PR"), self.V("D")
        w_in = self.wd["w_in"].rearrange("(k q) n -> q k n", q=128)
        mu0 = self.pv["mu"]
        for li in range(2):
            chunk = 3 * c.RP + li
            col0 = c.ATT_COLS + chunk * 128
            W, wreg = self.slab(w_in[:, :, col0:col0 + 128])
            for (a, wdt, prompt) in self.blocks():
                bank = self.proj_cols(W, wreg, li, a, wdt)
                self.cp(PR[:, 0, 0:wdt + 1], bank[:, 0:wdt + 1], r=[("PS", li)], w=["PR"], eng="act")
                self.shift_out(PR, 0, chunk, a, wdt, prompt)
                if prompt:
                    self.tt(D[:, 0, 0:wdt], PR[:, 0, 0:wdt], PR[:, 0, 1:wdt + 1], ALU.subtract, r=["PR"], w=["D"])
                else:
                    self.tt(D[:, 0, 0:wdt], self.SSH[:, chunk, :], PR[:, 0, 1:wdt + 1], ALU.subtract, r=["PR", "SSH"], w=["D"])
                self.stt(D[:, 0, 0:wdt], D[:, 0, 0:wdt], PV[:, mu0 + chunk:mu0 + chunk + 1], PR[:, 0, 1:wdt + 1],
                         ALU.mult, ALU.add, r=["D", "PR", "PVEC"], w=["D"])
                if li == 0:
                    self.act(LORA[0:64, 0, a:a + wdt], D[0:64, 0, 0:wdt], AF.Tanh, r=["D"], w=["LORA"])
                    self.cp(LORA[64:128, 0, a:a + wdt], D[64:128, 0, 0:wdt], r=["D"], w=["LORA"])
                else:
                    self.act(LORA[:, 1, a:a + wdt], D[:, 0, 0:wdt], AF.Sigmoid, r=["D"], w=["LORA"])

    def shift_out(self, PR, x, chunk, a, wdt, prompt):
        c = self.cfg
        SHO = self.SHO
        if prompt and a + wdt == c.NP + 1:
            self.cp(SHO[:, chunk, 0:1], PR[:, x, wdt:wdt + 1], r=["PR"], w=["SHO"])
        if not prompt:
            self.cp(SHO[:, chunk, 1:1 + c.NS], PR[:, x, 1:1 + wdt], r=["PR"], w=["SHO"])

    def rwkv(self):
        c = self.cfg
        self.p.barrier()
        self.mset(self.V("MIXR")[:, :, 0:1], 0.0, w=[("MIXR", pr) for pr in range(c.RP)])
        self.rwkv_consts()
        self.rwkv_lora()
        for pr in range(c.RP):
            self.rwkv_pair(pr)
        self.dma("sp", self.dout["shift_o"][:], self.SHO[:], r=["SHO"])

    def rwkv_pair(self, pr):
        c = self.cfg
        w_in = self.wd["w_in"].rearrange("(k q) n -> q k n", q=128)
        slabs = []
        for x in range(3):
            col0 = c.ATT_COLS + x * c.RW + pr * 128
            slabs.append(self.slab(w_in[:, :, col0:col0 + 128]))
        H, HIN = self.V("H"), self.V("HIN")
        blks = self.blocks()
        for (a, wdt, prompt) in blks[:-1]:
            self.rwkv_unit(pr, slabs, a, wdt, prompt)
        if c.halves == 1:
            self.rwkv_unit(pr, slabs, *blks[-1])
        self.mset(H[:], 0.0, w=["H"])
        if c.halves > 1:
            self.rwkv_chain(False)
            self.dma("sp", self.hx_in[pr], H[:], r=["H"], w=[("hx_in", pr)])
            rg = self.replica_groups()
            hin, hout = self.hx_in[pr], self.hx_out[pr]
            if self.no_cc:
                self.dma("sp", hout[0:128, :], hin, r=[("hx_in", pr)], w=[("hx_out", pr)])
            else:
                self.p.cc(lambda e: e.collective_compute("AllGather", ALU.bypass, replica_groups=rg, ins=[hin], outs=[hout]),
                          r=[("hx_in", pr)], w=[("hx_out", pr)])
            self.rwkv_unit(pr, slabs, *blks[-1])
            self.dma("sp", HIN[:], self.hx_out[pr][0:128, :], r=[("hx_out", pr)], w=["HIN"])
            self.ts(H[:], HIN[:], self.FLAG[:, 0:1], None, ALU.mult, r=["HIN", "FLAG"], w=["H"])
        self.rwkv_chain(True)
        self.dma("sp", self.dout["wkv_p"][:, pr, :], H[:], r=["H"])
        self.rwkv_post(pr)

    def replica_groups(self):
        c = self.cfg
        return [[i * c.halves + j for j in range(c.halves)] for i in range(c.n_cores // c.halves)]

    def rwkv_unit(self, pr, slabs, a, wdt, prompt):
        c = self.cfg
        PV = self.PVEC
        pvc = lambda nm: PV[:, self.pv[nm] + pr:self.pv[nm] + pr + 1]
        V = self.V
        PR, D, XM, Aa, KK, T1, KP, NBb = (V(n) for n in ("PR", "D", "XM", "Aa", "KK", "T1", "KP", "NBb"))
        BV, Gg = V("BVP")[:, a:a + wdt], V("GGP")[:, a:a + wdt]
        RKb, SQb, LORA, LW = V("RKb"), V("SQb"), V("LORA"), V("LW")
        CF, CB = self.CSTF, self.CSTB
        mu0 = self.pv["mu"]
        w_ = slice(0, wdt)
        for x in range(3):
            W, wreg = slabs[x]
            bank = self.proj_cols(W, wreg, x, a, wdt)
            self.cp(PR[:, x, 0:wdt + 1], bank[:, 0:wdt + 1], r=[("PS", x)], w=["PR"], eng="act")
            self.shift_out(PR, x, x * c.RP + pr, a, wdt, prompt)
        if prompt:
            self.tt(D[:, :, w_], PR[:, :, 0:wdt], PR[:, :, 1:wdt + 1], ALU.subtract, r=["PR"], w=["D"])
        else:
            self.tt(D[:, :, w_], self.SSH[:, pr:pr + 2 * c.RP + 1:c.RP, :], PR[:, :, 1:wdt + 1], ALU.subtract, r=["PR", "SSH"], w=["D"])
        for x in range(3):
            ch = x * c.RP + pr
            self.stt(XM[:, x, w_], D[:, x, w_], PV[:, mu0 + ch:mu0 + ch + 1], PR[:, x, 1:wdt + 1], ALU.mult, ALU.add,
                     r=["D", "PR", "PVEC"], w=["XM"])
        XR, XK, XV = XM[:, 0, w_], XM[:, 1, w_], XM[:, 2, w_]
        b3 = self.PS[3]
        self.mm(b3[:, w_], LW[64:128, 0, pr * 128:(pr + 1) * 128], LORA[64:128, 0, a:a + wdt], r=["LW", "LORA"], w=[("PS", 3)])
        self.act(Aa[:, w_], b3[:, w_], AF.Sigmoid, bias=pvc("a0"), scale=1.0, r=[("PS", 3), "PVEC"], w=["Aa"])
        self.ts(KK[:, w_], XK, pvc("k_k"), None, ALU.mult, r=["XM", "PVEC"], w=["KK"])
        self.tt(SQb[:, w_], KK[:, w_], KK[:, w_], ALU.mult, r=["KK"], w=["SQb"])
        self.mm(b3[:, 256:256 + wdt], CB[:, 1, :], SQb[:, w_], r=["SQb", "CSTB"], w=[("PS", 3)])
        self.cp(T1[:, w_], b3[:, 256:256 + wdt], r=[("PS", 3)], w=["T1"], eng="act")
        self.rsqrt_pool(T1[:, w_], T1[:, w_], r=["T1"], w=["T1"])
        self.tt(KK[:, w_], KK[:, w_], T1[:, w_], ALU.mult, r=["KK", "T1"], w=["KK"])
        self.ts(T1[:, w_], Aa[:, w_], -1.0, pvc("k_a"), ALU.add, ALU.mult, r=["Aa", "PVEC", "KK"], w=["T1"])
        self.stt(KP[:, w_], T1[:, w_], 1.0, XK, ALU.add, ALU.mult, r=["T1", "XM"], w=["KP"])
        self.stt(NBb[:, w_], KK[:, w_], -1.0, Aa[:, w_], ALU.mult, ALU.mult, r=["KK", "Aa"], w=["NBb"])
        self.stt(RKb[:, w_], XR, pvc("r_k"), KP[:, w_], ALU.mult, ALU.mult, r=["XM", "KP", "PVEC"], w=["RKb"])
        self.mm(b3[:, w_], CB[:, 1, :], RKb[:, w_], r=["RKb", "CSTB"], w=[("PS", 3)])
        self.tt(BV, b3[:, w_], XV, ALU.mult, r=[("PS", 3), "XM"], w=["BVP"])
        self.mm(b3[:, 256:256 + wdt], LW[:, 1, pr * 128:(pr + 1) * 128], LORA[:, 1, a:a + wdt], r=["LW", "LORA"], w=[("PS", 3)])
        self.cp(Gg, b3[:, 256:256 + wdt], r=[("PS", 3)], w=["GGP"], eng="act")
        if prompt:
            self.rwkv_chunks(pr, a, wdt)
        else:
            self.rwkv_decode_step(pr, a, wdt)

    def rwkv_post(self, pr):
        c = self.cfg
        V = self.V
        PV = self.PVEC
        pvc = lambda nm: PV[:, self.pv[nm] + pr:self.pv[nm] + pr + 1]
        YP, BVP, GGP, DD, RS, MIXR = V("YP"), V("BVP"), V("GGP"), V("DD"), V("RS"), V("MIXR")
        CF = self.CSTF
        for (a, wdt, prompt) in self.blocks():
            w_ = slice(0, wdt)
            Y, BV, Gg = YP[:, a:a + wdt], BVP[:, a:a + wdt], GGP[:, a:a + wdt]
            b6 = self.PS[6]
            self.mm(b6[:, w_], CF[:, 1, :], Y, r=["YP", "CSTF"], w=[("PS", 6)])
            self.stt(DD[:, w_], b6[:, w_], -1.0 / 64, Y, ALU.mult, ALU.add, r=[("PS", 6), "YP"], w=["DD"])
            self.tt(RS[:, w_], DD[:, w_], DD[:, w_], ALU.mult, r=["DD"], w=["RS"])
            self.mm(b6[:, 256:256 + wdt], CF[:, 1, :], RS[:, w_], r=["RS", "CSTF"], w=[("PS", 6)])
            self.ts(RS[:, w_], b6[:, 256:256 + wdt], 1.0 / 64, self.GN_EPS, ALU.mult, ALU.add, r=[("PS", 6)], w=["RS"])
            self.rsqrt_pool(RS[:, w_], RS[:, w_], r=["RS"], w=["RS"])
            self.tt(DD[:, w_], DD[:, w_], RS[:, w_], ALU.mult, r=["DD", "RS"], w=["DD"])
            self.stt(DD[:, w_], DD[:, w_], pvc("ln_w"), BV, ALU.mult, ALU.add, r=["DD", "BVP", "PVEC"], w=["DD"])
            self.stt(MIXR[:, pr, a:a + wdt], DD[:, w_], pvc("ln_b"), Gg, ALU.add, ALU.mult, r=["DD", "GGP", "PVEC"], w=[("MIXR", pr)])

    def rwkv_decode_step(self, pr, a, wdt):
        c = self.cfg
        V = self.V
        NS = wdt
        XM, KK, KP, NBb = V("XM"), V("KK"), V("KP"), V("NBb")
        Y = V("YP")[:, a:a + wdt]
        ST, T2, T3, DIAG, VI, SA, YI, SIGF, WDEC, SEL1 = (V(n) for n in ("ST", "T2", "T3", "DIAG", "VI", "SA", "YI", "SIGF", "WDEC", "SEL1"))
        LW, LORA, ONEF = V("LW"), V("LORA"), V("ONEF")
        CF = self.CSTF
        PV = self.PVEC
        w0c = PV[:, self.pv["w0"] + pr:self.pv["w0"] + pr + 1]
        b5 = self.PS[5]
        self.mm(b5[:, 0:NS], LW[0:64, 0, pr * 128:(pr + 1) * 128], LORA[0:64, 0, a:a + NS], r=["LW", "LORA"], w=[("PS", 5)])
        self.act(SIGF[:, 0:NS], b5[:, 0:NS], AF.Sigmoid, bias=w0c, scale=1.0, r=[("PS", 5), "PVEC"], w=["SIGF"])
        self.act(WDEC[:, 0:NS], SIGF[:, 0:NS], AF.Exp, scale=-float(np.exp(-0.5)), r=["SIGF"], w=["WDEC"])
        for n in range(NS):
            src = self.din["wkv_s"][n, 2 * pr:2 * pr + 2, :, :].rearrange("h i j -> i h j")
            self.dma("sp", ST[0:64, n, :, :], src, w=["ST"])
        for hh in range(2):
            self.mm(b5[0:64, 16 + hh * NS:16 + (hh + 1) * NS], CF[:, 0, 64 * hh:64 * hh + 64], XM[:, 2, 0:NS], r=["XM", "CSTF"], w=[("PS", 5)])
        self.cp(VI[0:64, :, :], b5[0:64, 16:16 + 2 * NS].rearrange("p (h n) -> p h n", h=2), r=[("PS", 5)], w=["VI"])
        srcs = (KK[:, 0:NS], WDEC[:, 0:NS], NBb[:, 0:NS], KP[:, 0:NS], XM[:, 0, 0:NS])
        regs = ("KK", "WDEC", "NBb", "KP", "XM")
        B = []
        for q, (sv, rg) in enumerate(zip(srcs, regs)):
            self.tt(DIAG[:, :, :], CF[:, 0, :].unsqueeze(1).broadcast_to([128, NS, 128]), sv.unsqueeze(2).broadcast_to([128, NS, 128]),
                    ALU.mult, r=["CSTF", rg], w=["DIAG"])
            bank = self.PS[q]
            for n in range(NS):
                self.mm(bank[0:64, n * 128:(n + 1) * 128], ONEF[:, 0:64], DIAG[:, n, :], r=["ONEF", "DIAG"], w=[("PS", q)])
            B.append(bank[0:64, 0:NS * 128].rearrange("p (n h j) -> p n h j", n=NS, h=2))
        KKB, WB, NBB, KPB, RB = B
        st = ST[0:64, :, :, :]
        t2, t3 = T2[0:64, :, :, :], T3[0:64, :, :, :]
        self.tt(t2, st, KKB, ALU.mult, r=["ST", ("PS", 0)], w=["T2"])
        self.red(SA[0:64, :, :], t2, ALU.add, r=["T2"], w=["SA"])
        self.tt(t2, st, WB, ALU.mult, r=["ST", ("PS", 1), "SA"], w=["T2"])
        self.tt(t3, NBB, SA[0:64, :, :].unsqueeze(3).broadcast_to([64, NS, 2, 64]), ALU.mult, r=[("PS", 2), "SA"], w=["T3"])
        self.tt(t2, t2, t3, ALU.add, r=["T2", "T3"], w=["T2"])
        self.tt(t3, KPB, VI[0:64, :, :].rearrange("p h n -> p n h").unsqueeze(3).broadcast_to([64, NS, 2, 64]), ALU.mult,
                r=[("PS", 3), "VI", "T2"], w=["T3"])
        self.tt(t2, t2, t3, ALU.add, r=["T2", "T3"], w=["T2"])
        for n in range(NS):
            dst = self.dout["wkv_s_o"][n, 2 * pr:2 * pr + 2, :, :].rearrange("h i j -> i h j")
            self.dma("sp", dst, T2[0:64, n, :, :], r=["T2"])
        self.tt(t3, t2, RB, ALU.mult, r=["T2", ("PS", 4)], w=["T3"])
        self.red(YI[0:64, :, :], t3, ALU.add, r=["T3"], w=["YI"])
        self.mm(b5[:, 32:32 + NS], CF[0:64, 0, :], YI[0:64, :, 0], start=True, stop=False, r=["YI", "CSTF"], w=[("PS", 5)])
        self.mm(b5[:, 32:32 + NS], SEL1[0:64, :], YI[0:64, :, 1], start=False, stop=True, r=["YI", "SEL1"], w=[("PS", 5)])
        self.cp(Y, b5[:, 32:32 + NS], r=[("PS", 5)], w=["YP"])

    def rwkv_chunks(self, pr, a, wdt):
        c = self.cfg
        V = self.V
        XM, KK, KP, NBb = V("XM"), V("KK"), V("KP"), V("NBb")
        c0 = (a - 1) // 64
        Y = V("YP")[:, a:a + wdt]
        W, WCC, KR, KH, NBH, KDE, NBDE, VB, RTF = (V(n) for n in ("W", "WCC", "KR", "KH", "NBH", "KDE", "NBDE", "VB", "RTF"))
        SIGT, TM, AM, NM, TT, MV, PZ = (V(n) for n in ("SIGT", "TM", "AM", "NM", "TT", "MV", "PZ"))
        DIAGW = V("DIAGW")
        NCq = wdt // 64
        PHIT, G, QT = (V(n)[:, c0:c0 + NCq, :] for n in ("PHITP", "GP", "QTP"))
        M4, TRIL, IDH, IDHB, ONEF, W0R, LW, LORA = (V(n) for n in ("MASK4", "TRIL", "IDH", "IDHB", "ONEF", "W0R", "LW", "LORA"))
        CF, CB = self.CSTF, self.CSTB
        NC_ = wdt // 64
        PSA = self.PSALL
        v4 = lambda ap: ap.rearrange("p (a b) -> p a b", a=NC_)
        b5 = self.PS[5].rearrange("p (a b) -> p a b", a=4)
        for cc in range(NC_):
            t0 = a + cc * 64
            self.mm(b5[0:64, cc, :], ONEF[0:1, 0:64], W0R[0:1, pr * 128:(pr + 1) * 128], start=True, stop=False,
                    r=["ONEF", "W0R"], w=[("PS", 5)])
            self.mm(b5[0:64, cc, :], LORA[0:64, 0, t0:t0 + 64], LW[0:64, 0, pr * 128:(pr + 1) * 128], start=False, stop=True,
                    r=["LORA", "LW"], w=[("PS", 5)])
        self.act(SIGT[0:64, 0:NC_, :], b5[0:64, 0:NC_, :], AF.Sigmoid, r=[("PS", 5)], w=["SIGT"])
        b4 = self.PS[4].rearrange("p (i a b) -> p i a b", i=2, a=4)
        for cc in range(NC_):
            for i in range(2):
                self.mm(b4[:, i, cc, :], SIGT[0:64, cc, :], CF[0:64, 4 + i, 0:64], r=["SIGT", "CSTF"], w=[("PS", 4)])
        self.act(W[:, 0, 0:NC_, :], b4[:, 1, 0:NC_, :], AF.Exp, r=[("PS", 4)], w=[("W", 0)])
        self.act(W[:, 1, 0:NC_, :], b4[:, 0, 0:NC_, :], AF.Exp, r=[("PS", 4)], w=[("W", 1)])
        self.act(W[:, 2, 0:NC_, :], b4[:, 0, 0:NC_, :], AF.Exp, scale=-1.0, r=[("PS", 4)], w=[("W", 2)])
        self.cp(WCC[:, 0:NC_], b4[:, 0, 0:NC_, 63], r=[("PS", 4)], w=["WCC"])
        for cc in range(NC_):
            self.act(W[:, 3, cc, :], b4[:, 0, cc, :], AF.Exp, bias=WCC[:, cc:cc + 1], scale=-1.0, r=[("PS", 4), "WCC"], w=[("W", 3)])
        n_ = slice(0, NC_)
        self.tt(KR[:, n_, 0, :], v4(KK[:, 0:wdt]), W[:, 0, n_, :], ALU.mult, r=["KK", ("W", 0)], w=["KR"])
        self.tt(RTF[:, n_, :], v4(XM[:, 0, 0:wdt]), W[:, 1, n_, :], ALU.mult, r=["XM", ("W", 1)], w=["RTF"])
        self.cp(KR[:, n_, 1, :], RTF[:, n_, :], r=["RTF"], w=["KR"], eng="act")
        self.tt(KH[:, n_, :], v4(KP[:, 0:wdt]), W[:, 2, n_, :], ALU.mult, r=["KP", ("W", 2)], w=["KH"])
        self.tt(NBH[:, n_, :], v4(NBb[:, 0:wdt]), W[:, 2, n_, :], ALU.mult, r=["NBb", ("W", 2)], w=["NBH"])
        self.tt(KDE[:, n_, :], v4(KP[:, 0:wdt]), W[:, 3, n_, :], ALU.mult, r=["KP", ("W", 3)], w=["KDE"])
        self.tt(NBDE[:, n_, :], v4(NBb[:, 0:wdt]), W[:, 3, n_, :], ALU.mult, r=["NBb", ("W", 3)], w=["NBDE"])
        self.cp(VB[:, n_, :], v4(XM[:, 2, 0:wdt]), r=["XM"], w=["VB"], eng="act")
        ptm = PSA[:, 3072:4096].bitcast(BF16).rearrange("p (a b) -> p a b", a=16)
        for kind, src in enumerate((None, KDE, NBDE, VB)):
            for cc in range(NC_):
                in_ = KR[:, cc, 0, :] if kind == 0 else src[:, cc, :]
                self.tr(ptm[0:64, kind * 4 + cc, :], in_, CB[:, 0, :], r=["KR", "KDE", "NBDE", "VB", "CSTB"], w=[("PS", 6), ("PS", 7)])
        self.cp(TM[0:64, :, 0:NC_, :], ptm[0:64, :, :].rearrange("p (k a) b -> p k a b", k=4)[:, :, 0:NC_, :],
                r=[("PS", 6), ("PS", 7)], w=["TM"], eng="act")
        for hh in range(2):
            rows = slice(64 * hh, 64 * hh + 64)
            pah = PSA[0:64, 1024 * hh:1024 * hh + 1024].rearrange("p (a b) -> p a b", a=4)
            pnh = self.PS[4 + hh][0:64, 0:256].rearrange("p (a b) -> p a b", a=4)
            for cc in range(NC_):
                krf = KR[rows, cc, :, :].rearrange("p a b -> p (a b)")
                self.mm(pah[:, cc, 0:128], KH[rows, cc, :], krf, r=["KH", "KR"], w=[("PS", 2 * hh), ("PS", 2 * hh + 1)])
                self.mm(pah[:, cc, 128:256], NBH[rows, cc, :], krf, r=["NBH", "KR"], w=[("PS", 2 * hh), ("PS", 2 * hh + 1)])
                self.mm(pnh[:, cc, :], KR[rows, cc, 0, :], NBH[rows, cc, :], r=["NBH", "KR"], w=[("PS", 4 + hh)])
            self.tt(AM[0:64, hh, n_, :], pah[:, n_, :], M4[0:64, :].unsqueeze(1).broadcast_to([64, NC_, 256]), ALU.mult,
                    r=[("PS", 2 * hh), ("PS", 2 * hh + 1), "MASK4"], w=["AM"])
            self.tt(NM[0:64, 0, 1, 4 * hh:4 * hh + NC_, :], pnh[:, n_, :], TRIL[0:64, :].unsqueeze(1).broadcast_to([64, NC_, 64]), ALU.mult,
                    r=[("PS", 4 + hh), "TRIL"], w=[("NM", 0)])
        E_ = [(hh, cc) for hh in range(2) for cc in range(NC_)]
        eidx = lambda hh, cc: 4 * hh + cc
        idb = IDHB[0:64, :]
        for hh in range(2):
            self.tt(TT[0:64, 0, 4 * hh:4 * hh + NC_, :], AM[0:64, hh, n_, 128:192], idb.unsqueeze(1).broadcast_to([64, NC_, 64]), ALU.add,
                    r=["AM", "IDHB"], w=[("TT", 0)])
        b6 = self.PS[6].rearrange("p (a b) -> p a b", a=8)
        b7 = self.PS[7].rearrange("p (a b) -> p a b", a=8)
        b5 = self.PS[5].rearrange("p (a b) -> p a b", a=8)
        for k in range(5):
            cur, nxt = k % 2, (k + 1) % 2
            for (hh, cc) in E_:
                e = eidx(hh, cc)
                ntk = AM[0:64, hh, cc, 128:192] if k == 0 else NM[0:64, cur, 0, e, :]
                nk = NM[0:64, cur, 1, e, :]
                self.mm(b6[0:64, e, :], nk, ntk, r=[("NM", cur), "AM"], w=[("PS", 6)])
                self.mm(b7[0:64, e, :], ntk, nk, r=[("NM", cur), "AM"], w=[("PS", 7)])
            self.cp(NM[0:64, nxt, 0, :, :], b6[0:64, :, :], r=[("PS", 6)], w=[("NM", nxt)], eng="act")
            self.cp(NM[0:64, nxt, 1, :, :], b7[0:64, :, :], r=[("PS", 7)], w=[("NM", nxt)], eng="act")
            for (hh, cc) in E_:
                e = eidx(hh, cc)
                self.mm(b5[0:64, e, :], idb, TT[0:64, cur, e, :], start=True, stop=False, r=[("TT", cur), "IDHB"], w=[("PS", 5)])
                self.mm(b5[0:64, e, :], NM[0:64, nxt, 1, e, :], TT[0:64, cur, e, :], start=False, stop=True,
                        r=[("TT", cur), ("NM", nxt)], w=[("PS", 5)])
            self.cp(TT[0:64, nxt, :, :], b5[0:64, :, :], r=[("PS", 5)], w=[("TT", nxt)])
        TTf = TT[0:64, 1]
        b4 = self.PS[4].rearrange("p (a b) -> p a b", a=8)
        for (hh, cc) in E_:
            e = eidx(hh, cc)
            self.mm(b4[0:64, e, :], AM[0:64, hh, cc, 0:64], TM[0:64, 3, cc, 64 * hh:64 * hh + 64], r=["AM", "TM"], w=[("PS", 4)])
        self.cp(MV[0:64, :, :], b4[0:64, :, :], r=[("PS", 4)], w=["MV"], eng="act")
        pz = PSA[0:64, 0:1024].rearrange("p (a b) -> p a b", a=8)
        for (hh, cc) in E_:
            e = eidx(hh, cc)
            self.mm(pz[:, e, 0:64], TTf[:, e, :], TM[0:64, 0, cc, 64 * hh:64 * hh + 64], r=[("TT", 1), "TM"], w=[("PS", 0), ("PS", 1)])
            self.mm(pz[:, e, 64:128], TTf[:, e, :], MV[0:64, e, :], r=[("TT", 1), "MV"], w=[("PS", 0), ("PS", 1)])
        self.cp(PZ[0:64, :, :], pz, r=[("PS", 0), ("PS", 1)], w=["PZ"])
        php = PSA[:, 1024:1280].rearrange("p (a b) -> p a b", a=4)
        gp = PSA[:, 1280:1536].rearrange("p (a b) -> p a b", a=4)
        qtp = PSA[:, 1536:1792].rearrange("p (a b) -> p a b", a=4)
        y0p = PSA[:, 1792:2048].rearrange("p (a b) -> p a b", a=4)
        for (hh, cc) in E_:
            e = eidx(hh, cc)
            o = slice(64 * hh, 64 * hh + 64)
            hs = slice(64 * hh, 64 * hh + 64)
            P1, Z = PZ[0:64, e, 0:64], PZ[0:64, e, 64:128]
            self.mm(php[o, cc, :], P1, TM[0:64, 2, cc, hs], r=["PZ", "TM"], w=[("PS", 2)])
            self.mm(gp[o, cc, :], TM[0:64, 1, cc, hs], TM[0:64, 3, cc, hs], start=True, stop=False, r=["TM"], w=[("PS", 2)])
            self.mm(gp[o, cc, :], TM[0:64, 2, cc, hs], Z, start=False, stop=True, r=["TM", "PZ"], w=[("PS", 2)])
            self.mm(qtp[o, cc, :], P1, AM[0:64, hh, cc, 192:256], r=["PZ", "AM"], w=[("PS", 3)])
            self.mm(y0p[o, cc, :], TM[0:64, 3, cc, hs], AM[0:64, hh, cc, 64:128], start=True, stop=False, r=["TM", "AM"], w=[("PS", 3)])
            self.mm(y0p[o, cc, :], Z, AM[0:64, hh, cc, 192:256], start=False, stop=True, r=["PZ", "AM"], w=[("PS", 3)])
        self.tt(DIAGW[:, n_, :], IDH[:, :].unsqueeze(1).broadcast_to([128, NC_, 64]),
                W[:, 1, n_, 63:64].broadcast_to([128, NC_, 64]), ALU.mult, r=["IDH", ("W", 1)], w=["DIAGW"])
        self.tt(PHIT[:, n_, :], php[:, n_, :], DIAGW[:, n_, :], ALU.add, r=[("PS", 2), "DIAGW"], w=["PHITP"])
        self.cp(G[:, n_, :], gp[:, n_, :], r=[("PS", 2)], w=["GP"], eng="act")
        self.tt(QT[:, n_, :], qtp[:, n_, :], RTF[:, n_, :], ALU.add, r=[("PS", 3), "RTF"], w=["QTP"])
        self.cp(v4(Y), y0p[:, n_, :], r=[("PS", 3)], w=["YP"], eng="act")

    def rwkv_chain(self, emit_y):
        c = self.cfg
        V = self.V
        PHITP, GP, QTP, H, YP = V("PHITP"), V("GP"), V("QTP"), V("H"), V("YP")
        for blk0 in range(0, c.NCH, 4):
            nb = min(4, c.NCH - blk0)
            psh = [self.PS[4 + hh].rearrange("p (a i b) -> p a i b", a=4, i=2) for hh in range(2)]
            for cc in range(nb):
                ch = blk0 + cc
                for hh in range(2):
                    o = slice(64 * hh, 64 * hh + 64)
                    if emit_y:
                        self.mm(psh[hh][o, cc, 0, :], H[o, :], QTP[o, ch, :], r=["H", "QTP"], w=[("PS", 4 + hh)])
                    self.mm(psh[hh][o, cc, 1, :], PHITP[o, ch, :], H[o, :], r=["H", "PHITP"], w=[("PS", 4 + hh)])
                for hh in range(2):
                    o = slice(64 * hh, 64 * hh + 64)
                    self.tt(H[o, :], psh[hh][o, cc, 1, :], GP[o, ch, :], ALU.add, r=[("PS", 4 + hh), "GP"], w=["H"])
            if emit_y:
                a = 1 + blk0 * 64
                yv = YP[:, a:a + nb * 64].rearrange("p (a b) -> p a b", a=nb)
                for hh in range(2):
                    o = slice(64 * hh, 64 * hh + 64)
                    self.tt(yv[o, :, :], yv[o, :, :], psh[hh][o, 0:nb, 0, :], ALU.add, r=[("PS", 4 + hh), "YP"], w=["YP"])


class DecodeAttMixin:
    def attention_decode(self):
        c = self.cfg
        V = self.V
        NS, NKV, NQH = c.NS, c.NKV, c.NQH
        Q, KD2 = V("Q"), V("KD2")
        CKD, CVD, KTD, QP, SD, ED, PD, PTD, PT1, VNS, VN0, VND, MXD, SINKP = (V(n) for n in (
            "CKD", "CVD", "KTD", "QP", "SD", "ED", "PD", "PTD", "PT1", "VNS", "VN0", "VND", "MXD", "SINKP"))
        CB = self.CSTB
        KD = c.KD
        col0 = c.NP + 1
        GREV = self.GREV
        self.dma("sp", SINKP[0:NQH, 0:1], self.din["sinks"].rearrange("(h o) -> h o", o=1), w=["SINKP"])
        b2 = self.PS[2]
        for vc in range(c.KC):
            W, wreg = self.wv[vc]
            for k in range(KD):
                self.mm(b2[0:NS, vc * 128:(vc + 1) * 128], self.XN[:, k, col0:col0 + NS], W[:, k, :], start=(k == 0), stop=(k == KD - 1),
                        r=wreg + [("XN", k)], w=[("PS", 2)])
        self.cp(VNS[0:NS, :], b2[0:NS, 0:c.KVW], r=[("PS", 2)], w=["VNS"])
        self.dma("sp", self.dout["new_v_s"][:, :], VNS[0:NS, :], r=["VNS"])
        for n in range(NS):
            self.dma("sp", VN0[0:1, n, :], VNS[n:n + 1, :], r=["VNS"], w=["VN0"])
        vn0 = VN0[0:1, :, :].rearrange("p n (g d) -> p n g d", g=NKV)
        vnd = VND[0:1, :, :, :].rearrange("p n g (t d) -> p n g t d", t=2)
        for t in range(2):
            self.cp(vnd[:, :, :, t, :], vn0, r=["VN0"], w=["VND"])
        self.cp(self.KSS[:, :, :], KD2[0:64, :, 128 + col0:128 + col0 + NS], r=[("KD2", g, "s") for g in range(NKV)], w=["KSS"])
        self.dma("sp", self.dout["new_k_s"][:, :, :], self.KSS[:, :, :], r=["KSS"])
        self.dma("sp", self.dout["kc_s"][:, :, :], self.din["cache_k"][:, 1:128, :])
        self.dma("sp", self.dout["vc_s"][:, :, :], self.din["cache_v"][:, 1:128, :])
        self.mset(QP[:], 0.0, w=["QP"])
        for n in range(NS):
            for g in range(NKV):
                for par in range(2):
                    h0 = 4 * g + par
                    self.cp(QP[64 * par:64 * par + 64, n, g, h0:h0 + 3:2], Q[64 * par:64 * par + 64, 2 * g:2 * g + 2, col0 + n],
                            r=[("Q", 2 * g, "s"), ("Q", 2 * g + 1, "s")], w=["QP"])
        self.mset(SD[:], 0.0, w=["SD"])
        for n in range(NS):
            ck = self.din["cache_k"][n].rearrange("t (g d) -> t g d", g=NKV)
            cv = self.din["cache_v"][n].rearrange("t (g d) -> t g d", g=NKV)
            ckd = CKD[:, :, :].rearrange("p g (t d) -> p g t d", t=2)
            cvd = CVD[:, :, :].rearrange("p g (t d) -> p g t d", t=2)
            for t in range(2):
                self.dma("pool", ckd[:, :, t, :], ck, w=["CKD"])
                self.dma("pool", cvd[:, :, t, :], cv, w=["CVD"])
            tb = self.PS[4][:, :].bitcast(BF16).rearrange("p (g k) -> p g k", g=NKV)
            for g in range(NKV):
                self.tr(tb[:, g, 0:128], CKD[:, g, :], CB[:, 0, :], r=["CKD", "CSTB"], w=[("PS", 4)])
            self.cp(KTD[:, :, 0:128], tb[:, :, 0:128], r=[("PS", 4)], w=["KTD"], eng="act")
            self.cp(KTD[:, :, 128:129], KD2[:, :, 128 + col0 + n:128 + col0 + n + 1], r=[("KD2", g, "s") for g in range(NKV)], w=["KTD"])
            b0 = self.PS[0]
            for g in range(NKV):
                self.mm(b0[0:NQH, 0:129], QP[:, n, g, :], KTD[:, g, 0:129], start=(g == 0), stop=(g == NKV - 1),
                        r=["QP", "KTD"], w=[("PS", 0)])
            self.stt(SD[0:NQH, 0:129], b0[0:NQH, 0:129], 0.125, GREV[:, 127:256], ALU.mult, ALU.add, r=[("PS", 0), "GREV"], w=["SD"])
            self.cp(SD[0:NQH, 129:130], SINKP[0:NQH, 0:1], r=["SINKP"], w=["SD"])
            self.red(MXD[0:NQH, 0:1], SD[0:NQH, 0:130], ALU.max, r=["SD"], w=["MXD"])
            self.ts(MXD[0:NQH, 1:2], MXD[0:NQH, 0:1], -1.0, None, ALU.mult, r=["MXD"], w=["MXD"])
            self.act(ED[0:NQH, 0:130], SD[0:NQH, 0:130], AF.Exp, bias=MXD[0:NQH, 1:2], scale=1.0, accum_out=MXD[0:NQH, 2:3],
                     r=["SD", "MXD"], w=["ED", "MXD"])
            self.recip(MXD[0:NQH, 3:4], MXD[0:NQH, 2:3], r=["MXD"], w=["MXD"])
            self.ts(PD[0:NQH, 0:130], ED[0:NQH, 0:130], MXD[0:NQH, 3:4], None, ALU.mult, r=["ED", "MXD"], w=["PD"])
            tp = self.PS[5][:, :].bitcast(BF16)
            self.tr(tp[:, 0:NQH], PD[0:NQH, 0:128], CB[0:NQH, 0, 0:NQH], r=["PD", "CSTB"], w=[("PS", 5)])
            self.tr(tp[0:1, 64:64 + NQH], PD[0:NQH, 128:129], CB[0:NQH, 0, 0:NQH], r=["PD", "CSTB"], w=[("PS", 5)])
            self.cp(PTD[:, 0:NQH], tp[:, 0:NQH], r=[("PS", 5)], w=["PTD"], eng="act")
            self.cp(PT1[0:1, 0:NQH], tp[0:1, 64:64 + NQH], r=[("PS", 5)], w=["PT1"], eng="act")
            b1 = self.PS[1]
            for g in range(NKV):
                self.mm(b1[:, 4 * g:4 * g + 4], CVD[:, g, :], PTD[:, 4 * g:4 * g + 4], start=True, stop=False,
                        r=["CVD", "PTD"], w=[("PS", 1)])
                self.mm(b1[:, 4 * g:4 * g + 4], VND[0:1, n, g, :], PT1[0:1, 4 * g:4 * g + 4], start=False, stop=True,
                        r=["VND", "PT1"], w=[("PS", 1)])
            for g in range(NKV):
                for par in range(2):
                    h0 = 4 * g + par
                    self.cp(Q[64 * par:64 * par + 64, 2 * g:2 * g + 2, col0 + n], b1[64 * par:64 * par + 64, h0:h0 + 3:2],
                            r=[("PS", 1), "QP"], w=[("Q", 2 * g, "s"), ("Q", 2 * g + 1, "s")], eng="act")

    def w_out_phase(self):
        c = self.cfg
        KD = c.KD
        X, Q, MIXR = self.X, self.V("Q"), self.V("MIXR")
        self.p.barrier()
        self.dma("sp", X[:], self.xpark[:], r=["xpark"], w=[("X", k) for k in range(KD)])
        w_out = self.wd["w_out"].rearrange("(k q) n -> q k n", q=128)
        step = 0
        for o in range(KD):
            W, wreg = self.slab(w_out[:, :, o * 128:(o + 1) * 128])
            for ci, (a, b) in enumerate(c.CH):
                bi = step % 2
                step += 1
                bank = self.PS[bi]
                for k in range(KD):
                    if k < c.QC:
                        rhs, rr = Q[:, k, a:b], self.allcols("Q", k)
                    else:
                        rhs, rr = MIXR[:, k - c.QC, a:b], [("MIXR", k - c.QC)]
                    self.mm(bank[:, 0:b - a], W[:, k, :], rhs, start=(k == 0), stop=(k == KD - 1), r=wreg + rr, w=[("PS", bi)])
                self.tt(X[:, o, a:b], bank[:, 0:b - a], X[:, o, a:b], ALU.add, r=[("PS", bi), ("X", o)], w=[("X", o)])


class FullBuilder(DecodeAttMixin, RwkvMixin, MixerBuilder):
    def declare_io(self):
        MixerBuilder.declare_io(self)
        c = self.cfg
        self.inp("w0row", [1, c.RW])
        self.inp("w2", [64, c.RW])
        self.inp("a2", [64, c.RW])
        self.inp("g2", [128, c.RW])
        self.inp("sshift", [128, c.SC, c.NS])
        self.outp("shift_o", [128, c.SC, 1 + c.NS])
        self.outp("wkv_p", [128, c.RP, 64])
        self.inp("wkv_s", [c.NS, c.RH, 64, 64])
        if c.halves > 1:
            self.inp("flag", [128, 1])
            self.hx_in = self.nc.dram_tensor("hx_in", [c.RP, 128, 64], F32, kind="Internal").ap()
            self.hx_out = self.nc.dram_tensor("hx_out", [c.RP, 128 * c.halves, 64], F32, kind="Internal").ap()
            kw = c.NKV * 128 + c.KVW
            self.kvx_in = self.nc.dram_tensor("kvx_in", [128, kw], BF16, kind="Internal").ap()
            self.kvx_out = self.nc.dram_tensor("kvx_out", [128 * c.halves, kw], BF16, kind="Internal").ap()
        self.inp("cache_k", [c.NS, 128, c.KVW])
        self.inp("cache_v", [c.NS, 128, c.KVW])
        self.outp("new_v_s", [c.NS, c.KVW])
        self.outp("new_k_s", [64, c.NKV, c.NS])
        self.outp("kc_s", [c.NS, 127, c.KVW])
        self.outp("vc_s", [c.NS, 127, c.KVW])
        self.outp("wkv_s_o", [c.NS, c.RH, 64, 64])
        if self.debug:
            self.outp("dbg_mixr", [128, c.RP, c.NT], BF16)

    def alloc_extra(self, sb):
        MixerBuilder.alloc_extra(self, sb)
        c = self.cfg
        self.SSH = sb("SSH", [128, c.SC, c.NS], F32)
        self.SHO = sb("SHO", [128, c.SC, 1 + c.NS], F32)
        self.KSS = sb("KSS", [64, c.NKV, c.NS], F32)
        self.FLAG = sb("FLAG", [128, 1], F32)

    def mixer(self):
        c = self.cfg
        self.slab_i = 0
        if "att" in self.parts:
            self.attention_inputs()
            self.proj_qkv()
            self.attention_prompt()
            self.attention_decode()
            if self.debug:
                self.dma("sp", self.dout["dbg_q"][:], self.V("Q")[:], r=[r_ for ch in range(c.QC) for r_ in self.allcols("Q", ch)])
        if "rwkv" in self.parts:
            self.rwkv()
            if self.debug:
                self.dma("sp", self.dout["dbg_mixr"][:, :, 1:c.NT], self.V("MIXR")[:, :, 1:c.NT], r=[("MIXR", pr) for pr in range(c.RP)])
        if "wout" in self.parts:
            self.w_out_phase()

    parts = ("att", "rwkv", "wout")


def _fm(vec, nchunks):
    return np.ascontiguousarray(np.asarray(vec, np.float32).reshape(nchunks, 128).T)


def prep_inputs(cfg, builder, inp):
    c = cfg
    pv = builder.pv
    pvec = np.zeros((128, pv["_n"]), np.float32)

    def setp(name, vec, n):
        pvec[:, pv[name]:pv[name] + n] = _fm(np.asarray(vec).reshape(-1), n)

    setp("n1", inp["ffn1_norm"][0], c.KD)
    setp("nm", inp["mix_norm"][0], c.KD)
    setp("n2", inp["ffn2_norm"][0], c.KD)
    setp("nf", inp["final_norm"], c.KD)
    setp("mu", inp["shift_mu"][0], c.SC)
    for nm, key in (("w0", "decay_w0"), ("a0", "aaa_a0"), ("k_k", "key_k"), ("k_a", "key_a"), ("r_k", "bonus_r_k"),
                    ("ln_w", "ln_x_w"), ("ln_b", "ln_x_b")):
        setp(nm, inp[key][0], c.RP)
    hc = host_consts(c)
    rb = np.concatenate([np.asarray(inp["rel_bias"], np.float32), np.full((1, c.NQH), -30000, np.float32)], 0)
    shared = {
        "pvec": pvec, "cst": hc["cst"], "oh_rev": hc["oh_rev"], "rb_ext": rb,
        "sinks": np.ascontiguousarray(inp["attn_sinks"][0], np.float32),
        "f1g": inp["ffn1_w_gate"][0], "f1u": inp["ffn1_w_up"][0], "f1d": inp["ffn1_w_down"][0],
        "w_in": inp["w_in"][0], "w_out": inp["w_out"][0],
        "f2g": inp["ffn2_w_gate"][0], "f2u": inp["ffn2_w_up"][0], "f2d": inp["ffn2_w_down"][0],
        "w0row": np.ascontiguousarray(inp["decay_w0"][0][None, :]), "w2": inp["decay_w2"][0], "a2": inp["aaa_a2"][0],
        "g2": inp["gate_g2"][0],
    }
    shared = {k: np.ascontiguousarray(v, dtype=np.float32) for k, v in shared.items()}
    maps = []
    for core in range(c.n_cores):
        b, half = core // c.halves, core % c.halves
        t0 = half * c.NP
        xp = inp["x_prompt"][b]
        halo = xp[t0 - 1:t0] if half > 0 else np.zeros((1, c.D), np.float32)
        x = np.concatenate([halo, xp[t0:t0 + c.NP], inp["x_sample"][core * c.NS:(core + 1) * c.NS, 0]], 0)
        xT = np.ascontiguousarray(x.T.reshape(c.KD, 128, c.NT).transpose(1, 0, 2), dtype=np.float32)
        ss = inp["state_shift"][0][core * c.NS:(core + 1) * c.NS]
        sshift = np.ascontiguousarray(ss.T.reshape(c.SC, 128, c.NS).transpose(1, 0, 2), dtype=np.float32)
        m = dict(shared)
        if c.halves > 1:
            m["flag"] = np.full((128, 1), 0.0 if half == 0 else 1.0, np.float32)
        m.update({
            "xT": xT, "sshift": sshift,
            "mask0": np.full((128, 128), -30000.0 if half == 0 else 0.0, np.float32),
            "wkv_s": np.ascontiguousarray(inp["state_wkv"][0][core * c.NS:(core + 1) * c.NS], dtype=np.float32),
            "cache_k": np.ascontiguousarray(inp["cache_k"][0][core * c.NS:(core + 1) * c.NS].reshape(c.NS, 128, c.KVW), dtype=np.float32),
            "cache_v": np.ascontiguousarray(inp["cache_v"][0][core * c.NS:(core + 1) * c.NS].reshape(c.NS, 128, c.KVW), dtype=np.float32),
        })
        maps.append(m)
    return maps


def assemble(cfg, res, batch):
    c = cfg
    S = c.NP * c.halves
    DB = c.n_cores * c.NS
    y_p = np.zeros((batch, S, c.D), np.float32)
    y_s = np.zeros((DB, 1, c.D), np.float32)
    nk_p = np.zeros((1, batch, 128, c.NKV, 64), np.float32)
    nv_p = np.zeros((1, batch, 128, c.NKV, 64), np.float32)
    wkv_p = np.zeros((1, batch, c.RH, 64, 64), np.float32)
    sh_p = np.zeros((1, batch, c.SHIFT_COLS), np.float32)
    nk_s = np.zeros((1, DB, 128, c.NKV, 64), np.float32)
    nv_s = np.zeros((1, DB, 128, c.NKV, 64), np.float32)
    wkv_s = np.zeros((1, DB, c.RH, 64, 64), np.float32)
    sh_s = np.zeros((1, DB, c.SHIFT_COLS), np.float32)
    for core in range(c.n_cores):
        r = res[core]
        b, half = core // c.halves, core % c.halves
        y = np.asarray(r["yT"]).transpose(1, 0, 2).reshape(c.D, c.NP + c.NS).T
        y_p[b, half * c.NP:(half + 1) * c.NP] = y[:c.NP]
        y_s[core * c.NS:(core + 1) * c.NS, 0] = y[c.NP:]
        sho = np.asarray(r["shift_o"])
        shf = sho.transpose(2, 1, 0).reshape(1 + c.NS, c.SHIFT_COLS)
        if half == c.halves - 1:
            nk_p[0, b] = np.asarray(r["new_k"]).transpose(2, 1, 0)
            nv_p[0, b] = np.asarray(r["new_v"]).reshape(128, c.NKV, 64)
            wk = np.asarray(r["wkv_p"])
            wkv_p[0, b] = wk.reshape(2, 64, c.RP, 64).transpose(2, 0, 3, 1).reshape(c.RH, 64, 64)
            sh_p[0, b] = shf[0]
        sl = slice(core * c.NS, (core + 1) * c.NS)
        nk_s[0, sl, :127] = np.asarray(r["kc_s"]).reshape(c.NS, 127, c.NKV, 64)
        nk_s[0, sl, 127] = np.asarray(r["new_k_s"]).transpose(2, 1, 0)
        nv_s[0, sl, :127] = np.asarray(r["vc_s"]).reshape(c.NS, 127, c.NKV, 64)
        nv_s[0, sl, 127] = np.asarray(r["new_v_s"]).reshape(c.NS, c.NKV, 64)
        wkv_s[0, sl] = np.asarray(r["wkv_s_o"])
        sh_s[0, sl] = shf[1:]
    return (y_p, y_s, nk_p, nv_p, wkv_p, sh_p, nk_s, nv_s, wkv_s, sh_s)


def kernel(**inputs):
    inp = {k: np.asarray(v) for k, v in inputs.items()}
    cfg = Cfg(D=2048, DFF=5504, NP=1024, NS=4, n_cores=8, halves=2, GS=8)
    b = FullBuilder(cfg)
    nc = b.build()
    maps = prep_inputs(cfg, b, inp)
    res = run_bass_kernel_spmd(nc, maps, core_ids=list(range(cfg.n_cores)))
    return assemble(cfg, res.results, inp["x_prompt"].shape[0])
```

```python
import numpy as np
import concourse.bass as bass
import concourse.mybir as mybir
from concourse.bass_utils import run_bass_kernel_spmd

F32 = mybir.dt.float32
BF16 = mybir.dt.bfloat16
I32 = mybir.dt.int32
AF = mybir.ActivationFunctionType
ALU = mybir.AluOpType
AX = mybir.AxisListType

ENGS = ("pe", "act", "dve", "pool", "sp")
DMAQ = ("sp", "act", "pool")
NSEM_DMA = 12
SEG = 6000


class Prog:
    def __init__(self):
        self.ops = {e: [] for e in ENGS}
        self.lastw = {}
        self.lastr = {}
        self.ndma = {q: 0 for q in DMAQ}
        self.pending = {e: [] for e in ENGS}

    def _deps(self, eng, r, w):
        deps = list(self.pending[eng])
        self.pending[eng] = []
        for reg in r:
            lw = self.lastw.get(reg)
            if lw is not None:
                deps.append(lw)
        for reg in w:
            lw = self.lastw.get(reg)
            if lw is not None and not (lw[0] == "eng" and lw[1] == eng and eng == "pe"):
                deps.append(lw)
            for k, ref in self.lastr.get(reg, {}).items():
                if not (ref[0] == "eng" and ref[1] == eng and eng == "pe"):
                    deps.append(ref)
        return deps

    def _mark(self, ref, r, w):
        for reg in r:
            key = (ref[0], ref[1]) if ref[0] == "eng" else ("dma", ref[1], ref[2] % NSEM_DMA)
            if ref[0] == "cc":
                key = ("cc", ref[2])
            self.lastr.setdefault(reg, {})[key] = ref
        for reg in w:
            self.lastw[reg] = ref
            self.lastr[reg] = {}

    @staticmethod
    def _excl(r, w):
        extra = [x for x in r if isinstance(x, tuple) and x[0] == "PS" and x not in w]
        return list(r), list(w) + extra

    def op(self, eng, fn, r=(), w=()):
        r, w = self._excl(r, w)
        deps = self._deps(eng, r, w)
        idx = len(self.ops[eng])
        self.ops[eng].append({"fn": fn, "deps": deps, "sig": False, "dma": None})
        self._mark(("eng", eng, idx), r, w)
        return ("eng", eng, idx)

    def dma(self, q, fn, r=(), w=()):
        deps = self._deps(q, r, w)
        k = self.ndma[q]
        self.ndma[q] += 1
        if k >= NSEM_DMA:
            deps.append(("dma", q, k - NSEM_DMA))
        idx = len(self.ops[q])
        self.ops[q].append({"fn": fn, "deps": deps, "sig": False, "dma": k})
        self._mark(("dma", q, k), r, w)
        return ("dma", q, k)

    def cc(self, fn, r=(), w=()):
        deps = self._deps("pool", r, w)
        k = getattr(self, "ncc", 0)
        self.ncc = k + 1
        self.ops["pool"].append({"fn": fn, "deps": deps, "sig": False, "dma": None, "cc": k})
        self._mark(("cc", "pool", k), r, w)
        return ("cc", "pool", k)

    def barrier(self):
        refs = []
        for e in ENGS:
            if self.ops[e]:
                last = len(self.ops[e]) - 1
                j = last
                while j >= 0 and (self.ops[e][j]["dma"] is not None or self.ops[e][j].get("cc") is not None):
                    j -= 1
                if j >= 0:
                    refs.append(("eng", e, j))
        for q in DMAQ:
            n = self.ndma[q]
            for k in range(max(0, n - NSEM_DMA), n):
                refs.append(("dma", q, k))
        for k in range(getattr(self, "ncc", 0)):
            refs.append(("cc", "pool", k))
        for e in ENGS:
            self.pending[e] = list(refs)

    def emit(self, nc, stack):
        for e in ENGS:
            for o in self.ops[e]:
                for d in o["deps"]:
                    if d[0] == "eng":
                        self.ops[d[1]][d[2]]["sig"] = True
        sigval = {}
        nsig = {}
        for e in ENGS:
            c = 0
            for i, o in enumerate(self.ops[e]):
                if o["sig"]:
                    c += 1
                    sigval[(e, i)] = c
            nsig[e] = c
        esem = {}
        for e in ENGS:
            nseg = (nsig[e] + SEG - 1) // SEG
            esem[e] = [stack.enter_context(nc.semaphore(f"s_{e}_{j}")) for j in range(max(1, nseg))]
        dsem = {q: [stack.enter_context(nc.semaphore(f"d_{q}_{j}")) for j in range(NSEM_DMA)]
                for q in DMAQ if self.ndma[q] > 0}
        csem = [stack.enter_context(nc.semaphore(f"cc_{j}")) for j in range(getattr(self, "ncc", 0))]
        block = stack.enter_context(nc.Block())
        prog = self

        def emit_engine(e, eng):
            waited = {}
            for i, o in enumerate(prog.ops[e]):
                for d in o["deps"]:
                    if d[0] == "eng":
                        if d[1] == e and d[2] >= i:
                            continue
                        v = sigval[(d[1], d[2])]
                        key = ("eng", d[1])
                        if waited.get(key, 0) >= v:
                            continue
                        waited[key] = v
                        seg, val = (v - 1) // SEG, (v - 1) % SEG + 1
                        eng.wait_ge(esem[d[1]][seg], val)
                    elif d[0] == "cc":
                        key = ("cc", d[2])
                        if waited.get(key, 0) >= 1:
                            continue
                        waited[key] = 1
                        eng.wait_ge(csem[d[2]], 1)
                    else:
                        _, q, k = d
                        key = ("dma", q, k % NSEM_DMA)
                        v = 16 * (k // NSEM_DMA + 1)
                        if waited.get(key, 0) >= v:
                            continue
                        waited[key] = v
                        eng.wait_ge(dsem[q][k % NSEM_DMA], v)
                ins = o["fn"](eng)
                if o.get("cc") is not None:
                    ins.then_inc(csem[o["cc"]])
                elif o["dma"] is not None:
                    ins.then_inc(dsem[e][o["dma"] % NSEM_DMA], 16)
                elif o["sig"]:
                    v = sigval[(e, i)]
                    ins.then_inc(esem[e][(v - 1) // SEG], 1)
            if e in DMAQ and prog.ndma[e] > 0:
                n = prog.ndma[e]
                for j in range(NSEM_DMA):
                    cnt = (n - j + NSEM_DMA - 1) // NSEM_DMA if n > j else 0
                    if cnt > 0 and waited.get(("dma", e, j), 0) < 16 * cnt:
                        eng.wait_ge(dsem[e][j], 16 * cnt)

        @block.tensor
        def _(t):
            emit_engine("pe", t)

        @block.scalar
        def _(s):
            emit_engine("act", s)

        @block.vector
        def _(v):
            emit_engine("dve", v)

        @block.gpsimd
        def _(g):
            emit_engine("pool", g)

        @block.sync
        def _(s):
            emit_engine("sp", s)


class Cfg:
    def __init__(self, D=2048, DFF=5504, NP=1024, NS=4, n_cores=8, halves=2, GS=8):
        self.D, self.DFF, self.NP, self.NS = D, DFF, NP, NS
        self.n_cores, self.halves, self.GS = n_cores, halves, GS
        self.KD = D // 128
        self.ATT = D // 2
        self.QC = self.ATT // 128
        self.NQH = self.ATT // 64
        self.NKV = self.NQH // 4
        self.KVW = self.NKV * 64
        self.KC = self.KVW // 128
        self.RW = D - self.ATT
        self.RH = self.RW // 64
        self.RP = self.RW // 128
        self.NF = (DFF + 127) // 128
        assert DFF % 128 == 0
        self.NT = NP + NS + 1
        self.NTP = (self.NT + 1) // 2 * 2
        self.ATT_COLS = self.ATT + 2 * self.KVW
        self.SHIFT_COLS = 3 * self.RW + 256
        self.IN_COLS = self.ATT_COLS + self.SHIFT_COLS
        self.SC = self.SHIFT_COLS // 128
        self.NB = NP // 128
        self.NCH = NP // 64
        n = (self.NT + 511) // 512
        base, rem = divmod(self.NT, n)
        self.CH = []
        s = 0
        for i in range(n):
            w = base + (1 if i < rem else 0)
            self.CH.append((s, s + w))
            s += w


def _groups(n, gs):
    out, s = [], 0
    while s < n:
        out.append((s, min(n, s + gs)))
        s += gs
    return out


class Builder:
    def __init__(self, cfg, stages=("ffn1", "mixer", "ffn2")):
        self.cfg = cfg
        self.stages = stages
        self.p = Prog()
        self.nc = bass.Bass("TRN2", target_bir_lowering=False)
        self.din = {}
        self.dout = {}
        self.psum_rr = 0

    def inp(self, name, shape, dt=F32):
        self.din[name] = self.nc.dram_tensor(name, list(shape), dt, kind="ExternalInput").ap()
        return self.din[name]

    def outp(self, name, shape, dt=F32):
        self.dout[name] = self.nc.dram_tensor(name, list(shape), dt, kind="ExternalOutput").ap()
        return self.dout[name]

    def pvec_layout(self):
        c = self.cfg
        off = {}
        o = 0
        for nm, n in (("n1", c.KD), ("nm", c.KD), ("n2", c.KD), ("nf", c.KD), ("mu", c.SC),
                      ("w0", c.RP), ("a0", c.RP), ("k_k", c.RP), ("k_a", c.RP), ("r_k", c.RP),
                      ("ln_w", c.RP), ("ln_b", c.RP)):
            off[nm] = o
            o += n
        off["_n"] = o
        return off

    def build(self):
        import contextlib
        c, nc, p = self.cfg, self.nc, self.p
        KD, NT, NF = c.KD, c.NT, c.NF
        with contextlib.ExitStack() as st:
            self.st = st
            sb = lambda name, shape, dt: st.enter_context(nc.sbuf_tensor(name, list(shape), dt))
            xT = self.inp("xT", [128, KD, NT])
            pv = self.pvec_layout()
            self.pv = pv
            pvec_d = self.inp("pvec", [128, pv["_n"]])
            cst_d = self.inp("cst", [128, 7, 128])
            wd = {}
            for nm, shp in (("f1g", [c.D, c.DFF]), ("f1u", [c.D, c.DFF]), ("f1d", [c.DFF, c.D]),
                            ("w_in", [c.D, c.IN_COLS]), ("w_out", [c.D, c.D]),
                            ("f2g", [c.D, c.DFF]), ("f2u", [c.D, c.DFF]), ("f2d", [c.DFF, c.D])):
                wd[nm] = self.inp(nm, shp)
            self.wd = wd
            yT = self.outp("yT", [128, KD, c.NP + c.NS])
            self.declare_io()

            XN = sb("XN", [128, KD, NT], BF16)
            PVEC = sb("PVEC", [128, pv["_n"]], F32)
            CSTF = sb("CSTF", [128, 7, 128], F32)
            CSTB = sb("CSTB", [128, 7, 128], BF16)
            ONESB = sb("ONESB", [128, 128], BF16)
            RSTD = sb("RSTD", [128, 512], F32)
            SQ = sb("SQ", [128, 2, 512], BF16)
            self.XN, self.PVEC, self.CSTF, self.CSTB, self.ONESB = XN, PVEC, CSTF, CSTB, ONESB
            self.RSTD, self.SQ = RSTD, SQ
            self.SG = sb("SG", [128, 2, 512], F32)
            self.alloc_extra(sb)
            SCR_BYTES = self.scratch_bytes()
            SCR = sb("SCR", [128, SCR_BYTES // 4], F32)
            self.SCR = SCR
            self.XB = (KD * NT * 4 + 63) // 64 * 64
            X = self.carve(0, [KD, NT], F32)
            self.X = X
            self.xpark = nc.dram_tensor("xpark", [128, KD, NT], F32, kind="Internal").ap()
            self.PSALL = st.enter_context(nc.psum_tensor("psall", [128, 4096], F32))
            self.PS = [self.PSALL[:, 512 * i:512 * (i + 1)] for i in range(8)]

            self.dma("sp", X[:], xT[:], w=[("X", k) for k in range(KD)])
            self.dma("sp", PVEC[:], pvec_d[:], w=["PVEC"])
            self.dma("sp", CSTF[:], cst_d[:], w=["CSTF"])
            self.cp(CSTB[:], CSTF[:], r=["CSTF"], w=["CSTB"])
            self.mset(ONESB[:], 1.0, w=["ONESB"])

            if "ffn1" in self.stages:
                self.rmsnorm("n1")
                self.ffn("f1g", "f1u", "f1d")
            if "mixer" in self.stages:
                self.rmsnorm("nm")
                self.dma("sp", self.xpark[:], X[:], r=[("X", k) for k in range(KD)], w=["xpark"])
                p.barrier()
                self.mixer()
                p.barrier()
            if "ffn2" in self.stages:
                self.rmsnorm("n2")
                self.ffn("f2g", "f2u", "f2d")
            self.rmsnorm("nf", final=True)
            self.dma("sp", yT[:], X[:, :, 1:NT], r=[("X", k) for k in range(KD)])
            self.extra_outputs()
            p.emit(nc, st)
        return nc

    def extra_outputs(self):
        pass

    def declare_io(self):
        pass

    def alloc_extra(self, sb):
        pass

    def mixer(self):
        pass

    def carve(self, off_bytes, shape, dt):
        n = int(np.prod(shape))
        esz = 4 if dt == F32 else 2
        assert off_bytes % 4 == 0
        nb = n * esz
        assert nb % 4 == 0
        assert off_bytes + nb <= self.SCR.shape[1] * 4, (off_bytes, nb, self.SCR.shape)
        v = self.SCR[:, off_bytes // 4:(off_bytes + nb) // 4]
        if dt != F32:
            v = v.bitcast(dt)
        if len(shape) == 1:
            return v
        names = "abcdefg"[:len(shape)]
        pat = "p (" + " ".join(names) + ") -> p " + " ".join(names)
        return v.rearrange(pat, **{n: int(sz) for n, sz in zip(names[:-1], shape[:-1])})

    def scratch_bytes(self):
        c = self.cfg
        ffn = (c.GS * c.NTP * 2 + 63) // 64 * 64 + c.GS * c.D * 2 + 4 * 2 * c.KD * 128 * 2
        ffn = (ffn + 63) // 64 * 64
        xb = (c.KD * c.NT * 4 + 63) // 64 * 64
        return max(xb + ffn, self.mixer_bytes())

    def mixer_bytes(self):
        return 0

    def ps(self):
        b = self.PS[self.psum_rr % 8]
        self.psum_rr += 1
        return b

    def mm(self, out, lhsT, rhs, start=True, stop=True, r=(), w=()):
        return self.p.op("pe", lambda e: e.matmul(out, lhsT=lhsT, rhs=rhs, start=start, stop=stop), r, w)

    def tr(self, out, in_, ident, r=(), w=()):
        return self.p.op("pe", lambda e: e.transpose(out, in_, ident), r, w)

    def act(self, out, in_, func, bias=None, scale=None, r=(), w=(), accum_out=None):
        kw = {}
        if bias is not None:
            kw["bias"] = bias
        if scale is not None:
            kw["scale"] = scale
        if accum_out is not None:
            kw["accum_out"] = accum_out
        return self.p.op("act", lambda e: e.activation(out=out, in_=in_, func=func, **kw), r, w)

    def ts(self, out, in0, s1, s2, op0, op1=None, r=(), w=(), eng="dve"):
        if op1 is None:
            return self.p.op(eng, lambda e: e.tensor_scalar(out=out, in0=in0, scalar1=s1, scalar2=None, op0=op0), r, w)
        return self.p.op(eng, lambda e: e.tensor_scalar(out=out, in0=in0, scalar1=s1, scalar2=s2, op0=op0, op1=op1), r, w)

    def stt(self, out, in0, scalar, in1, op0, op1, r=(), w=()):
        return self.p.op("dve", lambda e: e.scalar_tensor_tensor(out=out, in0=in0, scalar=scalar, in1=in1, op0=op0, op1=op1), r, w)

    def tt(self, out, in0, in1, op, r=(), w=(), eng="dve"):
        return self.p.op(eng, lambda e: e.tensor_tensor(out=out, in0=in0, in1=in1, op=op), r, w)

    def cp(self, out, in_, r=(), w=(), eng="dve"):
        if eng == "act":
            return self.p.op("act", lambda e: e.copy(out=out, in_=in_), r, w)
        return self.p.op(eng, lambda e: e.tensor_copy(out=out, in_=in_), r, w)

    def red(self, out, in_, op, r=(), w=()):
        return self.p.op("dve", lambda e: e.tensor_reduce(out=out, in_=in_, axis=AX.X, op=op), r, w)

    def recip(self, out, in_, r=(), w=()):
        return self.p.op("dve", lambda e: e.reciprocal(out=out, in_=in_), r, w)

    def rsqrt_pool(self, out, in_, r=(), w=()):
        self.ts(out, in_, 1e-24, None, ALU.max, r=r, w=w)
        self.act(out, out, AF.Ln, r=w, w=w)
        return self.act(out, out, AF.Exp, scale=-0.5, r=w, w=w)

    def mset(self, ap, val, r=(), w=(), eng="dve"):
        return self.p.op(eng, lambda e: e.memset(ap, val), r, w)

    def dma(self, q, out, in_, r=(), w=()):
        return self.p.dma(q, lambda e: e.dma_start(out=out, in_=in_), r, w)

    def rmsnorm(self, gname, final=False):
        c = self.cfg
        X, XN, PVEC, ONESB, RSTD, SQ = self.X, self.XN, self.PVEC, self.ONESB, self.RSTD, self.SQ
        KD = c.KD
        goff = self.pv[gname]
        for ci, (a, b) in enumerate(c.CH):
            w = b - a
            bi = 6 + (ci % 2)
            bank = self.PS[bi]
            breg = ("PS", bi)
            for k in range(KD):
                sl = k % 2
                self.act(SQ[:, sl, 0:w], X[:, k, a:b], AF.Square, r=[("X", k)], w=[("SQ", sl)])
                self.mm(bank[:, 0:w], ONESB[:], SQ[:, sl, 0:w], start=(k == 0), stop=(k == KD - 1),
                        r=[("SQ", sl), "ONESB"], w=[breg])
            self.ts(RSTD[:, 0:w], bank[:, 0:w], 1.0 / c.D, 1e-5, ALU.mult, ALU.add, r=[breg], w=["RSTD"])
            self.rsqrt_pool(RSTD[:, 0:w], RSTD[:, 0:w], r=["RSTD"], w=["RSTD"])
            for k in range(KD):
                dst = X if final else XN
                dreg = ("X", k) if final else ("XN", k)
                self.stt(dst[:, k, a:b], X[:, k, a:b], PVEC[:, goff + k:goff + k + 1], RSTD[:, 0:w],
                         ALU.mult, ALU.mult, r=[("X", k), "RSTD", "PVEC"], w=[dreg])

    def ffn(self, gname, uname, dname):
        c = self.cfg
        X, XN, SG = self.X, self.XN, self.SG
        KD, NT, GS = c.KD, c.NT, c.GS
        HB = self.carve(self.XB, [GS, c.NTP], BF16)
        o1 = self.XB + (GS * c.NTP * 2 + 63) // 64 * 64
        WD = self.carve(o1, [GS, c.D], BF16)
        o2 = o1 + GS * c.D * 2
        NSLOT = 4
        WGU = self.carve(o2, [NSLOT * 2, KD, 128], BF16)
        wg_d = self.wd[gname].rearrange("(k q) n -> q k n", q=128)
        wu_d = self.wd[uname].rearrange("(k q) n -> q k n", q=128)
        wdn_d = self.wd[dname].rearrange("(j q) n -> q j n", q=128)
        step = 0
        slot_i = 0
        for (g0, g1) in _groups(c.NF, GS):
            ng = g1 - g0
            self.dma("pool", WD[:, 0:ng, :], wdn_d[:, g0:g1, :], w=[("WD", j) for j in range(ng)])
            for j in range(g0, g1):
                sl = slot_i % NSLOT
                slot_i += 1
                self.dma("pool", WGU[:, 2 * sl, :, :], wg_d[:, :, j * 128:(j + 1) * 128], w=[("WGU", 2 * sl)])
                self.dma("pool", WGU[:, 2 * sl + 1, :, :], wu_d[:, :, j * 128:(j + 1) * 128], w=[("WGU", 2 * sl + 1)])
                for ci, (a, b) in enumerate(c.CH):
                    w = b - a
                    bi = (step % 2) * 2
                    sgs = step % 2
                    step += 1
                    bg, bu = self.PS[bi], self.PS[bi + 1]
                    for k in range(KD):
                        self.mm(bg[:, 0:w], WGU[:, 2 * sl, k, :], XN[:, k, a:b], start=(k == 0), stop=(k == KD - 1),
                                r=[("WGU", 2 * sl), ("XN", k)], w=[("PS", bi)])
                    for k in range(KD):
                        self.mm(bu[:, 0:w], WGU[:, 2 * sl + 1, k, :], XN[:, k, a:b], start=(k == 0), stop=(k == KD - 1),
                                r=[("WGU", 2 * sl + 1), ("XN", k)], w=[("PS", bi + 1)])
                    self.act(SG[:, sgs, 0:w], bg[:, 0:w], AF.Silu, r=[("PS", bi)], w=[("SG", sgs)])
                    self.tt(HB[:, j - g0, a:b], bu[:, 0:w], SG[:, sgs, 0:w], ALU.mult,
                            r=[("PS", bi + 1), ("SG", sgs)], w=[("HB", j - g0, ci)])
            dstep = 0
            for o in range(KD):
                for ci, (a, b) in enumerate(c.CH):
                    w = b - a
                    bi = 4 + (dstep % 2)
                    dstep += 1
                    bd = self.PS[bi]
                    for jj in range(ng):
                        self.mm(bd[:, 0:w], WD[:, jj, o * 128:(o + 1) * 128], HB[:, jj, a:b], start=(jj == 0), stop=(jj == ng - 1),
                                r=[("WD", jj), ("HB", jj, ci)], w=[("PS", bi)])
                    self.stt(X[:, o, a:b], bd[:, 0:w], 0.5, X[:, o, a:b], ALU.mult, ALU.add,
                             r=[("PS", bi), ("X", o)], w=[("X", o)])


class MixerBuilder(Builder):
    TB = 256
    NW = 4

    def layout(self):
        c = self.cfg
        L = {}
        xb = (c.KD * c.NT * 4 + 63) // 64 * 64
        o = 0

        def add(name, shape, dt):
            nonlocal o
            esz = 4 if dt == F32 else 2
            L[name] = (o, shape, dt)
            o += (int(np.prod(shape)) * esz + 63) // 64 * 64

        add("KD2", [c.NKV, 128 + c.NTP], BF16)
        add("VT", [c.NB + 1, c.KVW], BF16)
        add("BIAS", [c.NQH, 256], BF16)
        add("BREV", [c.NQH, 256], BF16)
        add("MASK0", [128], BF16)
        add("SINKB", [c.NQH], F32)
        add("S", [4, 260], F32)
        add("E", [4, 260], F32)
        add("PB", [4, 256], BF16)
        add("PT", [8, 128], BF16)
        add("MX", [16], F32)
        add("CKD", [c.NKV, 128], BF16)
        add("CVD", [c.NKV, 128], BF16)
        add("KTD", [c.NKV, 130], BF16)
        add("QP", [c.NS, c.NKV, c.NQH], BF16)
        add("SD", [132], F32)
        add("ED", [132], F32)
        add("PD", [132], BF16)
        add("PTD", [c.NQH + 16], BF16)
        add("PT1", [c.NQH + 16], BF16)
        add("VNS", [c.KVW], F32)
        add("VN0", [c.NS, c.KVW], F32)
        add("VND", [c.NS, c.NKV, 128], BF16)
        add("MXD", [8], F32)
        add("SINKP", [2], F32)
        oa1 = o
        o = 0
        TB = self.TB
        for nm in ("PR",):
            add(nm, [3, TB + 1], F32)
        add("D", [3, TB], F32)
        add("XM", [3, TB], F32)
        for nm in ("Aa", "KK", "T1", "KP", "NBb"):
            add(nm, [TB], F32)
        add("DDW", [512], F32)
        add("RSW", [512], F32)
        for nm in ("YP", "BVP", "GGP"):
            add(nm, [c.NT], F32)
        for nm in ("PHITP", "GP", "QTP"):
            add(nm, [c.NCH, 64], F32)
        add("HIN", [64], F32)
        for nm in ("RKb", "SQb"):
            add(nm, [TB], BF16)
        add("W", [4, 4, 64], F32)
        add("WCC", [4], F32)
        add("KR", [4, 2, 64], BF16)
        for nm in ("KH", "NBH", "KDE", "NBDE", "VB"):
            add(nm, [4, 64], BF16)
        add("RTF", [4, 64], F32)
        add("SIGT", [4, 128], F32)
        add("TM", [4, 4, 128], BF16)
        add("AM", [2, 4, 256], BF16)
        add("NM", [2, 2, 8, 64], BF16)
        add("TT", [2, 8, 64], BF16)
        add("MV", [8, 64], BF16)
        add("PZ", [8, 128], BF16)
        add("H", [64], F32)
        add("DIAGW", [4, 64], F32)
        add("MASK4", [256], F32)
        add("TRIL", [64], F32)
        add("IDH", [64], F32)
        add("IDHB", [64], BF16)
        add("ST", [c.NS, 2, 64], F32)
        add("T2", [c.NS, 2, 64], F32)
        add("T3", [c.NS, 2, 64], F32)
        add("DIAG", [c.NS, 128], F32)
        add("VI", [2, c.NS], F32)
        add("SA", [c.NS, 2], F32)
        add("YI", [c.NS, 2], F32)
        add("SIGF", [c.NS], F32)
        add("WDEC", [c.NS], F32)
        add("SEL1", [128], F32)
        o = max(xb, oa1, o)
        add("WR", [self.NW, c.KD, 128], BF16)
        add("Q", [c.QC, c.NTP], BF16)
        add("MIXR", [c.RP, c.NTP], BF16)
        add("LORA", [3, c.NTP], BF16)
        add("LW", [2, c.RW], BF16)
        add("W0R", [c.RW], F32)
        add("ONEF", [128], F32)
        self.L = L
        return o

    def mixer_bytes(self):
        return self.layout()

    def declare_io(self):
        c = self.cfg
        self.inp("oh_rev", [33, 384])
        self.inp("rb_ext", [33, c.NQH])
        self.inp("mask0", [128, 128])
        self.inp("sinks", [c.NQH])
        self.outp("new_k", [64, c.NKV, 128])
        self.outp("new_v", [128, c.KVW])
        self.gscr = self.nc.dram_tensor("gscr", [c.NQH, 384], F32, kind="Internal").ap()
        if self.debug:
            self.outp("dbg_q", [128, c.QC, c.NTP], BF16)

    def alloc_extra(self, sb):
        c = self.cfg
        self.KST = sb("KST", [64, c.NKV, 128], F32)
        self.VST = sb("VST", [128, c.KVW], F32)

    debug = False
    no_cc = False

    def dbg(self, name, ap, regs, dt=F32):
        if not self.debug:
            return
        d = self.nc.dram_tensor("dbg_" + name, list(ap.shape), dt, kind="ExternalOutput").ap()
        self.dma("sp", d, ap, r=regs)

    def V(self, name):
        off, shape, dt = self.L[name]
        return self.carve(off, shape, dt)

    def colregs(self, name, c_, a, b):
        cfg = self.cfg
        regs = set()
        for col in (a, b - 1):
            pass
        if a == 0:
            regs.add((name, c_, "h"))
        lo, hi = max(a, 1), min(b, cfg.NP + 1)
        if lo < hi:
            for blk in range((lo - 1) // 128, (hi - 2) // 128 + 1):
                regs.add((name, c_, blk))
        if b > cfg.NP + 1:
            regs.add((name, c_, "s"))
        return list(regs)

    def allcols(self, name, c_):
        return [(name, c_, "h")] + [(name, c_, b) for b in range(self.cfg.NB)] + [(name, c_, "s")]

    def slab(self, src_ap, dup64=False):
        WR = self.V("WR")
        sl = self.slab_i % self.NW
        self.slab_i += 1
        reg = ("WR", sl)
        if dup64:
            self.dma("pool", WR[:, sl, :, 0:64], src_ap, w=[reg])
            self.dma("pool", WR[:, sl, :, 64:128], src_ap, w=[])
            self.p.lastw[reg] = ("dma", "pool", self.p.ndma["pool"] - 1)
            self.p.lastw[("WRa", sl)] = ("dma", "pool", self.p.ndma["pool"] - 2)
            return WR[:, sl], [reg, ("WRa", sl)]
        self.dma("pool", WR[:, sl], src_ap, w=[reg])
        return WR[:, sl], [reg]

    def mixer(self):
        c = self.cfg
        self.slab_i = 0
        self.attention_inputs()
        self.proj_qkv()
        self.attention_prompt()
        if self.debug:
            self.dma("sp", self.dout["dbg_q"][:], self.V("Q")[:], r=[r_ for ch in range(c.QC) for r_ in self.allcols("Q", ch)])
            self.dbg("S", self.V("S"), [("S", 0), ("S", 1)])
            self.dbg("E", self.V("E"), [("E", s_) for s_ in range(4)])
            self.dbg("MX", self.V("MX"), ["MX", "NMX", "RDEN"])
            self.dbg("BIAS", self.V("BIAS"), ["BIAS"], BF16)
            self.dbg("KD2", self.V("KD2"), [], BF16)

    def attention_inputs(self):
        c = self.cfg
        BIAS, MASK0, SINKB = self.V("BIAS"), self.V("MASK0"), self.V("SINKB")
        oh = self.din["oh_rev"]
        rb = self.din["rb_ext"]
        OH = self.st.enter_context(self.nc.sbuf_tensor("OH", [33, 384], F32))
        RB = self.st.enter_context(self.nc.sbuf_tensor("RB", [33, c.NQH], F32))
        GREV = self.st.enter_context(self.nc.sbuf_tensor("GREV", [c.NQH, 384], F32))
        self.GREV = GREV
        self.dma("sp", OH[:], oh[:], w=["OH"])
        self.dma("sp", RB[:], rb[:], w=["RB"])
        bank = self.PS[7]
        self.mm(bank[0:c.NQH, 0:384], RB[:], OH[:], r=["OH", "RB"], w=[("PS", 7)])
        self.cp(GREV[:], bank[0:c.NQH, 0:384], r=[("PS", 7)], w=["GREV"])
        gsc = self.gscr
        self.dma("sp", gsc[:], GREV[:], r=["GREV"], w=["gsc"])
        BREV = self.V("BREV")
        src = bass.AP(tensor=gsc.tensor, offset=0, ap=[[1, 128], [384, c.NQH], [1, 256]])
        self.dma("pool", BREV[:], src, r=["gsc"], w=["BREV"])
        brf = BREV[:, :, :].rearrange("p h k -> p (h k)")
        bif = BIAS[:, :, :].rearrange("p h k -> p (h k)")
        for j in range(c.NQH * 256 // 512):
            bi = j % 2
            self.mm(self.PS[bi][:, :], self.CSTB[:, 6, :], brf[:, j * 512:(j + 1) * 512], r=["BREV", "CSTB"], w=[("PS", bi)])
            self.cp(bif[:, j * 512:(j + 1) * 512], self.PS[bi][:, :], r=[("PS", bi)], w=["BIAS"])
        self.dma("pool", MASK0[:], self.din["mask0"][:], w=["MASK0"])
        self.dma("sp", SINKB[:], self.din["sinks"].partition_broadcast(128), w=["SINKB"])

    def proj_qkv(self):
        c = self.cfg
        XN, Q, KD2, VT = self.XN, self.V("Q"), self.V("KD2"), self.V("VT")
        KD = c.KD
        w_in = self.wd["w_in"].rearrange("(k q) n -> q k n", q=128)
        step = 0
        self.mset(KD2[:, :, 0:128], 0.0, w=[("KD2", g, "x") for g in range(c.NKV)])
        self.mset(VT[:, 0, :], 0.0, w=[("VT", 0)])
        for qc in range(c.QC):
            W, wreg = self.slab(w_in[:, :, qc * 128:(qc + 1) * 128])
            for ci, (a, b) in enumerate(c.CH):
                bi = step % 2
                step += 1
                bank = self.PS[bi]
                for k in range(KD):
                    self.mm(bank[:, 0:b - a], W[:, k, :], XN[:, k, a:b], start=(k == 0), stop=(k == KD - 1),
                            r=wreg + [("XN", k)], w=[("PS", bi)])
                self.cp(Q[:, qc, a:b], bank[:, 0:b - a], r=[("PS", bi)], w=self.colregs("Q", qc, a, b), eng="act")
        for g in range(c.NKV):
            W, wreg = self.slab(w_in[:, :, c.ATT + g * 64:c.ATT + (g + 1) * 64], dup64=True)
            for ci, (a, b) in enumerate(c.CH):
                bi = step % 2
                step += 1
                bank = self.PS[bi]
                for k in range(KD):
                    self.mm(bank[:, 0:b - a], W[:, k, :], XN[:, k, a:b], start=(k == 0), stop=(k == KD - 1),
                            r=wreg + [("XN", k)], w=[("PS", bi)])
                self.cp(KD2[:, g, 128 + a:128 + b], bank[:, 0:b - a], r=[("PS", bi)], w=self.colregs("KD2", g, a, b), eng="act")
                if b > c.NP + 1 - 128:
                    lo = max(a, c.NP + 1 - 128)
                    hi = min(b, c.NP + 1)
                    if lo < hi:
                        KST = self.KST
                        self.cp(KST[0:64, g, lo - (c.NP + 1 - 128):hi - (c.NP + 1 - 128)], bank[0:64, lo - a:hi - a],
                                r=[("PS", bi)], w=[("KST", g)])
        for g in range(c.NKV):
            self.dma("sp", self.dout["new_k"][:, g, :], self.KST[0:64, g, :], r=[("KST", g)])
        wv = []
        for vc in range(c.KC):
            W, wreg = self.slab(w_in[:, :, c.ATT + c.KVW + vc * 128:c.ATT + c.KVW + (vc + 1) * 128])
            wv.append((W, wreg))
        self.wv = wv
        for blk in range(c.NB):
            bi = 2 + blk % 2
            bank = self.PS[bi]
            a = 1 + blk * 128
            for vc in range(c.KC):
                W, wreg = wv[vc]
                for k in range(KD):
                    self.mm(bank[:, vc * 128:(vc + 1) * 128], XN[:, k, a:a + 128], W[:, k, :], start=(k == 0), stop=(k == KD - 1),
                            r=wreg + [("XN", k)], w=[("PS", bi)])
            self.cp(VT[:, 1 + blk, :], bank[:, 0:c.KVW], r=[("PS", bi)], w=[("VT", 1 + blk)], eng="act")
            if blk == c.NB - 1:
                self.cp(self.VST[:], bank[:, 0:c.KVW], r=[("PS", bi)], w=["VST"])
                self.dma("sp", self.dout["new_v"][:], self.VST[:], r=["VST"])
        if c.halves > 1:
            nk = c.NKV * 128
            lo = 128 + c.NP + 1 - 128
            self.dma("sp", self.kvx_in[:, 0:nk].rearrange("p (g t) -> p g t", g=c.NKV), KD2[:, :, lo:lo + 128],
                     r=[("KD2", g, c.NB - 1) for g in range(c.NKV)], w=["kvx_in_k"])
            self.dma("sp", self.kvx_in[:, nk:], VT[:, c.NB, :], r=[("VT", c.NB)], w=["kvx_in_v"])
            rg = self.replica_groups()
            kin, kout = self.kvx_in, self.kvx_out
            if self.no_cc:
                self.dma("sp", kout[0:128, :], kin, r=["kvx_in_k", "kvx_in_v"], w=["kvx_out"])
            else:
                self.p.cc(lambda e: e.collective_compute("AllGather", ALU.bypass, replica_groups=rg, ins=[kin], outs=[kout]),
                          r=["kvx_in_k", "kvx_in_v"], w=["kvx_out"])
            self.dma("sp", KD2[:, :, 1:129], self.kvx_out[0:128, 0:nk].rearrange("p (g t) -> p g t", g=c.NKV),
                     r=["kvx_out"], w=[("KD2", g, "x") for g in range(c.NKV)] + [("KD2", g, "h") for g in range(c.NKV)])
            self.dma("sp", VT[:, 0, :], self.kvx_out[0:128, nk:], r=["kvx_out"], w=[("VT", 0)])

    def attention_prompt(self):
        c = self.cfg
        Q, KD2, VT, BIAS, MASK0, SINKB = (self.V(n) for n in ("Q", "KD2", "VT", "BIAS", "MASK0", "SINKB"))
        S, E, PB, PT, MX = (self.V(n) for n in ("S", "E", "PB", "PT", "MX"))
        IDB = self.CSTB[:, 0, :]
        it = 0
        if self.debug:
            self.mset(S[:], 0.0, w=[("S", 0), ("S", 1)])
            self.mset(E[:], 0.0, w=[("E", s_) for s_ in range(4)])
            self.mset(MX[:], 0.0, w=["MX", "NMX", "RDEN"] + [("DEN", s_) for s_ in range(4)])
        order = list(range(1, c.NB)) + [0] if c.halves > 1 else list(range(c.NB))
        for blk in order:
            q0 = 1 + blk * 128
            k0 = 1 + blk * 128
            for g in range(c.NKV):
                pa, pb_ = 2 * (it % 2), 2 * (it % 2) + 1
                it += 1
                bA, bB = self.PS[pa], self.PS[pb_]
                for par, bank, bi in ((0, bA, pa), (1, bB, pb_)):
                    for i in range(2):
                        ch = 2 * g + i
                        self.mm(bank[:, i * 256:(i + 1) * 256], Q[64 * par:64 * par + 64, ch, q0:q0 + 128],
                                KD2[64 * par:64 * par + 64, g, k0:k0 + 256],
                                r=[("Q", ch, blk), ("KD2", g, blk), ("KD2", g, blk - 1 if blk > 0 else "x")], w=[("PS", bi)])
                    h0 = 4 * g + par
                    self.stt(S[:, 2 * par:2 * par + 2, 0:256], bank[:, :].rearrange("p (a b) -> p a b", a=2), 0.125,
                             BIAS[:, h0:h0 + 3:2, :], ALU.mult, ALU.add, r=[("PS", bi), "BIAS"], w=[("S", par)])
                    self.cp(S[:, 2 * par:2 * par + 2, 256], SINKB[:, h0:h0 + 3:2], r=["SINKB"], w=[("S", par)])
                if blk == 0:
                    self.tt(S[:, :, 0:128], S[:, :, 0:128], MASK0[:].unsqueeze(1).broadcast_to([128, 4, 128]), ALU.add,
                            r=[("S", 0), ("S", 1), "MASK0"], w=[("S", 0), ("S", 1)])
                self.red(MX[:, 0:4], S[:, :, 0:257], ALU.max, r=[("S", 0), ("S", 1)], w=["MX"])
                self.ts(MX[:, 4:8], MX[:, 0:4], -1.0, None, ALU.mult, r=["MX"], w=["NMX"])
                for s in range(4):
                    self.act(E[:, s, 0:257], S[:, s, 0:257], AF.Exp, bias=MX[:, 4 + s:5 + s], scale=1.0,
                             accum_out=MX[:, 8 + s:9 + s], r=[("S", 0), ("S", 1), "NMX"], w=[("E", s), ("DEN", s)])
                self.recip(MX[:, 12:16], MX[:, 8:12], r=[("DEN", s) for s in range(4)], w=["RDEN"])
                self.tt(PB[:, :, :], E[:, :, 0:256], MX[:, 12:16].unsqueeze(2).broadcast_to([128, 4, 256]), ALU.mult,
                        r=[("E", s) for s in range(4)] + ["RDEN"], w=["PB"])
                tb = 4 + (it % 2)
                tbank = self.PS[tb][:, :].bitcast(BF16).rearrange("p (a b) -> p a b", a=8)
                for s in range(4):
                    for kb in range(2):
                        self.tr(tbank[:, 2 * s + kb, :], PB[:, s, kb * 128:(kb + 1) * 128], IDB, r=["PB", "CSTB"], w=[("PS", tb)])
                self.cp(PT[:, :, :], tbank, r=[("PS", tb)], w=["PT"], eng="act")
                ob = 6 + (it % 2)
                obank = self.PS[ob]
                for s in range(4):
                    par, i = s // 2, s % 2
                    for kb in range(2):
                        self.mm(obank[64 * par:64 * par + 64, i * 128:(i + 1) * 128], VT[:, blk + kb, g * 64:(g + 1) * 64],
                                PT[:, 2 * s + kb, :], start=(kb == 0), stop=(kb == 1),
                                r=["PT", ("VT", blk + kb)], w=[("PS", ob)])
                self.cp(Q[:, 2 * g:2 * g + 2, q0:q0 + 128], obank[:, 0:256].rearrange("p (a b) -> p a b", a=2),
                        r=[("PS", ob)], w=[("Q", 2 * g, blk), ("Q", 2 * g + 1, blk)], eng="act")


def t5_bucket_np(dist, n_buckets=32, max_distance=128):
    n = np.maximum(dist, 0)
    max_exact = n_buckets // 2
    nf = np.maximum(n, 1).astype(np.float32)
    large = max_exact + (np.log(nf / max_exact) / np.float32(np.log(max_distance / max_exact))
                         * (n_buckets - max_exact)).astype(np.int32)
    large = np.minimum(large, n_buckets - 1)
    return np.where(n < max_exact, n, large)


def host_consts(cfg):
    out = {}
    cst = np.zeros((128, 7, 128), np.float32)
    cst[:, 6, :] = np.eye(128, dtype=np.float32)[::-1]
    cst[:, 0, :] = np.eye(128, dtype=np.float32)
    blk = np.zeros((128, 128), np.float32)
    blk[0:64, 0:64] = 1
    blk[64:, 64:] = 1
    cst[:, 1, :] = blk
    s_ = np.arange(128)[:, None]
    t_ = np.arange(128)[None, :]
    cst[:, 2, :] = (s_ < t_)
    cst[:, 3, :] = (s_ <= t_)
    cst[:, 4, :] = (s_ <= t_) * np.float32(-np.exp(-0.5))
    cst[:, 5, :] = (s_ < t_) * np.float32(-np.exp(-0.5))
    out["cst"] = cst
    oh = np.zeros((33, 384), np.float32)
    for m in range(384):
        d = 255 - m
        if 0 <= d <= 128 and m < 383:
            oh[int(t5_bucket_np(np.array(d))), m] = 1
        else:
            oh[32, m] = 1
    out["oh_rev"] = oh
    return out


class RwkvMixin:
    GN_EPS = 64e-5

    def rwkv_consts(self):
        c = self.cfg
        CF = self.CSTF
        M4, TRIL, IDH, IDHB, ONEF, W0R, LW = (self.V(n) for n in ("MASK4", "TRIL", "IDH", "IDHB", "ONEF", "W0R", "LW"))
        for q in range(4):
            self.cp(M4[0:64, q * 64:(q + 1) * 64], CF[0:64, 2 + (q % 2), 0:64], r=["CSTF"], w=["MASK4"])
        self.ts(TRIL[0:64, :], CF[0:64, 3, 0:64], -1.0, 1.0, ALU.mult, ALU.add, r=["CSTF"], w=["TRIL"])
        self.cp(IDH[0:64, :], CF[0:64, 0, 0:64], r=["CSTF"], w=["IDH"])
        self.cp(IDH[64:128, :], CF[64:128, 0, 64:128], r=["CSTF"], w=["IDH"])
        self.cp(IDHB[:], IDH[:], r=["IDH"], w=["IDHB"])
        self.mset(ONEF[:], 1.0, w=["ONEF"])
        SEL1 = self.V("SEL1")
        self.mset(SEL1[0:64, :], 0.0, w=["SEL1"])
        self.cp(SEL1[0:64, 64:128], CF[0:64, 0, 0:64], r=["CSTF"], w=["SEL1"])
        self.dma("sp", W0R[0:1, :], self.din["w0row"][:], w=["W0R"])
        self.dma("pool", LW[0:64, 0, :], self.din["w2"][:], w=["LW"])
        self.dma("pool", LW[64:128, 0, :], self.din["a2"][:], w=["LW"])
        self.dma("pool", LW[:, 1, :], self.din["g2"][:], w=["LW"])
        self.dma("sp", self.SSH[:], self.din["sshift"][:], w=["SSH"])
        if c.halves > 1:
            self.dma("sp", self.FLAG[:], self.din["flag"][:], w=["FLAG"])

    def blocks(self):
        c = self.cfg
        TB = min(self.TB, c.NP)
        out = [(1 + TB * i, TB, True) for i in range(c.NP // TB)]
        out.append((c.NP + 1, c.NS, False))
        return out

    def proj_cols(self, W, wreg, bank_i, a, wdt):
        KD = self.cfg.KD
        bank = self.PS[bank_i]
        for k in range(KD):
            self.mm(bank[:, 0:wdt + 1], W[:, k, :], self.XN[:, k, a - 1:a + wdt], start=(k == 0), stop=(k == KD - 1),
                    r=wreg + [("XN", k)], w=[("PS", bank_i)])
        return bank

    def rwkv_lora(self):
        c = self.cfg
        PV, LORA, PR, D = self.PVEC, self.V("LORA"), self.V("PR"), self.V("D")
        w_in = self.wd["w_in"].rearrange("(k q) n -> q k n", q=128)
        mu0 = self.pv["mu"]
        for li in range(2):
            chunk = 3 * c.RP + li
            col0 = c.ATT_COLS + chunk * 128
            W, wreg = self.slab(w_in[:, :, col0:col0 + 128])
            for (a, wdt, prompt) in self.blocks():
                bank = self.proj_cols(W, wreg, li, a, wdt)
                self.cp(PR[:, 0, 0:wdt + 1], bank[:, 0:wdt + 1], r=[("PS", li)], w=["PR"], eng="act")
                self.shift_out(PR, 0, chunk, a, wdt, prompt)
                if prompt:
                    self.tt(D[:, 0, 0:wdt], PR[:, 0, 0:wdt], PR[:, 0, 1:wdt + 1], ALU.subtract, r=["PR"], w=["D"])
                else:
                    self.tt(D[:, 0, 0:wdt], self.SSH[:, chunk, :], PR[:, 0, 1:wdt + 1], ALU.subtract, r=["PR", "SSH"], w=["D"])
                self.stt(D[:, 0, 0:wdt], D[:, 0, 0:wdt], PV[:, mu0 + chunk:mu0 + chunk + 1], PR[:, 0, 1:wdt + 1],
                         ALU.mult, ALU.add, r=["D", "PR", "PVEC"], w=["D"])
                if li == 0:
                    self.act(LORA[0:64, 0, a:a + wdt], D[0:64, 0, 0:wdt], AF.Tanh, r=["D"], w=["LORA"])
                    self.cp(LORA[64:128, 0, a:a + wdt], D[64:128, 0, 0:wdt], r=["D"], w=["LORA"])
                else:
                    self.act(LORA[:, 1, a:a + wdt], D[:, 0, 0:wdt], AF.Sigmoid, r=["D"], w=["LORA"])

    def shift_out(self, PR, x, chunk, a, wdt, prompt):
        c = self.cfg
        SHO = self.SHO
        if prompt and a + wdt == c.NP + 1:
            self.cp(SHO[:, chunk, 0:1], PR[:, x, wdt:wdt + 1], r=["PR"], w=["SHO"])
        if not prompt:
            self.cp(SHO[:, chunk, 1:1 + c.NS], PR[:, x, 1:1 + wdt], r=["PR"], w=["SHO"])

    def rwkv(self):
        c = self.cfg
        self.p.barrier()
        self.mset(self.V("MIXR")[:, :, 0:1], 0.0, w=[("MIXR", pr) for pr in range(c.RP)])
        self.rwkv_consts()
        self.rwkv_lora()
        for pr in range(c.RP):
            self.rwkv_pair(pr)
        self.dma("sp", self.dout["shift_o"][:], self.SHO[:], r=["SHO"])

    def rwkv_pair(self, pr):
        c = self.cfg
        w_in = self.wd["w_in"].rearrange("(k q) n -> q k n", q=128)
        slabs = []
        for x in range(3):
            col0 = c.ATT_COLS + x * c.RW + pr * 128
            slabs.append(self.slab(w_in[:, :, col0:col0 + 128]))
        H, HIN = self.V("H"), self.V("HIN")
        blks = self.blocks()
        for (a, wdt, prompt) in blks[:-1]:
            self.rwkv_unit(pr, slabs, a, wdt, prompt)
        if c.halves == 1:
            self.rwkv_unit(pr, slabs, *blks[-1])
        self.mset(H[:], 0.0, w=["H"])
        if c.halves > 1:
            self.rwkv_chain(False)
            self.dma("sp", self.hx_in[pr], H[:], r=["H"], w=[("hx_in", pr)])
            rg = self.replica_groups()
            hin, hout = self.hx_in[pr], self.hx_out[pr]
            if self.no_cc:
                self.dma("sp", hout[0:128, :], hin, r=[("hx_in", pr)], w=[("hx_out", pr)])
            else:
                self.p.cc(lambda e: e.collective_compute("AllGather", ALU.bypass, replica_groups=rg, ins=[hin], outs=[hout]),
                          r=[("hx_in", pr)], w=[("hx_out", pr)])
            self.rwkv_unit(pr, slabs, *blks[-1])
            self.dma("sp", HIN[:], self.hx_out[pr][0:128, :], r=[("hx_out", pr)], w=["HIN"])
            self.ts(H[:], HIN[:], self.FLAG[:, 0:1], None, ALU.mult, r=["HIN", "FLAG"], w=["H"])
        self.rwkv_chain(True)
        self.dma("sp", self.dout["wkv_p"][:, pr, :], H[:], r=["H"])
        self.rwkv_post(pr)

    def replica_groups(self):
        c = self.cfg
        return [[i * c.halves + j for j in range(c.halves)] for i in range(c.n_cores // c.halves)]

    def rwkv_unit(self, pr, slabs, a, wdt, prompt):
        c = self.cfg
        PV = self.PVEC
        pvc = lambda nm: PV[:, self.pv[nm] + pr:self.pv[nm] + pr + 1]
        V = self.V
        PR, D, XM, Aa, KK, T1, KP, NBb = (V(n) for n in ("PR", "D", "XM", "Aa", "KK", "T1", "KP", "NBb"))
        BV, Gg = V("BVP")[:, a:a + wdt], V("GGP")[:, a:a + wdt]
        RKb, SQb, LORA, LW = V("RKb"), V("SQb"), V("LORA"), V("LW")
        CF, CB = self.CSTF, self.CSTB
        mu0 = self.pv["mu"]
        w_ = slice(0, wdt)
        b3 = self.PS[3]
        self.mm(b3[:, w_], LW[64:128, 0, pr * 128:(pr + 1) * 128], LORA[64:128, 0, a:a + wdt], r=["LW", "LORA"], w=[("PS", 3)])
        self.act(Aa[:, w_], b3[:, w_], AF.Sigmoid, bias=pvc("a0"), scale=1.0, r=[("PS", 3), "PVEC"], w=["Aa"])
        self.mm(b3[:, 256:256 + wdt], LW[:, 1, pr * 128:(pr + 1) * 128], LORA[:, 1, a:a + wdt], r=["LW", "LORA"], w=[("PS", 3)])
        self.cp(Gg, b3[:, 256:256 + wdt], r=[("PS", 3)], w=["GGP"], eng="act")
        if prompt:
            self.rwkv_decay(pr, a, wdt)
        for x in range(3):
            W, wreg = slabs[x]
            bank = self.proj_cols(W, wreg, x, a, wdt)
            self.cp(PR[:, x, 0:wdt + 1], bank[:, 0:wdt + 1], r=[("PS", x)], w=["PR"], eng="act")
            self.shift_out(PR, x, x * c.RP + pr, a, wdt, prompt)
        if prompt:
            self.tt(D[:, :, w_], PR[:, :, 0:wdt], PR[:, :, 1:wdt + 1], ALU.subtract, r=["PR"], w=["D"])
        else:
            self.tt(D[:, :, w_], self.SSH[:, pr:pr + 2 * c.RP + 1:c.RP, :], PR[:, :, 1:wdt + 1], ALU.subtract, r=["PR", "SSH"], w=["D"])
        for x in range(3):
            ch = x * c.RP + pr
            self.stt(XM[:, x, w_], D[:, x, w_], PV[:, mu0 + ch:mu0 + ch + 1], PR[:, x, 1:wdt + 1], ALU.mult, ALU.add,
                     r=["D", "PR", "PVEC"], w=["XM"])
        XR, XK, XV = XM[:, 0, w_], XM[:, 1, w_], XM[:, 2, w_]
        self.ts(KK[:, w_], XK, pvc("k_k"), None, ALU.mult, r=["XM", "PVEC"], w=["KK"])
        self.tt(SQb[:, w_], KK[:, w_], KK[:, w_], ALU.mult, r=["KK"], w=["SQb"])
        self.mm(b3[:, 256:256 + wdt], CB[:, 1, :], SQb[:, w_], r=["SQb", "CSTB"], w=[("PS", 3)])
        self.cp(T1[:, w_], b3[:, 256:256 + wdt], r=[("PS", 3)], w=["T1"], eng="act")
        self.rsqrt_pool(T1[:, w_], T1[:, w_], r=["T1"], w=["T1"])
        self.tt(KK[:, w_], KK[:, w_], T1[:, w_], ALU.mult, r=["KK", "T1"], w=["KK"])
        self.ts(T1[:, w_], Aa[:, w_], -1.0, pvc("k_a"), ALU.add, ALU.mult, r=["Aa", "PVEC", "KK"], w=["T1"])
        self.stt(KP[:, w_], T1[:, w_], 1.0, XK, ALU.add, ALU.mult, r=["T1", "XM"], w=["KP"])
        self.stt(NBb[:, w_], KK[:, w_], -1.0, Aa[:, w_], ALU.mult, ALU.mult, r=["KK", "Aa"], w=["NBb"])
        self.stt(RKb[:, w_], XR, pvc("r_k"), KP[:, w_], ALU.mult, ALU.mult, r=["XM", "KP", "PVEC"], w=["RKb"])
        self.mm(b3[:, w_], CB[:, 1, :], RKb[:, w_], r=["RKb", "CSTB"], w=[("PS", 3)])
        self.tt(BV, b3[:, w_], XV, ALU.mult, r=[("PS", 3), "XM"], w=["BVP"])
        if prompt:
            self.rwkv_chunks(pr, a, wdt)
        else:
            self.rwkv_decode_step(pr, a, wdt)

    def rwkv_post(self, pr):
        c = self.cfg
        V = self.V
        PV = self.PVEC
        pvc = lambda nm: PV[:, self.pv[nm] + pr:self.pv[nm] + pr + 1]
        YP, BVP, GGP, MIXR = V("YP"), V("BVP"), V("GGP"), V("MIXR")
        CF = self.CSTF
        pieces = []
        col = 1
        while col < c.NT:
            wdt = min(512, c.NT - col)
            pieces.append((col, wdt))
            col += wdt
        DDW, RSW = V("DDW"), V("RSW")
        for (a, wdt) in pieces:
            w_ = slice(0, wdt)
            Y, BV, Gg = YP[:, a:a + wdt], BVP[:, a:a + wdt], GGP[:, a:a + wdt]
            b6, b7 = self.PS[6], self.PS[7]
            self.mm(b6[:, w_], CF[:, 1, :], Y, r=["YP", "CSTF"], w=[("PS", 6)])
            self.stt(DDW[:, w_], b6[:, w_], -1.0 / 64, Y, ALU.mult, ALU.add, r=[("PS", 6), "YP"], w=["DDW"])
            self.tt(RSW[:, w_], DDW[:, w_], DDW[:, w_], ALU.mult, r=["DDW"], w=["RSW"])
            self.mm(b7[:, w_], CF[:, 1, :], RSW[:, w_], r=["RSW", "CSTF"], w=[("PS", 7)])
            self.ts(RSW[:, w_], b7[:, w_], 1.0 / 64, self.GN_EPS, ALU.mult, ALU.add, r=[("PS", 7)], w=["RSW"])
            self.rsqrt_pool(RSW[:, w_], RSW[:, w_], r=["RSW"], w=["RSW"])
            self.tt(DDW[:, w_], DDW[:, w_], RSW[:, w_], ALU.mult, r=["DDW", "RSW"], w=["DDW"])
            self.stt(DDW[:, w_], DDW[:, w_], pvc("ln_w"), BV, ALU.mult, ALU.add, r=["DDW", "BVP", "PVEC"], w=["DDW"])
            self.stt(MIXR[:, pr, a:a + wdt], DDW[:, w_], pvc("ln_b"), Gg, ALU.add, ALU.mult, r=["DDW", "GGP", "PVEC"], w=[("MIXR", pr)])

    def rwkv_decode_step(self, pr, a, wdt):
        c = self.cfg
        V = self.V
        NS = wdt
        XM, KK, KP, NBb = V("XM"), V("KK"), V("KP"), V("NBb")
        Y = V("YP")[:, a:a + wdt]
        ST, T2, T3, DIAG, VI, SA, YI, SIGF, WDEC, SEL1 = (V(n) for n in ("ST", "T2", "T3", "DIAG", "VI", "SA", "YI", "SIGF", "WDEC", "SEL1"))
        LW, LORA, ONEF = V("LW"), V("LORA"), V("ONEF")
        CF = self.CSTF
        PV = self.PVEC
        w0c = PV[:, self.pv["w0"] + pr:self.pv["w0"] + pr + 1]
        b5 = self.PS[5]
        self.mm(b5[:, 0:NS], LW[0:64, 0, pr * 128:(pr + 1) * 128], LORA[0:64, 0, a:a + NS], r=["LW", "LORA"], w=[("PS", 5)])
        self.act(SIGF[:, 0:NS], b5[:, 0:NS], AF.Sigmoid, bias=w0c, scale=1.0, r=[("PS", 5), "PVEC"], w=["SIGF"])
        self.act(WDEC[:, 0:NS], SIGF[:, 0:NS], AF.Exp, scale=-float(np.exp(-0.5)), r=["SIGF"], w=["WDEC"])
        for n in range(NS):
            src = self.din["wkv_s"][n, 2 * pr:2 * pr + 2, :, :].rearrange("h i j -> i h j")
            self.dma("sp", ST[0:64, n, :, :], src, w=["ST"])
        for hh in range(2):
            self.mm(b5[0:64, 16 + hh * NS:16 + (hh + 1) * NS], CF[:, 0, 64 * hh:64 * hh + 64], XM[:, 2, 0:NS], r=["XM", "CSTF"], w=[("PS", 5)])
        self.cp(VI[0:64, :, :], b5[0:64, 16:16 + 2 * NS].rearrange("p (h n) -> p h n", h=2), r=[("PS", 5)], w=["VI"])
        srcs = (KK[:, 0:NS], WDEC[:, 0:NS], NBb[:, 0:NS], KP[:, 0:NS], XM[:, 0, 0:NS])
        regs = ("KK", "WDEC", "NBb", "KP", "XM")
        B = []
        for q, (sv, rg) in enumerate(zip(srcs, regs)):
            self.tt(DIAG[:, :, :], CF[:, 0, :].unsqueeze(1).broadcast_to([128, NS, 128]), sv.unsqueeze(2).broadcast_to([128, NS, 128]),
                    ALU.mult, r=["CSTF", rg], w=["DIAG"])
            bank = self.PS[q]
            for n in range(NS):
                self.mm(bank[0:64, n * 128:(n + 1) * 128], ONEF[:, 0:64], DIAG[:, n, :], r=["ONEF", "DIAG"], w=[("PS", q)])
            B.append(bank[0:64, 0:NS * 128].rearrange("p (n h j) -> p n h j", n=NS, h=2))
        KKB, WB, NBB, KPB, RB = B
        st = ST[0:64, :, :, :]
        t2, t3 = T2[0:64, :, :, :], T3[0:64, :, :, :]
        self.tt(t2, st, KKB, ALU.mult, r=["ST", ("PS", 0)], w=["T2"])
        self.red(SA[0:64, :, :], t2, ALU.add, r=["T2"], w=["SA"])
        self.tt(t2, st, WB, ALU.mult, r=["ST", ("PS", 1), "SA"], w=["T2"])
        self.tt(t3, NBB, SA[0:64, :, :].unsqueeze(3).broadcast_to([64, NS, 2, 64]), ALU.mult, r=[("PS", 2), "SA"], w=["T3"])
        self.tt(t2, t2, t3, ALU.add, r=["T2", "T3"], w=["T2"])
        self.tt(t3, KPB, VI[0:64, :, :].rearrange("p h n -> p n h").unsqueeze(3).broadcast_to([64, NS, 2, 64]), ALU.mult,
                r=[("PS", 3), "VI", "T2"], w=["T3"])
        self.tt(t2, t2, t3, ALU.add, r=["T2", "T3"], w=["T2"])
        for n in range(NS):
            dst = self.dout["wkv_s_o"][n, 2 * pr:2 * pr + 2, :, :].rearrange("h i j -> i h j")
            self.dma("sp", dst, T2[0:64, n, :, :], r=["T2"])
        self.tt(t3, t2, RB, ALU.mult, r=["T2", ("PS", 4)], w=["T3"])
        self.red(YI[0:64, :, :], t3, ALU.add, r=["T3"], w=["YI"])
        self.mm(b5[:, 32:32 + NS], CF[0:64, 0, :], YI[0:64, :, 0], start=True, stop=False, r=["YI", "CSTF"], w=[("PS", 5)])
        self.mm(b5[:, 32:32 + NS], SEL1[0:64, :], YI[0:64, :, 1], start=False, stop=True, r=["YI", "SEL1"], w=[("PS", 5)])
        self.cp(Y, b5[:, 32:32 + NS], r=[("PS", 5)], w=["YP"])

    def rwkv_decay(self, pr, a, wdt):
        c = self.cfg
        V = self.V
        W, WCC, SIGT = V("W"), V("WCC"), V("SIGT")
        ONEF, W0R, LW, LORA = V("ONEF"), V("W0R"), V("LW"), V("LORA")
        CF = self.CSTF
        NC_ = wdt // 64
        b5 = self.PS[5].rearrange("p (a b) -> p a b", a=4)
        for cc in range(NC_):
            t0 = a + cc * 64
            self.mm(b5[0:64, cc, :], ONEF[0:1, 0:64], W0R[0:1, pr * 128:(pr + 1) * 128], start=True, stop=False,
                    r=["ONEF", "W0R"], w=[("PS", 5)])
            self.mm(b5[0:64, cc, :], LORA[0:64, 0, t0:t0 + 64], LW[0:64, 0, pr * 128:(pr + 1) * 128], start=False, stop=True,
                    r=["LORA", "LW"], w=[("PS", 5)])
        self.act(SIGT[0:64, 0:NC_, :], b5[0:64, 0:NC_, :], AF.Sigmoid, r=[("PS", 5)], w=["SIGT"])
        b4 = self.PS[4].rearrange("p (i a b) -> p i a b", i=2, a=4)
        for cc in range(NC_):
            for i in range(2):
                self.mm(b4[:, i, cc, :], SIGT[0:64, cc, :], CF[0:64, 4 + i, 0:64], r=["SIGT", "CSTF"], w=[("PS", 4)])
        self.act(W[:, 0, 0:NC_, :], b4[:, 1, 0:NC_, :], AF.Exp, r=[("PS", 4)], w=[("W", 0)])
        self.act(W[:, 1, 0:NC_, :], b4[:, 0, 0:NC_, :], AF.Exp, r=[("PS", 4)], w=[("W", 1)])
        self.act(W[:, 2, 0:NC_, :], b4[:, 0, 0:NC_, :], AF.Exp, scale=-1.0, r=[("PS", 4)], w=[("W", 2)])
        self.cp(WCC[:, 0:NC_], b4[:, 0, 0:NC_, 63], r=[("PS", 4)], w=["WCC"])
        for cc in range(NC_):
            self.act(W[:, 3, cc, :], b4[:, 0, cc, :], AF.Exp, bias=WCC[:, cc:cc + 1], scale=-1.0, r=[("PS", 4), "WCC"], w=[("W", 3)])

    def rwkv_chunks(self, pr, a, wdt):
        c = self.cfg
        V = self.V
        XM, KK, KP, NBb = V("XM"), V("KK"), V("KP"), V("NBb")
        c0 = (a - 1) // 64
        Y = V("YP")[:, a:a + wdt]
        W, WCC, KR, KH, NBH, KDE, NBDE, VB, RTF = (V(n) for n in ("W", "WCC", "KR", "KH", "NBH", "KDE", "NBDE", "VB", "RTF"))
        SIGT, TM, AM, NM, TT, MV, PZ = (V(n) for n in ("SIGT", "TM", "AM", "NM", "TT", "MV", "PZ"))
        DIAGW = V("DIAGW")
        NCq = wdt // 64
        PHIT, G, QT = (V(n)[:, c0:c0 + NCq, :] for n in ("PHITP", "GP", "QTP"))
        M4, TRIL, IDH, IDHB, ONEF, W0R, LW, LORA = (V(n) for n in ("MASK4", "TRIL", "IDH", "IDHB", "ONEF", "W0R", "LW", "LORA"))
        CF, CB = self.CSTF, self.CSTB
        NC_ = wdt // 64
        PSA = self.PSALL
        v4 = lambda ap: ap.rearrange("p (a b) -> p a b", a=NC_)
        n_ = slice(0, NC_)
        self.tt(KR[:, n_, 0, :], v4(KK[:, 0:wdt]), W[:, 0, n_, :], ALU.mult, r=["KK", ("W", 0)], w=["KR"])
        self.tt(RTF[:, n_, :], v4(XM[:, 0, 0:wdt]), W[:, 1, n_, :], ALU.mult, r=["XM", ("W", 1)], w=["RTF"])
        self.cp(KR[:, n_, 1, :], RTF[:, n_, :], r=["RTF"], w=["KR"], eng="act")
        self.tt(KH[:, n_, :], v4(KP[:, 0:wdt]), W[:, 2, n_, :], ALU.mult, r=["KP", ("W", 2)], w=["KH"])
        self.tt(NBH[:, n_, :], v4(NBb[:, 0:wdt]), W[:, 2, n_, :], ALU.mult, r=["NBb", ("W", 2)], w=["NBH"])
        self.tt(KDE[:, n_, :], v4(KP[:, 0:wdt]), W[:, 3, n_, :], ALU.mult, r=["KP", ("W", 3)], w=["KDE"])
        self.tt(NBDE[:, n_, :], v4(NBb[:, 0:wdt]), W[:, 3, n_, :], ALU.mult, r=["NBb", ("W", 3)], w=["NBDE"])
        self.cp(VB[:, n_, :], v4(XM[:, 2, 0:wdt]), r=["XM"], w=["VB"], eng="act")
        ptm = PSA[:, 3072:4096].bitcast(BF16).rearrange("p (a b) -> p a b", a=16)
        for kind, src in enumerate((None, KDE, NBDE, VB)):
            for cc in range(NC_):
                in_ = KR[:, cc, 0, :] if kind == 0 else src[:, cc, :]
                self.tr(ptm[0:64, kind * 4 + cc, :], in_, CB[:, 0, :], r=["KR", "KDE", "NBDE", "VB", "CSTB"], w=[("PS", 6), ("PS", 7)])
        self.cp(TM[0:64, :, 0:NC_, :], ptm[0:64, :, :].rearrange("p (k a) b -> p k a b", k=4)[:, :, 0:NC_, :],
                r=[("PS", 6), ("PS", 7)], w=["TM"], eng="act")
        for hh in range(2):
            rows = slice(64 * hh, 64 * hh + 64)
            pah = PSA[0:64, 1024 * hh:1024 * hh + 1024].rearrange("p (a b) -> p a b", a=4)
            pnh = self.PS[4 + hh][0:64, 0:256].rearrange("p (a b) -> p a b", a=4)
            for cc in range(NC_):
                krf = KR[rows, cc, :, :].rearrange("p a b -> p (a b)")
                self.mm(pah[:, cc, 0:128], KH[rows, cc, :], krf, r=["KH", "KR"], w=[("PS", 2 * hh), ("PS", 2 * hh + 1)])
                self.mm(pah[:, cc, 128:256], NBH[rows, cc, :], krf, r=["NBH", "KR"], w=[("PS", 2 * hh), ("PS", 2 * hh + 1)])
                self.mm(pnh[:, cc, :], KR[rows, cc, 0, :], NBH[rows, cc, :], r=["NBH", "KR"], w=[("PS", 4 + hh)])
            self.tt(AM[0:64, hh, n_, :], pah[:, n_, :], M4[0:64, :].unsqueeze(1).broadcast_to([64, NC_, 256]), ALU.mult,
                    r=[("PS", 2 * hh), ("PS", 2 * hh + 1), "MASK4"], w=["AM"])
            self.tt(NM[0:64, 0, 1, 4 * hh:4 * hh + NC_, :], pnh[:, n_, :], TRIL[0:64, :].unsqueeze(1).broadcast_to([64, NC_, 64]), ALU.mult,
                    r=[("PS", 4 + hh), "TRIL"], w=[("NM", 0)])
        E_ = [(hh, cc) for hh in range(2) for cc in range(NC_)]
        eidx = lambda hh, cc: 4 * hh + cc
        idb = IDHB[0:64, :]
        for hh in range(2):
            self.tt(TT[0:64, 0, 4 * hh:4 * hh + NC_, :], AM[0:64, hh, n_, 128:192], idb.unsqueeze(1).broadcast_to([64, NC_, 64]), ALU.add,
                    r=["AM", "IDHB"], w=[("TT", 0)])
        b6 = self.PS[6].rearrange("p (a b) -> p a b", a=8)
        b7 = self.PS[7].rearrange("p (a b) -> p a b", a=8)
        b5 = self.PS[5].rearrange("p (a b) -> p a b", a=8)
        for k in range(5):
            cur, nxt = k % 2, (k + 1) % 2
            for (hh, cc) in E_:
                e = eidx(hh, cc)
                ntk = AM[0:64, hh, cc, 128:192] if k == 0 else NM[0:64, cur, 0, e, :]
                nk = NM[0:64, cur, 1, e, :]
                self.mm(b6[0:64, e, :], nk, ntk, r=[("NM", cur), "AM"], w=[("PS", 6)])
                self.mm(b7[0:64, e, :], ntk, nk, r=[("NM", cur), "AM"], w=[("PS", 7)])
            self.cp(NM[0:64, nxt, 0, :, :], b6[0:64, :, :], r=[("PS", 6)], w=[("NM", nxt)], eng="act")
            self.cp(NM[0:64, nxt, 1, :, :], b7[0:64, :, :], r=[("PS", 7)], w=[("NM", nxt)], eng="act")
            for (hh, cc) in E_:
                e = eidx(hh, cc)
                self.mm(b5[0:64, e, :], idb, TT[0:64, cur, e, :], start=True, stop=False, r=[("TT", cur), "IDHB"], w=[("PS", 5)])
                self.mm(b5[0:64, e, :], NM[0:64, nxt, 1, e, :], TT[0:64, cur, e, :], start=False, stop=True,
                        r=[("TT", cur), ("NM", nxt)], w=[("PS", 5)])
            self.cp(TT[0:64, nxt, :, :], b5[0:64, :, :], r=[("PS", 5)], w=[("TT", nxt)])
        TTf = TT[0:64, 1]
        b4 = self.PS[4].rearrange("p (a b) -> p a b", a=8)
        for (hh, cc) in E_:
            e = eidx(hh, cc)
            self.mm(b4[0:64, e, :], AM[0:64, hh, cc, 0:64], TM[0:64, 3, cc, 64 * hh:64 * hh + 64], r=["AM", "TM"], w=[("PS", 4)])
        self.cp(MV[0:64, :, :], b4[0:64, :, :], r=[("PS", 4)], w=["MV"], eng="act")
        pz = PSA[0:64, 0:1024].rearrange("p (a b) -> p a b", a=8)
        for (hh, cc) in E_:
            e = eidx(hh, cc)
            self.mm(pz[:, e, 0:64], TTf[:, e, :], TM[0:64, 0, cc, 64 * hh:64 * hh + 64], r=[("TT", 1), "TM"], w=[("PS", 0), ("PS", 1)])
            self.mm(pz[:, e, 64:128], TTf[:, e, :], MV[0:64, e, :], r=[("TT", 1), "MV"], w=[("PS", 0), ("PS", 1)])
        self.cp(PZ[0:64, :, :], pz, r=[("PS", 0), ("PS", 1)], w=["PZ"])
        php = PSA[:, 1024:1280].rearrange("p (a b) -> p a b", a=4)
        gp = PSA[:, 1280:1536].rearrange("p (a b) -> p a b", a=4)
        qtp = PSA[:, 1536:1792].rearrange("p (a b) -> p a b", a=4)
        y0p = PSA[:, 1792:2048].rearrange("p (a b) -> p a b", a=4)
        for (hh, cc) in E_:
            e = eidx(hh, cc)
            o = slice(64 * hh, 64 * hh + 64)
            hs = slice(64 * hh, 64 * hh + 64)
            P1, Z = PZ[0:64, e, 0:64], PZ[0:64, e, 64:128]
            self.mm(php[o, cc, :], P1, TM[0:64, 2, cc, hs], r=["PZ", "TM"], w=[("PS", 2)])
            self.mm(gp[o, cc, :], TM[0:64, 1, cc, hs], TM[0:64, 3, cc, hs], start=True, stop=False, r=["TM"], w=[("PS", 2)])
            self.mm(gp[o, cc, :], TM[0:64, 2, cc, hs], Z, start=False, stop=True, r=["TM", "PZ"], w=[("PS", 2)])
            self.mm(qtp[o, cc, :], P1, AM[0:64, hh, cc, 192:256], r=["PZ", "AM"], w=[("PS", 3)])
            self.mm(y0p[o, cc, :], TM[0:64, 3, cc, hs], AM[0:64, hh, cc, 64:128], start=True, stop=False, r=["TM", "AM"], w=[("PS", 3)])
            self.mm(y0p[o, cc, :], Z, AM[0:64, hh, cc, 192:256], start=False, stop=True, r=["PZ", "AM"], w=[("PS", 3)])
        self.tt(DIAGW[:, n_, :], IDH[:, :].unsqueeze(1).broadcast_to([128, NC_, 64]),
                W[:, 1, n_, 63:64].broadcast_to([128, NC_, 64]), ALU.mult, r=["IDH", ("W", 1)], w=["DIAGW"])
        self.tt(PHIT[:, n_, :], php[:, n_, :], DIAGW[:, n_, :], ALU.add, r=[("PS", 2), "DIAGW"], w=["PHITP"])
        self.cp(G[:, n_, :], gp[:, n_, :], r=[("PS", 2)], w=["GP"], eng="act")
        self.tt(QT[:, n_, :], qtp[:, n_, :], RTF[:, n_, :], ALU.add, r=[("PS", 3), "RTF"], w=["QTP"])
        self.cp(v4(Y), y0p[:, n_, :], r=[("PS", 3)], w=["YP"], eng="act")

    def rwkv_chain(self, emit_y):
        c = self.cfg
        V = self.V
        PHITP, GP, QTP, H, YP = V("PHITP"), V("GP"), V("QTP"), V("H"), V("YP")
        for blk0 in range(0, c.NCH, 4):
            nb = min(4, c.NCH - blk0)
            psh = [self.PS[4 + hh].rearrange("p (a i b) -> p a i b", a=4, i=2) for hh in range(2)]
            for cc in range(nb):
                ch = blk0 + cc
                for hh in range(2):
                    o = slice(64 * hh, 64 * hh + 64)
                    if emit_y:
                        self.mm(psh[hh][o, cc, 0, :], H[o, :], QTP[o, ch, :], r=["H", "QTP"], w=[("PS", 4 + hh)])
                    self.mm(psh[hh][o, cc, 1, :], PHITP[o, ch, :], H[o, :], r=["H", "PHITP"], w=[("PS", 4 + hh)])
                for hh in range(2):
                    o = slice(64 * hh, 64 * hh + 64)
                    self.tt(H[o, :], psh[hh][o, cc, 1, :], GP[o, ch, :], ALU.add, r=[("PS", 4 + hh), "GP"], w=["H"])
            if emit_y:
                a = 1 + blk0 * 64
                yv = YP[:, a:a + nb * 64].rearrange("p (a b) -> p a b", a=nb)
                for hh in range(2):
                    o = slice(64 * hh, 64 * hh + 64)
                    self.tt(yv[o, :, :], yv[o, :, :], psh[hh][o, 0:nb, 0, :], ALU.add, r=[("PS", 4 + hh), "YP"], w=["YP"])


class DecodeAttMixin:
    def attention_decode(self):
        c = self.cfg
        V = self.V
        NS, NKV, NQH = c.NS, c.NKV, c.NQH
        Q, KD2 = V("Q"), V("KD2")
        CKD, CVD, KTD, QP, SD, ED, PD, PTD, PT1, VNS, VN0, VND, MXD, SINKP = (V(n) for n in (
            "CKD", "CVD", "KTD", "QP", "SD", "ED", "PD", "PTD", "PT1", "VNS", "VN0", "VND", "MXD", "SINKP"))
        CB = self.CSTB
        KD = c.KD
        col0 = c.NP + 1
        GREV = self.GREV
        self.dma("sp", SINKP[0:NQH, 0:1], self.din["sinks"].rearrange("(h o) -> h o", o=1), w=["SINKP"])
        b2 = self.PS[2]
        for vc in range(c.KC):
            W, wreg = self.wv[vc]
            for k in range(KD):
                self.mm(b2[0:NS, vc * 128:(vc + 1) * 128], self.XN[:, k, col0:col0 + NS], W[:, k, :], start=(k == 0), stop=(k == KD - 1),
                        r=wreg + [("XN", k)], w=[("PS", 2)])
        self.cp(VNS[0:NS, :], b2[0:NS, 0:c.KVW], r=[("PS", 2)], w=["VNS"])
        self.dma("sp", self.dout["new_v_s"][:, :], VNS[0:NS, :], r=["VNS"])
        for n in range(NS):
            self.dma("sp", VN0[0:1, n, :], VNS[n:n + 1, :], r=["VNS"], w=["VN0"])
        vn0 = VN0[0:1, :, :].rearrange("p n (g d) -> p n g d", g=NKV)
        vnd = VND[0:1, :, :, :].rearrange("p n g (t d) -> p n g t d", t=2)
        for t in range(2):
            self.cp(vnd[:, :, :, t, :], vn0, r=["VN0"], w=["VND"])
        self.cp(self.KSS[:, :, :], KD2[0:64, :, 128 + col0:128 + col0 + NS], r=[("KD2", g, "s") for g in range(NKV)], w=["KSS"])
        self.dma("sp", self.dout["new_k_s"][:, :, :], self.KSS[:, :, :], r=["KSS"])
        self.dma("sp", self.dout["kc_s"][:, :, :], self.din["cache_k"][:, 1:128, :])
        self.dma("sp", self.dout["vc_s"][:, :, :], self.din["cache_v"][:, 1:128, :])
        self.mset(QP[:], 0.0, w=["QP"])
        for n in range(NS):
            for g in range(NKV):
                for par in range(2):
                    h0 = 4 * g + par
                    self.cp(QP[64 * par:64 * par + 64, n, g, h0:h0 + 3:2], Q[64 * par:64 * par + 64, 2 * g:2 * g + 2, col0 + n],
                            r=[("Q", 2 * g, "s"), ("Q", 2 * g + 1, "s")], w=["QP"])
        self.mset(SD[:], 0.0, w=["SD"])
        for n in range(NS):
            ck = self.din["cache_k"][n].rearrange("t (g d) -> t g d", g=NKV)
            cv = self.din["cache_v"][n].rearrange("t (g d) -> t g d", g=NKV)
            ckd = CKD[:, :, :].rearrange("p g (t d) -> p g t d", t=2)
            cvd = CVD[:, :, :].rearrange("p g (t d) -> p g t d", t=2)
            for t in range(2):
                self.dma("pool", ckd[:, :, t, :], ck, w=["CKD"])
                self.dma("pool", cvd[:, :, t, :], cv, w=["CVD"])
            tb = self.PS[4][:, :].bitcast(BF16).rearrange("p (g k) -> p g k", g=NKV)
            for g in range(NKV):
                self.tr(tb[:, g, 0:128], CKD[:, g, :], CB[:, 0, :], r=["CKD", "CSTB"], w=[("PS", 4)])
            self.cp(KTD[:, :, 0:128], tb[:, :, 0:128], r=[("PS", 4)], w=["KTD"], eng="act")
            self.cp(KTD[:, :, 128:129], KD2[:, :, 128 + col0 + n:128 + col0 + n + 1], r=[("KD2", g, "s") for g in range(NKV)], w=["KTD"])
            b0 = self.PS[0]
            for g in range(NKV):
                self.mm(b0[0:NQH, 0:129], QP[:, n, g, :], KTD[:, g, 0:129], start=(g == 0), stop=(g == NKV - 1),
                        r=["QP", "KTD"], w=[("PS", 0)])
            self.stt(SD[0:NQH, 0:129], b0[0:NQH, 0:129], 0.125, GREV[:, 127:256], ALU.mult, ALU.add, r=[("PS", 0), "GREV"], w=["SD"])
            self.cp(SD[0:NQH, 129:130], SINKP[0:NQH, 0:1], r=["SINKP"], w=["SD"])
            self.red(MXD[0:NQH, 0:1], SD[0:NQH, 0:130], ALU.max, r=["SD"], w=["MXD"])
            self.ts(MXD[0:NQH, 1:2], MXD[0:NQH, 0:1], -1.0, None, ALU.mult, r=["MXD"], w=["MXD"])
            self.act(ED[0:NQH, 0:130], SD[0:NQH, 0:130], AF.Exp, bias=MXD[0:NQH, 1:2], scale=1.0, accum_out=MXD[0:NQH, 2:3],
                     r=["SD", "MXD"], w=["ED", "MXD"])
            self.recip(MXD[0:NQH, 3:4], MXD[0:NQH, 2:3], r=["MXD"], w=["MXD"])
            self.ts(PD[0:NQH, 0:130], ED[0:NQH, 0:130], MXD[0:NQH, 3:4], None, ALU.mult, r=["ED", "MXD"], w=["PD"])
            tp = self.PS[5][:, :].bitcast(BF16)
            self.tr(tp[:, 0:NQH], PD[0:NQH, 0:128], CB[0:NQH, 0, 0:NQH], r=["PD", "CSTB"], w=[("PS", 5)])
            self.tr(tp[0:1, 64:64 + NQH], PD[0:NQH, 128:129], CB[0:NQH, 0, 0:NQH], r=["PD", "CSTB"], w=[("PS", 5)])
            self.cp(PTD[:, 0:NQH], tp[:, 0:NQH], r=[("PS", 5)], w=["PTD"], eng="act")
            self.cp(PT1[0:1, 0:NQH], tp[0:1, 64:64 + NQH], r=[("PS", 5)], w=["PT1"], eng="act")
            b1 = self.PS[1]
            for g in range(NKV):
                self.mm(b1[:, 4 * g:4 * g + 4], CVD[:, g, :], PTD[:, 4 * g:4 * g + 4], start=True, stop=False,
                        r=["CVD", "PTD"], w=[("PS", 1)])
                self.mm(b1[:, 4 * g:4 * g + 4], VND[0:1, n, g, :], PT1[0:1, 4 * g:4 * g + 4], start=False, stop=True,
                        r=["VND", "PT1"], w=[("PS", 1)])
            for g in range(NKV):
                for par in range(2):
                    h0 = 4 * g + par
                    self.cp(Q[64 * par:64 * par + 64, 2 * g:2 * g + 2, col0 + n], b1[64 * par:64 * par + 64, h0:h0 + 3:2],
                            r=[("PS", 1), "QP"], w=[("Q", 2 * g, "s"), ("Q", 2 * g + 1, "s")], eng="act")

    def w_out_phase(self):
        c = self.cfg
        KD = c.KD
        X, Q, MIXR = self.X, self.V("Q"), self.V("MIXR")
        self.p.barrier()
        self.dma("sp", X[:], self.xpark[:], r=["xpark"], w=[("X", k) for k in range(KD)])
        w_out = self.wd["w_out"].rearrange("(k q) n -> q k n", q=128)
        step = 0
        for o in range(KD):
            W, wreg = self.slab(w_out[:, :, o * 128:(o + 1) * 128])
            for ci, (a, b) in enumerate(c.CH):
                bi = step % 2
                step += 1
                bank = self.PS[bi]
                for k in range(KD):
                    if k < c.QC:
                        rhs, rr = Q[:, k, a:b], self.allcols("Q", k)
                    else:
                        rhs, rr = MIXR[:, k - c.QC, a:b], [("MIXR", k - c.QC)]
                    self.mm(bank[:, 0:b - a], W[:, k, :], rhs, start=(k == 0), stop=(k == KD - 1), r=wreg + rr, w=[("PS", bi)])
                self.tt(X[:, o, a:b], bank[:, 0:b - a], X[:, o, a:b], ALU.add, r=[("PS", bi), ("X", o)], w=[("X", o)])


class FullBuilder(DecodeAttMixin, RwkvMixin, MixerBuilder):
    def declare_io(self):
        MixerBuilder.declare_io(self)
        c = self.cfg
        self.inp("w0row", [1, c.RW])
        self.inp("w2", [64, c.RW])
        self.inp("a2", [64, c.RW])
        self.inp("g2", [128, c.RW])
        self.inp("sshift", [128, c.SC, c.NS])
        self.outp("shift_o", [128, c.SC, 1 + c.NS])
        self.outp("wkv_p", [128, c.RP, 64])
        self.inp("wkv_s", [c.NS, c.RH, 64, 64])
        if c.halves > 1:
            self.inp("flag", [128, 1])
            self.hx_in = self.nc.dram_tensor("hx_in", [c.RP, 128, 64], F32, kind="Internal").ap()
            self.hx_out = self.nc.dram_tensor("hx_out", [c.RP, 128 * c.halves, 64], F32, kind="Internal").ap()
            kw = c.NKV * 128 + c.KVW
            self.kvx_in = self.nc.dram_tensor("kvx_in", [128, kw], BF16, kind="Internal").ap()
            self.kvx_out = self.nc.dram_tensor("kvx_out", [128 * c.halves, kw], BF16, kind="Internal").ap()
        self.inp("cache_k", [c.NS, 128, c.KVW])
        self.inp("cache_v", [c.NS, 128, c.KVW])
        self.outp("new_v_s", [c.NS, c.KVW])
        self.outp("new_k_s", [64, c.NKV, c.NS])
        self.outp("kc_s", [c.NS, 127, c.KVW])
        self.outp("vc_s", [c.NS, 127, c.KVW])
        self.outp("wkv_s_o", [c.NS, c.RH, 64, 64])
        if self.debug:
            self.outp("dbg_mixr", [128, c.RP, c.NT], BF16)

    def alloc_extra(self, sb):
        MixerBuilder.alloc_extra(self, sb)
        c = self.cfg
        self.SSH = sb("SSH", [128, c.SC, c.NS], F32)
        self.SHO = sb("SHO", [128, c.SC, 1 + c.NS], F32)
        self.KSS = sb("KSS", [64, c.NKV, c.NS], F32)
        self.FLAG = sb("FLAG", [128, 1], F32)

    def mixer(self):
        c = self.cfg
        self.slab_i = 0
        if "att" in self.parts:
            self.attention_inputs()
            self.proj_qkv()
            self.attention_prompt()
            self.attention_decode()
            if self.debug:
                self.dma("sp", self.dout["dbg_q"][:], self.V("Q")[:], r=[r_ for ch in range(c.QC) for r_ in self.allcols("Q", ch)])
        if "rwkv" in self.parts:
            self.rwkv()
            if self.debug:
                self.dma("sp", self.dout["dbg_mixr"][:, :, 1:c.NT], self.V("MIXR")[:, :, 1:c.NT], r=[("MIXR", pr) for pr in range(c.RP)])
        if "wout" in self.parts:
            self.w_out_phase()

    parts = ("att", "rwkv", "wout")


def _fm(vec, nchunks):
    return np.ascontiguousarray(np.asarray(vec, np.float32).reshape(nchunks, 128).T)


def prep_inputs(cfg, builder, inp):
    c = cfg
    pv = builder.pv
    pvec = np.zeros((128, pv["_n"]), np.float32)

    def setp(name, vec, n):
        pvec[:, pv[name]:pv[name] + n] = _fm(np.asarray(vec).reshape(-1), n)

    setp("n1", inp["ffn1_norm"][0], c.KD)
    setp("nm", inp["mix_norm"][0], c.KD)
    setp("n2", inp["ffn2_norm"][0], c.KD)
    setp("nf", inp["final_norm"], c.KD)
    setp("mu", inp["shift_mu"][0], c.SC)
    for nm, key in (("w0", "decay_w0"), ("a0", "aaa_a0"), ("k_k", "key_k"), ("k_a", "key_a"), ("r_k", "bonus_r_k"),
                    ("ln_w", "ln_x_w"), ("ln_b", "ln_x_b")):
        setp(nm, inp[key][0], c.RP)
    hc = host_consts(c)
    rb = np.concatenate([np.asarray(inp["rel_bias"], np.float32), np.full((1, c.NQH), -30000, np.float32)], 0)
    shared = {
        "pvec": pvec, "cst": hc["cst"], "oh_rev": hc["oh_rev"], "rb_ext": rb,
        "sinks": np.ascontiguousarray(inp["attn_sinks"][0], np.float32),
        "f1g": inp["ffn1_w_gate"][0], "f1u": inp["ffn1_w_up"][0], "f1d": inp["ffn1_w_down"][0],
        "w_in": inp["w_in"][0], "w_out": inp["w_out"][0],
        "f2g": inp["ffn2_w_gate"][0], "f2u": inp["ffn2_w_up"][0], "f2d": inp["ffn2_w_down"][0],
        "w0row": np.ascontiguousarray(inp["decay_w0"][0][None, :]), "w2": inp["decay_w2"][0], "a2": inp["aaa_a2"][0],
        "g2": inp["gate_g2"][0],
    }
    shared = {k: np.ascontiguousarray(v, dtype=np.float32) for k, v in shared.items()}
    maps = []
    for core in range(c.n_cores):
        b, half = core // c.halves, core % c.halves
        t0 = half * c.NP
        xp = inp["x_prompt"][b]
        halo = xp[t0 - 1:t0] if half > 0 else np.zeros((1, c.D), np.float32)
        x = np.concatenate([halo, xp[t0:t0 + c.NP], inp["x_sample"][core * c.NS:(core + 1) * c.NS, 0]], 0)
        xT = np.ascontiguousarray(x.T.reshape(c.KD, 128, c.NT).transpose(1, 0, 2), dtype=np.float32)
        ss = inp["state_shift"][0][core * c.NS:(core + 1) * c.NS]
        sshift = np.ascontiguousarray(ss.T.reshape(c.SC, 128, c.NS).transpose(1, 0, 2), dtype=np.float32)
        m = dict(shared)
        if c.halves > 1:
            m["flag"] = np.full((128, 1), 0.0 if half == 0 else 1.0, np.float32)
        m.update({
            "xT": xT, "sshift": sshift,
            "mask0": np.full((128, 128), -30000.0 if half == 0 else 0.0, np.float32),
            "wkv_s": np.ascontiguousarray(inp["state_wkv"][0][core * c.NS:(core + 1) * c.NS], dtype=np.float32),
            "cache_k": np.ascontiguousarray(inp["cache_k"][0][core * c.NS:(core + 1) * c.NS].reshape(c.NS, 128, c.KVW), dtype=np.float32),
            "cache_v": np.ascontiguousarray(inp["cache_v"][0][core * c.NS:(core + 1) * c.NS].reshape(c.NS, 128, c.KVW), dtype=np.float32),
        })
        maps.append(m)
    return maps


def assemble(cfg, res, batch):
    c = cfg
    S = c.NP * c.halves
    DB = c.n_cores * c.NS
    y_p = np.zeros((batch, S, c.D), np.float32)
    y_s = np.zeros((DB, 1, c.D), np.float32)
    nk_p = np.zeros((1, batch, 128, c.NKV, 64), np.float32)
    nv_p = np.zeros((1, batch, 128, c.NKV, 64), np.float32)
    wkv_p = np.zeros((1, batch, c.RH, 64, 64), np.float32)
    sh_p = np.zeros((1, batch, c.SHIFT_COLS), np.float32)
    nk_s = np.zeros((1, DB, 128, c.NKV, 64), np.float32)
    nv_s = np.zeros((1, DB, 128, c.NKV, 64), np.float32)
    wkv_s = np.zeros((1, DB, c.RH, 64, 64), np.float32)
    sh_s = np.zeros((1, DB, c.SHIFT_COLS), np.float32)
    for core in range(c.n_cores):
        r = res[core]
        b, half = core // c.halves, core % c.halves
        y = np.asarray(r["yT"]).transpose(1, 0, 2).reshape(c.D, c.NP + c.NS).T
        y_p[b, half * c.NP:(half + 1) * c.NP] = y[:c.NP]
        y_s[core * c.NS:(core + 1) * c.NS, 0] = y[c.NP:]
        sho = np.asarray(r["shift_o"])
        shf = sho.transpose(2, 1, 0).reshape(1 + c.NS, c.SHIFT_COLS)
        if half == c.halves - 1:
            nk_p[0, b] = np.asarray(r["new_k"]).transpose(2, 1, 0)
            nv_p[0, b] = np.asarray(r["new_v"]).reshape(128, c.NKV, 64)
            wk = np.asarray(r["wkv_p"])
            wkv_p[0, b] = wk.reshape(2, 64, c.RP, 64).transpose(2, 0, 3, 1).reshape(c.RH, 64, 64)
            sh_p[0, b] = shf[0]
        sl = slice(core * c.NS, (core + 1) * c.NS)
        nk_s[0, sl, :127] = np.asarray(r["kc_s"]).reshape(c.NS, 127, c.NKV, 64)
        nk_s[0, sl, 127] = np.asarray(r["new_k_s"]).transpose(2, 1, 0)
        nv_s[0, sl, :127] = np.asarray(r["vc_s"]).reshape(c.NS, 127, c.NKV, 64)
        nv_s[0, sl, 127] = np.asarray(r["new_v_s"]).reshape(c.NS, c.NKV, 64)
        wkv_s[0, sl] = np.asarray(r["wkv_s_o"])
        sh_s[0, sl] = shf[1:]
    return (y_p, y_s, nk_p, nv_p, wkv_p, sh_p, nk_s, nv_s, wkv_s, sh_s)


def kernel(**inputs):
    inp = {k: np.asarray(v) for k, v in inputs.items()}
    cfg = Cfg(D=2048, DFF=5504, NP=1024, NS=4, n_cores=8, halves=2, GS=8)
    b = FullBuilder(cfg)
    nc = b.build()
    maps = prep_inputs(cfg, b, inp)
    res = run_bass_kernel_spmd(nc, maps, core_ids=list(range(cfg.n_cores)))
    return assemble(cfg, res.results, inp["x_prompt"].shape[0])
```
